# Optimizing a Trainium2 kernel written in Bass

```python
import math
import jax, jax.numpy as jnp
from jax import lax
import numpy as np

D_MODEL = 1024
BATCH = 16
SEQ = 2048
DEPTH = 4

N_META = 16
CHUNK = 64
PAD = CHUNK - N_META
CONV_K = 5
NORM_EPS = 1e-6
N_BRANCH = 3

S5_WIDTH = D_MODEL // 2
S5_GROUP = 16
S5_GROUPS = S5_WIDTH // S5_GROUP
S5_STATE = 64

SSD_WIDTH = D_MODEL
SSD_HEAD_DIM = 64
SSD_HEADS = SSD_WIDTH // SSD_HEAD_DIM
SSD_GROUPS = 2
SSD_STATE = 128
SSD_BC = SSD_GROUPS * SSD_STATE

GDN_WIDTH = D_MODEL // 2
GDN_HEAD_DIM = 128
GDN_HEADS = GDN_WIDTH // GDN_HEAD_DIM

IN_SIZES = (S5_WIDTH, S5_WIDTH,
            SSD_WIDTH, SSD_WIDTH, SSD_BC, SSD_BC, 2 * SSD_HEADS,
            GDN_WIDTH, GDN_WIDTH, GDN_WIDTH, GDN_WIDTH, 2 * GDN_HEADS, 2 * GDN_HEADS,
            N_BRANCH * D_MODEL)
D_IN = (2 * S5_WIDTH + 2 * SSD_WIDTH + 2 * SSD_BC + 2 * SSD_HEADS
        + 4 * GDN_WIDTH + 4 * GDN_HEADS + N_BRANCH * D_MODEL)

kernel_name = 'bidir_hybrid_s5_ssd_gdn_meta'


def split_columns(t, sizes):
    out, start = [], 0
    for s in sizes:
        out.append(t[..., start:start + s])
        start += s
    return out


def _rms(x):
    xf = x.astype(jnp.float32)
    return xf * lax.rsqrt(jnp.mean(xf * xf, axis=-1, keepdims=True) + NORM_EPS)


def rmsnorm(x, w):
    return _rms(x).astype(x.dtype) * w.astype(x.dtype)


def l2norm(x):
    return x * lax.rsqrt(jnp.sum(x * x, axis=-1, keepdims=True) + NORM_EPS)


def depthwise_conv(x, w):
    c = x.shape[-1]
    half = (CONV_K - 1) // 2
    return lax.conv_general_dilated(x, w[:, None, :].astype(x.dtype), window_strides=(1,),
                                    padding=[(half, half)], dimension_numbers=('NWC', 'WIO', 'NWC'),
                                    feature_group_count=c)


def pad_front(t):
    return jnp.pad(t, [(0, 0), (PAD, 0)] + [(0, 0)] * (t.ndim - 2))


def flip(t):
    return jnp.flip(t, axis=1)


def s5_scan_dir(ug, a_re, a_im, log_dt, b_re, b_im, c_re, c_im, reverse):
    dt = jnp.exp(log_dt)[:, None]
    mag = jnp.exp(a_re * dt)
    ab_re = mag * jnp.cos(a_im * dt)
    ab_im = mag * jnp.sin(a_im * dt)
    den = a_re * a_re + a_im * a_im
    f_re = ((ab_re - 1.0) * a_re + ab_im * a_im) / den
    f_im = (ab_im * a_re - (ab_re - 1.0) * a_im) / den
    bb_re = f_re[..., None] * b_re - f_im[..., None] * b_im
    bb_im = f_re[..., None] * b_im + f_im[..., None] * b_re
    ut = jnp.swapaxes(ug, 0, 1)
    bu_re = jnp.einsum('lbgp,gnp->lbgn', ut, bb_re)
    bu_im = jnp.einsum('lbgp,gnp->lbgn', ut, bb_im)
    seq_len = ut.shape[0]
    ar = jnp.broadcast_to(ab_re, (seq_len, 1) + ab_re.shape)
    ai = jnp.broadcast_to(ab_im, (seq_len, 1) + ab_im.shape)

    def combine(e1, e2):
        a1r, a1i, b1r, b1i = e1
        a2r, a2i, b2r, b2i = e2
        return (a2r * a1r - a2i * a1i, a2r * a1i + a2i * a1r,
                a2r * b1r - a2i * b1i + b2r, a2r * b1i + a2i * b1r + b2i)

    _, _, hr, hi = lax.associative_scan(combine, (ar, ai, bu_re, bu_im), reverse=reverse, axis=0)
    return jnp.einsum('lbgn,gpn->blgp', hr, c_re) - jnp.einsum('lbgn,gpn->blgp', hi, c_im)


def s5_branch(u, z, a_re, a_im, log_dt, b_re, b_im, c_re, c_im, d, w_glu, b_glu):
    f32 = jnp.float32
    bsz, seq_len, _ = u.shape
    uf = u.astype(f32)
    ug = uf.reshape(bsz, seq_len, S5_GROUPS, S5_GROUP)
    y = d.astype(f32) * uf
    for direction in range(2):
        y = y + s5_scan_dir(ug, a_re[direction].astype(f32), a_im[direction].astype(f32),
                            log_dt[direction].astype(f32), b_re[direction].astype(f32),
                            b_im[direction].astype(f32), c_re[direction].astype(f32),
                            c_im[direction].astype(f32), reverse=(direction == 1)
                            ).reshape(bsz, seq_len, S5_WIDTH)
    y = jax.nn.gelu(y)
    y = y * jax.nn.sigmoid(y @ w_glu.astype(f32) + b_glu.astype(f32))
    return y * jax.nn.silu(z.astype(f32))


def ssd_chunk_scan(x, dt, a, bm, cm):
    bsz, lp, nh, hp = x.shape
    ng, ns = bm.shape[2], bm.shape[3]
    ne = nh // ng
    nc = lp // CHUNK
    x = x.reshape(bsz, nc, CHUNK, ng, ne, hp)
    dt = dt.reshape(bsz, nc, CHUNK, ng, ne)
    bm = bm.reshape(bsz, nc, CHUNK, ng, ns)
    cm = cm.reshape(bsz, nc, CHUNK, ng, ns)
    a_cum = jnp.cumsum(jnp.moveaxis(dt * a.reshape(ng, ne), 2, -1), axis=-1)
    xdt = x * dt[..., None]
    lower = jnp.tril(jnp.ones((CHUNK, CHUNK), dtype=bool))
    decay = jnp.exp(jnp.where(lower, a_cum[..., :, None] - a_cum[..., None, :], -jnp.inf))
    scores = jnp.einsum('bcign,bcjgn->bcgij', cm, bm)
    y_diag = jnp.einsum('bcgeij,bcjgep->bcigep', scores[:, :, :, None] * decay, xdt)
    tail = jnp.exp(a_cum[..., -1:] - a_cum)
    states = jnp.einsum('bcjgn,bcgej,bcjgep->bcgepn', bm, tail, xdt)
    chunk_decay = jnp.exp(a_cum[..., -1])

    def step(s, inp):
        st, dec = inp
        return s * dec[..., None, None] + st, s

    _, s_in = lax.scan(step, jnp.zeros_like(states[:, 0]),
                       (jnp.moveaxis(states, 1, 0), jnp.moveaxis(chunk_decay, 1, 0)))
    s_in = jnp.moveaxis(s_in, 0, 1)
    y_off = (jnp.einsum('bcign,bcgepn->bcigep', cm, s_in)
             * jnp.moveaxis(jnp.exp(a_cum), -1, 2)[..., None])
    return (y_diag + y_off).reshape(bsz, lp, nh, hp)


def ssd_branch(xs, z, bm, cm, dt_raw, conv_w, conv_b, a_log, dt_bias, d, norm_w):
    f32 = jnp.float32
    bsz, seq_len, _ = xs.shape
    xbc = depthwise_conv(jnp.concatenate([xs, bm, cm], axis=-1), conv_w)
    xbc = jax.nn.silu(xbc.astype(f32) + conv_b.astype(f32))
    xc, bc, cc = split_columns(xbc, (SSD_WIDTH, SSD_BC, SSD_BC))
    xh = xc.reshape(bsz, seq_len, SSD_HEADS, SSD_HEAD_DIM)
    bc = bc.reshape(bsz, seq_len, SSD_GROUPS, SSD_STATE)
    cc = cc.reshape(bsz, seq_len, SSD_GROUPS, SSD_STATE)
    dt = jax.nn.softplus(dt_raw.astype(f32).reshape(bsz, seq_len, 2, SSD_HEADS) + dt_bias.astype(f32))
    a = -jnp.exp(a_log.astype(f32))
    xp, bp, cp, dtp = pad_front(xh), pad_front(bc), pad_front(cc), pad_front(dt)
    y_fwd = ssd_chunk_scan(xp, dtp[:, :, 0], a[0], bp, cp)
    y_bwd = flip(ssd_chunk_scan(flip(xp), flip(dtp[:, :, 1]), a[1], flip(bp), flip(cp)))
    y = (y_fwd + y_bwd)[:, PAD:] + d.astype(f32)[:, None] * xh
    y = y.reshape(bsz, seq_len, SSD_WIDTH) * jax.nn.silu(z.astype(f32))
    y = _rms(y.reshape(bsz, seq_len, SSD_GROUPS, SSD_WIDTH // SSD_GROUPS)).reshape(bsz, seq_len, SSD_WIDTH)
    return y * norm_w.astype(f32)


def gdn_chunk_scan(q, k, v, g, beta):
    bsz, lp, nh, dk = q.shape
    dv = v.shape[-1]
    nc = lp // CHUNK

    def chunks(t):
        t = jnp.moveaxis(t, 2, 1)
        return t.reshape((bsz, nh, nc, CHUNK) + t.shape[3:])

    q, k, v, g, beta = chunks(q), chunks(k), chunks(v), chunks(g), chunks(beta)
    g = jnp.cumsum(g, axis=-1)
    incl = jnp.tril(jnp.ones((CHUNK, CHUNK), dtype=bool))
    strict = jnp.tril(jnp.ones((CHUNK, CHUNK), dtype=bool), k=-1)
    decay = jnp.exp(jnp.where(incl, g[..., :, None] - g[..., None, :], -jnp.inf))
    kb = k * beta[..., None]
    a_mat = jnp.where(strict, jnp.einsum('bhcik,bhcjk->bhcij', kb, k) * decay, 0.0)
    eye = jnp.eye(CHUNK, dtype=a_mat.dtype)
    rhs = jnp.concatenate([v * beta[..., None], kb * jnp.exp(g)[..., None]], axis=-1)
    sol = lax.linalg.triangular_solve(a_mat + eye, rhs, left_side=True, lower=True, unit_diagonal=True)
    u, w = sol[..., :dv], sol[..., dv:]
    attn = jnp.where(incl, jnp.einsum('bhcik,bhcjk->bhcij', q, k) * decay, 0.0)
    qg = q * jnp.exp(g)[..., None]
    g_last = g[..., -1]
    kt = k * jnp.exp(g_last[..., None] - g)[..., None]

    def step(s, inp):
        u_c, w_c, qg_c, attn_c, kt_c, gl_c = inp
        v_new = u_c - jnp.einsum('bhik,bhkv->bhiv', w_c, s)
        o_c = jnp.einsum('bhik,bhkv->bhiv', qg_c, s) + jnp.einsum('bhij,bhjv->bhiv', attn_c, v_new)
        s = s * jnp.exp(gl_c)[..., None, None] + jnp.einsum('bhik,bhiv->bhkv', kt_c, v_new)
        return s, o_c

    mv = lambda t: jnp.moveaxis(t, 2, 0)
    s0 = jnp.zeros((bsz, nh, dk, dv), q.dtype)
    _, o = lax.scan(step, s0, (mv(u), mv(w), mv(qg), mv(attn), mv(kt), mv(g_last)))
    o = jnp.moveaxis(o, 0, 2).reshape(bsz, nh, lp, dv)
    return jnp.moveaxis(o, 1, 2)


def gdn_branch(q, k, v, z, beta_raw, a_raw, conv_w, a_log, dt_bias, norm_w):
    f32 = jnp.float32
    bsz, seq_len, _ = q.shape
    qkv = jax.nn.silu(depthwise_conv(jnp.concatenate([q, k, v], axis=-1), conv_w).astype(f32))
    qc, kc, vc = split_columns(qkv, (GDN_WIDTH, GDN_WIDTH, GDN_WIDTH))
    shp = (bsz, seq_len, GDN_HEADS, GDN_HEAD_DIM)
    qh = l2norm(qc.reshape(shp)) * (GDN_HEAD_DIM ** -0.5)
    kh = l2norm(kc.reshape(shp))
    vh = vc.reshape(shp)
    beta = jax.nn.sigmoid(beta_raw.astype(f32).reshape(bsz, seq_len, 2, GDN_HEADS))
    g = -jnp.exp(a_log.astype(f32)) * jax.nn.softplus(
        a_raw.astype(f32).reshape(bsz, seq_len, 2, GDN_HEADS) + dt_bias.astype(f32))
    qp, kp, vp, gp, bp = pad_front(qh), pad_front(kh), pad_front(vh), pad_front(g), pad_front(beta)
    o_fwd = gdn_chunk_scan(qp, kp, vp, gp[:, :, 0], bp[:, :, 0])
    o_bwd = flip(gdn_chunk_scan(flip(qp), flip(kp), flip(vp), flip(gp[:, :, 1]), flip(bp[:, :, 1])))
    o = _rms((o_fwd + o_bwd)[:, PAD:]) * norm_w.astype(f32)
    return o.reshape(bsz, seq_len, GDN_WIDTH) * jax.nn.silu(z.astype(f32))


def setup_inputs(seed: int = 0) -> dict:
    key = jax.random.key(seed)
    ks = jax.random.split(key, 32)
    f32 = jnp.float32
    nrm = lambda k, shape, scale: jax.random.normal(k, shape, f32) * scale

    def log_uniform_dt(k, shape):
        return jnp.exp(jax.random.uniform(k, shape, f32, math.log(1e-3), math.log(1e-1)))

    def inv_softplus_dt(k, shape):
        dt = log_uniform_dt(k, shape)
        return dt + jnp.log(-jnp.expm1(-dt))

    n_idx = jnp.arange(S5_STATE, dtype=f32)
    return {
        'x': nrm(ks[0], (BATCH, SEQ, D_MODEL), 1.0),
        'meta_tokens': nrm(ks[1], (N_META, D_MODEL), 1.0),
        'norm_w': 1.0 + nrm(ks[2], (DEPTH, D_MODEL), 0.02),
        'w_in': nrm(ks[3], (DEPTH, D_MODEL, D_IN), D_MODEL ** -0.5),
        's5_a_re': -0.5 * jnp.exp(nrm(ks[4], (DEPTH, 2, S5_GROUPS, S5_STATE), 0.05)),
        's5_a_im': math.pi * n_idx + nrm(ks[5], (DEPTH, 2, S5_GROUPS, S5_STATE), 0.05),
        's5_log_dt': jnp.log(log_uniform_dt(ks[6], (DEPTH, 2, S5_GROUPS))),
        's5_b_re': nrm(ks[7], (DEPTH, 2, S5_GROUPS, S5_STATE, S5_GROUP), (2 * S5_GROUP) ** -0.5),
        's5_b_im': nrm(ks[8], (DEPTH, 2, S5_GROUPS, S5_STATE, S5_GROUP), (2 * S5_GROUP) ** -0.5),
        's5_c_re': nrm(ks[9], (DEPTH, 2, S5_GROUPS, S5_GROUP, S5_STATE), (2 * S5_STATE) ** -0.5),
        's5_c_im': nrm(ks[10], (DEPTH, 2, S5_GROUPS, S5_GROUP, S5_STATE), (2 * S5_STATE) ** -0.5),
        's5_d': nrm(ks[11], (DEPTH, S5_WIDTH), 1.0),
        's5_w_glu': nrm(ks[12], (DEPTH, S5_WIDTH, S5_WIDTH), S5_WIDTH ** -0.5),
        's5_b_glu': nrm(ks[13], (DEPTH, S5_WIDTH), 0.01),
        'w_s5_out': nrm(ks[14], (DEPTH, S5_WIDTH, D_MODEL), S5_WIDTH ** -0.5),
        'ssd_conv_w': nrm(ks[15], (DEPTH, CONV_K, SSD_WIDTH + 2 * SSD_BC), CONV_K ** -0.5),
        'ssd_conv_b': nrm(ks[16], (DEPTH, SSD_WIDTH + 2 * SSD_BC), 0.01),
        'ssd_a_log': jnp.log(jax.random.uniform(ks[17], (DEPTH, 2, SSD_HEADS), f32, 1.0, 16.0)),
        'ssd_dt_bias': inv_softplus_dt(ks[18], (DEPTH, 2, SSD_HEADS)),
        'ssd_d': 1.0 + nrm(ks[19], (DEPTH, SSD_HEADS), 0.1),
        'ssd_norm_w': 1.0 + nrm(ks[20], (DEPTH, SSD_WIDTH), 0.02),
        'w_ssd_out': nrm(ks[21], (DEPTH, SSD_WIDTH, D_MODEL), SSD_WIDTH ** -0.5),
        'gdn_conv_w': nrm(ks[22], (DEPTH, CONV_K, 3 * GDN_WIDTH), CONV_K ** -0.5),
        'gdn_a_log': jnp.log(jax.random.uniform(ks[23], (DEPTH, 2, GDN_HEADS), f32, 1.0, 16.0)),
        'gdn_dt_bias': inv_softplus_dt(ks[24], (DEPTH, 2, GDN_HEADS)),
        'gdn_norm_w': 1.0 + nrm(ks[25], (DEPTH, GDN_HEAD_DIM), 0.02),
        'w_gdn_out': nrm(ks[26], (DEPTH, GDN_WIDTH, D_MODEL), GDN_WIDTH ** -0.5),
        'w_out': nrm(ks[27], (DEPTH, D_MODEL, D_MODEL), D_MODEL ** -0.5),
        'final_norm_w': 1.0 + nrm(ks[28], (D_MODEL,), 0.02),
    }


def reference(x, meta_tokens, norm_w, w_in,
              s5_a_re, s5_a_im, s5_log_dt, s5_b_re, s5_b_im, s5_c_re, s5_c_im, s5_d, s5_w_glu, s5_b_glu,
              w_s5_out,
              ssd_conv_w, ssd_conv_b, ssd_a_log, ssd_dt_bias, ssd_d, ssd_norm_w, w_ssd_out,
              gdn_conv_w, gdn_a_log, gdn_dt_bias, gdn_norm_w, w_gdn_out,
              w_out, final_norm_w):
    bsz = x.shape[0]
    meta = jnp.broadcast_to(meta_tokens[None].astype(x.dtype), (bsz, N_META, D_MODEL))
    h = jnp.concatenate([meta, x], axis=1)
    seq_len = h.shape[1]
    for i in range(DEPTH):
        hn = rmsnorm(h, norm_w[i])
        (u_s5, z_s5, x_ssd, z_ssd, b_ssd, c_ssd, dt_ssd,
         q_gdn, k_gdn, v_gdn, z_gdn, beta_gdn, a_gdn, gates) = split_columns(hn @ w_in[i], IN_SIZES)
        y_s5 = s5_branch(u_s5, z_s5, s5_a_re[i], s5_a_im[i], s5_log_dt[i], s5_b_re[i], s5_b_im[i],
                         s5_c_re[i], s5_c_im[i], s5_d[i], s5_w_glu[i], s5_b_glu[i])
        y_ssd = ssd_branch(x_ssd, z_ssd, b_ssd, c_ssd, dt_ssd, ssd_conv_w[i], ssd_conv_b[i],
                           ssd_a_log[i], ssd_dt_bias[i], ssd_d[i], ssd_norm_w[i])
        y_gdn = gdn_branch(q_gdn, k_gdn, v_gdn, z_gdn, beta_gdn, a_gdn, gdn_conv_w[i],
                           gdn_a_log[i], gdn_dt_bias[i], gdn_norm_w[i])
        gate = jax.nn.sigmoid(gates.astype(jnp.float32)).reshape(bsz, seq_len, N_BRANCH, D_MODEL)
        merged = (gate[:, :, 0] * (y_s5 @ w_s5_out[i])
                  + gate[:, :, 1] * (y_ssd @ w_ssd_out[i])
                  + gate[:, :, 2] * (y_gdn @ w_gdn_out[i]))
        h = h + (merged @ w_out[i]).astype(h.dtype)
    h = rmsnorm(h, final_norm_w)
    return h[:, N_META:]
```

```python
import numpy as np
import concourse.bass as bass
import concourse.mybir as mybir
from concourse.bass_utils import run_bass_kernel_spmd
from contextlib import ExitStack

F32 = mybir.dt.float32
BF16 = mybir.dt.bfloat16
ALU = mybir.AluOpType
AF = mybir.ActivationFunctionType
AX = mybir.AxisListType

N_CORES = 8
DBG = {}
D = 1024
DEPTH = 4
SEQ = 2048
NMETA = 16
LP = 2176
PAD = 112
NT = 17
EPS = 1e-6
BLOCKS = [(0, 512), (512, 512), (1024, 512), (1536, 512), (2048, 128)]
N_DMA_SEMS = 40
MAGIC = 12582912.0
TWO_PI = float(2 * np.pi)


class Prog:
    ENGS = ("pe", "act", "dve", "pool", "sp")

    def __init__(self, nc, stack):
        self.nc = nc
        self.ops = {e: [] for e in self.ENGS}
        self.cnt = {e: 0 for e in self.ENGS}
        self.seen = {e: {} for e in self.ENGS}
        self.state = {}
        self.dma_i = 0
        self.dma_tot = [0] * N_DMA_SEMS
        self.pending_dma = {e: [] for e in self.ENGS}
        self.sems = {e: stack.enter_context(nc.semaphore("s_" + e)) for e in self.ENGS}
        for i in range(N_DMA_SEMS):
            self.sems[("d", i)] = stack.enter_context(nc.semaphore("s_d%d" % i))
        self.nblocks = 0
        self.ninstr = 0
        self.relaxed = False

    def _need(self, eng, tok, waits):
        if tok is None:
            return
        sk, val = tok
        if sk == "pe" and eng == "pe":
            return
        if sk == eng and val > self.cnt[eng] and not DBG.get("strict"):
            return
        if (self.relaxed or eng == "dve") and not DBG.get("strict") and sk == eng and val <= self.cnt[eng] - 1:
            return
        if self.seen[eng].get(sk, 0) >= val:
            return
        self.seen[eng][sk] = val
        waits.append(tok)

    def _deps(self, eng, reads, writes):
        waits = []
        for k in reads:
            st = self.state.get(k)
            if st is not None:
                self._need(eng, st[0], waits)
        for k in writes:
            st = self.state.get(k)
            if st is not None:
                self._need(eng, st[0], waits)
                for t in st[1]:
                    self._need(eng, t, waits)
        return waits

    def _commit(self, tok, reads, writes):
        for k in reads:
            st = self.state.setdefault(k, [None, []])
            st[1].append(tok)
        for k in writes:
            self.state[k] = [tok, []]

    @staticmethod
    def _excl(r, w):
        r2 = [k for k in r if not (isinstance(k, tuple) and k[0] == "bk")]
        if len(r2) == len(r):
            return r, w
        return r2, list(w) + [k for k in r if isinstance(k, tuple) and k[0] == "bk"]

    def op(self, eng, fn, r=(), w=(), sig=True):
        r, w = self._excl(r, w)
        waits = self._deps(eng, r, w)
        tok = (eng, self.cnt[eng] + 1)
        if sig:
            self.cnt[eng] += 1
        self.ops[eng].append((waits, fn, eng if sig else None))
        self._commit(tok, r, w)
        return tok

    def dma(self, eng, out, in_, r=(), w=(), **kw):
        waits = self._deps(eng, r, w)
        si = self.dma_i % N_DMA_SEMS
        self.dma_i += 1
        prev = self.dma_tot[si]
        if prev > 0:
            self._need(eng, (("d", si), prev), waits)
        self.dma_tot[si] = prev + 16
        tok = (("d", si), prev + 16)
        fn = lambda e, out=out, in_=in_, kw=kw: e.dma_start(out=out, in_=in_, **kw)
        self.ops[eng].append((waits, fn, ("d", si)))
        self._commit(tok, r, w)
        self.pending_dma[eng].append(tok)
        return tok

    def flush(self):
        nc = self.nc
        for e in self.ENGS:
            fw = []
            for t in self.pending_dma[e]:
                self._need(e, t, fw)
            self.pending_dma[e] = []
            if fw:
                self.ops[e].append((fw, None, None))
        sems = self.sems
        if DBG.get("dump") and self.nblocks >= DBG["dump"]:
            for e in self.ENGS:
                print("ENGINE", e, "cnt_end", self.cnt[e])
                c = None
                for waits, fn, sg in self.ops[e]:
                    print("   waits", waits, "sig", sg, "fn", None if fn is None else fn.__code__.co_names[-1] if fn.__code__.co_names else "?")
        with nc.Block() as block:
            def run(engname):
                def body(e):
                    for waits, fn, sg in self.ops[engname]:
                        for sk, val in waits:
                            e.wait_ge(sems[sk], val)
                            self.ninstr += 1
                        if fn is None:
                            continue
                        ins = fn(e)
                        self.ninstr += 1
                        if sg is not None:
                            ins.then_inc(sems[sg], 16 if isinstance(sg, tuple) else 1)
                return body
            block.tensor(run("pe"))
            block.scalar(run("act"))
            block.vector(run("dve"))
            block.gpsimd(run("pool"))
            block.sync(run("sp"))
        self.ops = {e: [] for e in self.ENGS}
        self.state = {}
        self.nblocks += 1

    def mm(self, out, lhsT, rhs, start, stop, r, w):
        return self.op("pe", lambda e: e.matmul(out, lhsT, rhs, start=start, stop=stop), r, w, sig=stop)

    def tr(self, out, in_, ident, r, w):
        return self.op("pe", lambda e: e.transpose(out, in_, ident), r, w)

    def act(self, out, in_, func, r, w, bias=None, scale=1.0, accum_out=None):
        kw = {}
        if bias is not None:
            kw["bias"] = bias
        if accum_out is not None:
            kw["accum_out"] = accum_out
        return self.op("act", lambda e: e.activation(out=out, in_=in_, func=func, scale=scale, **kw), r, w)

    def tt(self, eng, out, a, b, op, r, w, sig=True):
        return self.op(eng, lambda e: e.tensor_tensor(out=out, in0=a, in1=b, op=op), r, w, sig=sig)

    def ts(self, eng, out, a, s1, s2, op0, op1, r, w):
        if s2 is None:
            return self.op(eng, lambda e: e.tensor_single_scalar(out=out, in_=a, scalar=s1, op=op0), r, w)
        return self.op(eng, lambda e: e.tensor_scalar(out=out, in0=a, scalar1=s1, scalar2=s2, op0=op0, op1=op1), r, w)

    def stt(self, out, in0, scalar, in1, op0, op1, r, w):
        return self.op("dve", lambda e: e.scalar_tensor_tensor(out=out, in0=in0, scalar=scalar, in1=in1, op0=op0, op1=op1), r, w)

    def cp(self, eng, out, in_, r, w):
        if eng == "act":
            return self.op("act", lambda e: e.copy(out, in_), r, w)
        return self.op(eng, lambda e: e.tensor_copy(out, in_), r, w)

    def ms(self, eng, ap, val, w):
        return self.op(eng, lambda e: e.memset(ap, val), (), w)

    def recip(self, out, in_, r, w):
        return self.op("dve", lambda e: e.reciprocal(out, in_), r, w)

    def scan(self, out, d0, d1, init, op0, op1, r, w):
        return self.op("dve", lambda e: e.tensor_tensor_scan(out=out, data0=d0, data1=d1, initial=init, op0=op0, op1=op1), r, w)


N_FM = 59
T_U, T_ZS5, T_X, T_B, T_C, T_DT, T_Q, T_K, T_V, T_BETA, T_ARAW, T_GATE = 0, 4, 8, 16, 18, 20, 21, 25, 29, 33, 34, 35
W_FM = 0
W_ZSSD = W_FM + N_FM * 1024
W_ZGDN = W_ZSSD + 8 * 1024
W_GLU = W_ZGDN + 8 * 512
W_S5O = W_GLU + 4 * 512
W_SSDO = W_S5O + 4 * 1024
W_GDNO = W_SSDO + 8 * 1024
W_OUT = W_GDNO + 4 * 1024
NW = W_OUT + 8 * 1024


def _kt_layout(w):
    k, c = w.shape
    return np.ascontiguousarray(w.reshape(k // 128, 128, c).transpose(1, 0, 2)).reshape(128, -1)


def _fm_cols():
    tiles = []
    for base, n in ((0, 4), (512, 4), (1024, 8), (3072, 2), (3328, 2)):
        for t in range(n):
            tiles.append(list(range(base + 128 * t, base + 128 * (t + 1))))
    dt = [-1] * 128
    dt[0:16] = range(3584, 3600)
    dt[32:48] = range(3600, 3616)
    tiles.append(dt)
    for base in (3616, 4128, 4640):
        for t in range(4):
            tiles.append(list(range(base + 128 * t, base + 128 * (t + 1))))
    be = [-1] * 128
    be[0:4] = range(5664, 5668)
    be[32:36] = range(5668, 5672)
    tiles.append(be)
    ar = [-1] * 128
    ar[0:4] = range(5672, 5676)
    ar[32:36] = range(5676, 5680)
    tiles.append(ar)
    for b in range(3):
        for t in range(8):
            tiles.append(list(range(5680 + b * 1024 + 128 * t, 5680 + b * 1024 + 128 * (t + 1))))
    assert len(tiles) == N_FM
    return tiles


S_NORMW = 0
S_ARE_A = 8
S_AIM_A = 40
S_LDT_A = 72
S_ARE_B = 104
S_AIM_B = 616
S_LDT_B = 1128
S_BRE = 1640
S_BIM = S_BRE + 4096
S_CRE = S_BIM + 4096
S_CIM = S_CRE + 4096
S_S5D = S_CIM + 4096
S_BGLU = S_S5D + 4
S_SCW = S_BGLU + 4
S_SCB = S_SCW + 60
S_SALOG = S_SCB + 12
S_SDTB = S_SALOG + 1
S_SD = S_SDTB + 1
S_SNW = S_SD + 16
S_GCW = S_SNW + 1024
S_GALOG = S_GCW + 60
S_GDTB = S_GALOG + 1
S_GNW = S_GDTB + 1
NS = S_GNW + 128


def _arrange_layer(i, inp):
    f = np.float32
    wall = np.zeros((128, NW), f)
    win = np.asarray(inp["w_in"][i], f)
    winp = np.concatenate([win, np.zeros((D, 1), f)], axis=1)
    for t, cols in enumerate(_fm_cols()):
        wall[:, W_FM + t * 1024:W_FM + (t + 1) * 1024] = _kt_layout(winp[:, cols])
    wall[:, W_ZSSD:W_ZGDN] = _kt_layout(win[:, 2048:3072])
    wall[:, W_ZGDN:W_GLU] = _kt_layout(win[:, 5152:5664])
    wall[:, W_GLU:W_S5O] = _kt_layout(np.asarray(inp["s5_w_glu"][i], f))
    wall[:, W_S5O:W_SSDO] = _kt_layout(np.asarray(inp["w_s5_out"][i], f))
    wall[:, W_SSDO:W_GDNO] = _kt_layout(np.asarray(inp["w_ssd_out"][i], f))
    wall[:, W_GDNO:W_OUT] = _kt_layout(np.asarray(inp["w_gdn_out"][i], f))
    wall[:, W_OUT:NW] = _kt_layout(np.asarray(inp["w_out"][i], f))

    sm = np.zeros((128, NS), f)
    sm[:, S_NORMW:S_NORMW + 8] = np.asarray(inp["norm_w"][i], f).reshape(8, 128).T
    are = np.asarray(inp["s5_a_re"][i], f)
    aim = np.asarray(inp["s5_a_im"][i], f)
    ldt = np.asarray(inp["s5_log_dt"][i], f)
    for nm, off in ((are, S_ARE_A), (aim, S_AIM_A)):
        a4 = nm.reshape(2, 16, 2, 64)
        sm[:, off:off + 32] = a4.transpose(2, 3, 0, 1).reshape(128, 32)
    l4 = np.broadcast_to(ldt.reshape(2, 16, 2, 1), (2, 16, 2, 64))
    sm[:, S_LDT_A:S_LDT_A + 32] = l4.transpose(2, 3, 0, 1).reshape(128, 32)
    for nm, off in ((are, S_ARE_B), (aim, S_AIM_B)):
        a5 = nm.reshape(2, 4, 4, 2, 1, 64)
        a5 = np.broadcast_to(a5, (2, 4, 4, 2, 16, 64))
        sm[:, off:off + 512] = a5.transpose(2, 3, 4, 0, 1, 5).reshape(128, 512)
    l5 = np.broadcast_to(ldt.reshape(2, 4, 4, 2, 1, 1), (2, 4, 4, 2, 16, 64))
    sm[:, S_LDT_B:S_LDT_B + 512] = l5.transpose(2, 3, 4, 0, 1, 5).reshape(128, 512)
    for nm, off in ((inp["s5_b_re"], S_BRE), (inp["s5_b_im"], S_BIM)):
        b = np.asarray(nm[i], f).reshape(2, 4, 4, 2, 64, 16)
        z = np.zeros((4, 2, 16, 2, 4, 4, 2, 64), f)
        for q in range(4):
            for m in range(2):
                z[q, m, :, :, :, q, m, :] = b[:, :, q, m].transpose(3, 0, 1, 2)
        sm[:, off:off + 4096] = z.reshape(128, 4096)
    for nm, off in ((inp["s5_c_re"], S_CRE), (inp["s5_c_im"], S_CIM)):
        c = np.asarray(nm[i], f).reshape(2, 16, 2, 16, 64)
        z = np.zeros((2, 64, 2, 16, 4, 2, 16), f)
        for pr in range(16):
            for m in range(2):
                z[m, :, :, pr, pr % 4, m, :] = c[:, pr, m].transpose(2, 0, 1)
        sm[:, off:off + 4096] = z.reshape(128, 4096)
    sm[:, S_S5D:S_S5D + 4] = np.asarray(inp["s5_d"][i], f).reshape(4, 128).T
    sm[:, S_BGLU:S_BGLU + 4] = np.asarray(inp["s5_b_glu"][i], f).reshape(4, 128).T
    scw = np.asarray(inp["ssd_conv_w"][i], f)
    sm[:, S_SCW:S_SCW + 60] = scw.reshape(5, 12, 128).transpose(2, 1, 0).reshape(128, 60)
    sm[:, S_SCB:S_SCB + 12] = np.asarray(inp["ssd_conv_b"][i], f).reshape(12, 128).T
    for nm, off in ((inp["ssd_a_log"], S_SALOG), (inp["ssd_dt_bias"], S_SDTB)):
        v = np.asarray(nm[i], f)
        sm[0:16, off] = v[0]
        sm[32:48, off] = v[1]
    sm[:, S_SD:S_SD + 16] = np.asarray(inp["ssd_d"][i], f)[None, :]
    sm[:, S_SNW:S_SNW + 1024] = np.asarray(inp["ssd_norm_w"][i], f)[None, :]
    gcw = np.asarray(inp["gdn_conv_w"][i], f)
    sm[:, S_GCW:S_GCW + 60] = gcw.reshape(5, 12, 128).transpose(2, 1, 0).reshape(128, 60)
    for nm, off in ((inp["gdn_a_log"], S_GALOG), (inp["gdn_dt_bias"], S_GDTB)):
        v = np.asarray(nm[i], f)
        sm[0:4, off] = v[0]
        sm[32:36, off] = v[1]
    sm[:, S_GNW:S_GNW + 128] = np.asarray(inp["gdn_norm_w"][i], f)[None, :]
    return wall, sm


C_IDENT = 0
C_ONES = 128
NC_CONST = 256
K_SEL = 0
K_NEGF = K_SEL + 32 * 128
K_NEGB = K_NEGF + 128
K_LASTF = K_NEGB + 128
K_LASTB = K_LASTF + 128
K_SELG = K_LASTB + 128
K_GNEGF = K_SELG + 8 * 128
K_GNEGB = K_GNEGF + 128
K_GSTRF = K_GNEGB + 128
K_GSTRB = K_GSTRF + 128
K_GLASTF = K_GSTRB + 128
K_GLASTB = K_GLASTF + 128
K_GCHF = K_GLASTB + 128
K_GCHB = K_GCHF + 256
NK = K_GCHB + 256
NEG = -30000.0


def _consts():
    c = np.zeros((128, NC_CONST), np.float32)
    c[:, C_IDENT:C_IDENT + 128] = np.eye(128, dtype=np.float32)
    c[:, C_ONES:C_ONES + 128] = 1.0
    return c


def _consts2():
    k = np.zeros((128, NK), np.float32)
    for d in range(2):
        for h in range(16):
            k[d * 32 + h, K_SEL + (d * 16 + h) * 128:K_SEL + (d * 16 + h + 1) * 128] = 1.0
        for h in range(4):
            k[d * 32 + h, K_SELG + (d * 4 + h) * 128:K_SELG + (d * 4 + h + 1) * 128] = 1.0
    j = np.arange(128)[:, None]
    i = np.arange(128)[None, :]
    same = (j // 64) == (i // 64)
    k[:, K_NEGF:K_NEGF + 128] = np.where(j <= i, 0.0, NEG)
    k[:, K_NEGB:K_NEGB + 128] = np.where(j >= i, 0.0, NEG)
    k[127, K_LASTF:K_LASTF + 128] = 1.0
    k[0, K_LASTB:K_LASTB + 128] = 1.0
    k[:, K_GNEGF:K_GNEGF + 128] = np.where(same & (j <= i), 0.0, NEG)
    k[:, K_GNEGB:K_GNEGB + 128] = np.where(same & (j >= i), 0.0, NEG)
    k[:, K_GSTRF:K_GSTRF + 128] = np.where(same & (j < i), 1.0, 0.0)
    k[:, K_GSTRB:K_GSTRB + 128] = np.where(same & (j > i), 1.0, 0.0)
    k[:, K_GLASTF:K_GLASTF + 128] = np.where(j == 64 * (i // 64) + 63, 1.0, 0.0)
    k[:, K_GLASTB:K_GLASTB + 128] = np.where(j == 64 * (i // 64), 1.0, 0.0)
    for sc in range(2):
        k[64 * sc + 63, K_GCHF + sc * 128:K_GCHF + (sc + 1) * 128] = 1.0
        k[64 * sc, K_GCHB + sc * 128:K_GCHB + (sc + 1) * 128] = 1.0
    return k


_UID = [0]


def sbt(nc, name, shape, dt):
    _UID[0] += 1
    return nc.sbuf_tensor("%s_%d" % (name, _UID[0]), shape, dt)


def rawap(t_ap, off, dims):
    return bass.AP(t_ap.tensor, t_ap.offset + off, [list(t_ap.ap[0])] + [list(d) for d in dims])


def fm_tile(env, l, t, n=1):
    return env["wallb"][l, :, W_FM + t * 1024:W_FM + (t + n) * 1024].rearrange("p (n k c) -> p n k c", n=n, k=8)


def phase_norm(env, l):
    nc, P, banks = env["nc"], env["P"], env["banks"]
    with ExitStack() as ph:
        nw = ph.enter_context(sbt(nc, "nw", [128, 8], F32))
        P.dma("sp", nw[:], env["small_in"][l, :, S_NORMW:S_NORMW + 8], w=["nw"])
        hb = [ph.enter_context(sbt(nc, "nh%d" % i, [128, 8, 512], F32)) for i in range(2)]
        sqb = [ph.enter_context(sbt(nc, "nsq%d" % i, [128, 8, 512], BF16)) for i in range(2)]
        rsb = [ph.enter_context(sbt(nc, "nrs%d" % i, [128, 512], F32)) for i in range(2)]
        hnb = [ph.enter_context(sbt(nc, "nhn%d" % i, [128, 8, 512], BF16)) for i in range(2)]
        it = 0
        for s in range(env["nseq"]):
            for (c0, bw) in BLOCKS:
                i = it % 2
                it += 1
                key = "n%d" % i
                P.dma("sp", hb[i][:, :, 0:bw], env["hT"][s, :, :, c0:c0 + bw], w=[key + "h"])
                P.act(sqb[i][:, :, 0:bw], hb[i][:, :, 0:bw], AF.Square, [key + "h"], [key + "sq"])
                for kt in range(8):
                    P.mm(banks[it % 2][:, 0:bw], env["onesb"], sqb[i][:, kt, 0:bw], kt == 0, kt == 7, ["cstb", key + "sq"], [("bk", it % 2)])
                P.act(rsb[i][:, 0:bw], banks[it % 2][:, 0:bw], AF.Sqrt, [("bk", it % 2)], [key + "rs"], bias=EPS, scale=1.0 / D)
                P.recip(rsb[i][:, 0:bw], rsb[i][:, 0:bw], [key + "rs"], [key + "rs"])
                for kt in range(8):
                    P.stt(hnb[i][:, kt, 0:bw], hb[i][:, kt, 0:bw], nw[:, kt:kt + 1], rsb[i][:, 0:bw], ALU.mult, ALU.mult,
                          [key + "h", key + "rs", "nw"], [key + "hn"])
                P.dma("sp", env["hnT"][s, :, :, c0:c0 + bw], hnb[i][:, :, 0:bw], r=[key + "hn"])
        P.flush()


def rr_sin(P, out, x, shift, tmp, rk, wk, tk):
    P.ts("dve", out, x, float(shift), None, ALU.add, None, rk, [wk])
    P.ts("dve", tmp, out, 1.0 / TWO_PI, MAGIC, ALU.mult, ALU.add, [wk], [tk])
    P.ts("dve", tmp, tmp, -MAGIC, -TWO_PI, ALU.add, ALU.mult, [tk], [tk])
    P.tt("dve", tmp, tmp, out, ALU.add, [tk, wk], [tk])
    P.act(out, tmp, AF.Sin, [tk], [wk])


def s5_lambda(P, nc, ph, src, n, tag):
    t = {}
    for nm in ("dt", "xr", "th", "mag", "sn", "cs", "lr", "li", "tmp"):
        t[nm] = ph.enter_context(sbt(nc, tag + nm, [128, n], F32))
    k = tag
    P.act(t["dt"][:], src[:, 2, :], AF.Exp, [k + "src"], [k + "dt"])
    P.tt("dve", t["xr"][:], src[:, 0, :], t["dt"][:], ALU.mult, [k + "src", k + "dt"], [k + "xr"])
    P.tt("dve", t["th"][:], src[:, 1, :], t["dt"][:], ALU.mult, [k + "src", k + "dt"], [k + "th"])
    P.act(t["mag"][:], t["xr"][:], AF.Exp, [k + "xr"], [k + "mag"])
    rr_sin(P, t["sn"][:], t["th"][:], 0.0, t["tmp"][:], [k + "th"], k + "sn", k + "tmp")
    rr_sin(P, t["cs"][:], t["th"][:], float(np.pi / 2), t["tmp"][:], [k + "th"], k + "cs", k + "tmp")
    P.tt("dve", t["lr"][:], t["mag"][:], t["cs"][:], ALU.mult, [k + "mag", k + "cs"], [k + "lr"])
    P.tt("dve", t["li"][:], t["mag"][:], t["sn"][:], ALU.mult, [k + "mag", k + "sn"], [k + "li"])
    return t


def phase_s5(env, l):
    nc, P, banks = env["nc"], env["P"], env["banks"]
    small = env["small_in"]
    with ExitStack() as lay:
        LA = lay.enter_context(sbt(nc, "LA", [128, 32, 2], F32))
        LB = lay.enter_context(sbt(nc, "LB", [128, 32, 2], F32))
        WB = lay.enter_context(sbt(nc, "WB", [128, 2, 4096], BF16))
        WO = lay.enter_context(sbt(nc, "WO", [128, 2, 4096], BF16))
        s5d = lay.enter_context(sbt(nc, "s5d", [128, 8], F32))
        P.dma("sp", s5d[:], small[l, :, S_S5D:S_S5D + 8], w=["s5d"])
        with ExitStack() as ph:
            srcA = ph.enter_context(sbt(nc, "srcA", [128, 3, 32], F32))
            srcB = ph.enter_context(sbt(nc, "srcB", [128, 3, 512], F32))
            P.dma("sp", srcA[:], small[l, :, S_ARE_A:S_ARE_A + 96].rearrange("p (a n) -> p a n", a=3), w=["Asrc"])
            P.dma("sp", srcB[:], small[l, :, S_ARE_B:S_ARE_B + 1536].rearrange("p (a n) -> p a n", a=3), w=["Bsrc"])
            tA = s5_lambda(P, nc, ph, srcA, 32, "A")
            P.cp("dve", LA[:, :, 0], tA["lr"][:], ["Alr"], ["LA0"])
            P.cp("dve", LA[:, :, 1], tA["lr"][:], ["Alr"], ["LA1"])
            P.ts("dve", LB[:, :, 0], tA["li"][:], -1.0, None, ALU.mult, None, ["Ali"], ["LB0"])
            P.cp("dve", LB[:, :, 1], tA["li"][:], ["Ali"], ["LB1"])
            tB = s5_lambda(P, nc, ph, srcB, 512, "B")
            den = ph.enter_context(sbt(nc, "den", [128, 512], F32))
            t1 = ph.enter_context(sbt(nc, "pt1", [128, 512], F32))
            t2 = ph.enter_context(sbt(nc, "pt2", [128, 512], F32))
            fre = ph.enter_context(sbt(nc, "fre", [128, 512], F32))
            fim = ph.enter_context(sbt(nc, "fim", [128, 512], F32))
            are, aim = srcB[:, 0, :], srcB[:, 1, :]
            P.tt("dve", den[:], are, are, ALU.mult, ["Bsrc"], ["den"])
            P.tt("dve", t1[:], aim, aim, ALU.mult, ["Bsrc"], ["t1"])
            P.tt("dve", den[:], den[:], t1[:], ALU.add, ["den", "t1"], ["den"])
            P.recip(den[:], den[:], ["den"], ["den"])
            lm1 = tB["tmp"]
            P.ts("dve", lm1[:], tB["lr"][:], -1.0, None, ALU.add, None, ["Blr"], ["Btmp"])
            P.tt("dve", t1[:], lm1[:], are, ALU.mult, ["Btmp", "Bsrc", "den"], ["t1"])
            P.tt("dve", t2[:], tB["li"][:], aim, ALU.mult, ["Bli", "Bsrc"], ["t2"])
            P.tt("dve", t1[:], t1[:], t2[:], ALU.add, ["t1", "t2"], ["t1"])
            P.tt("dve", fre[:], t1[:], den[:], ALU.mult, ["t1", "den"], ["fre"])
            P.tt("dve", t1[:], tB["li"][:], are, ALU.mult, ["Bli", "Bsrc", "fre"], ["t1"])
            P.tt("dve", t2[:], lm1[:], aim, ALU.mult, ["Btmp", "Bsrc"], ["t2"])
            P.tt("dve", t1[:], t1[:], t2[:], ALU.subtract, ["t1", "t2"], ["t1"])
            P.tt("dve", fim[:], t1[:], den[:], ALU.mult, ["t1", "den"], ["fim"])
            braw = ph.enter_context(sbt(nc, "braw", [128, 2, 4096], F32))
            P.dma("sp", braw[:], small[l, :, S_BRE:S_BRE + 8192].rearrange("p (a n) -> p a n", a=2), w=["braw"])
            bt1 = ph.enter_context(sbt(nc, "bt1", [128, 4096], F32))
            bt2 = ph.enter_context(sbt(nc, "bt2", [128, 4096], F32))
            v4 = lambda ap: ap.rearrange("p (d q n) -> p d q n", d=8, q=8)
            fb = lambda f: f[:].rearrange("p (d n) -> p d n", d=8).unsqueeze(2).broadcast_to([128, 8, 8, 64])
            P.tt("dve", v4(bt1[:]), v4(braw[:, 0, :]), fb(fre), ALU.mult, ["braw", "fre"], ["bt1"])
            P.tt("dve", v4(bt2[:]), v4(braw[:, 1, :]), fb(fim), ALU.mult, ["braw", "fim"], ["bt2"])
            P.tt("dve", WB[:, 0, :], bt1[:], bt2[:], ALU.subtract, ["bt1", "bt2"], ["WB0"])
            P.tt("dve", v4(bt1[:]), v4(braw[:, 1, :]), fb(fre), ALU.mult, ["braw", "fre", "WB0"], ["bt1"])
            P.tt("dve", v4(bt2[:]), v4(braw[:, 0, :]), fb(fim), ALU.mult, ["braw", "fim", "WB0"], ["bt2"])
            P.tt("dve", WB[:, 1, :], bt1[:], bt2[:], ALU.add, ["bt1", "bt2"], ["WB1"])
            P.dma("sp", braw[:], small[l, :, S_CRE:S_CRE + 8192].rearrange("p (a n) -> p a n", a=2), r=["WB0", "WB1"], w=["craw"])
            P.cp("act", WO[:, 0, :], braw[:, 0, :], ["craw"], ["WO0"])
            P.ts("dve", WO[:, 1, :], braw[:, 1, :], -1.0, None, ALU.mult, None, ["craw"], ["WO1"])
            P.flush()

        NS = env["nseq"]
        TB = 64
        NB = LP // TB
        with ExitStack() as ph:
            uT = [ph.enter_context(sbt(nc, "uT", [128, 4, LP], BF16)) for s in range(NS)]
            ybuf = [ph.enter_context(sbt(nc, "ybuf", [128, 4, LP], BF16)) for s in range(NS)]
            with ExitStack() as pa:
                hnb = [pa.enter_context(sbt(nc, "s5hn%d" % i, [128, 8, 512], BF16)) for i in range(2)]
                wu = pa.enter_context(sbt(nc, "wu", [128, 4, 8, 128], BF16))
                P.dma("sp", wu[:], fm_tile(env, l, T_U, 4), w=["wu"])
                it = 0
                for s in range(NS):
                    for (c0, bw) in BLOCKS:
                        i = it % 2
                        it += 1
                        P.dma("sp", hnb[i][:, :, 0:bw], env["hnT"][s, :, :, c0:c0 + bw], w=[("hn", i)])
                        for a in range(4):
                            bk = (it * 4 + a) % 8
                            for kt in range(8):
                                P.mm(banks[bk][:, 0:bw], wu[:, a, kt, :], hnb[i][:, kt, 0:bw], kt == 0, kt == 7, ["wu", ("hn", i)], [("bk", bk)])
                            P.cp("act" if a % 2 else "dve", uT[s][:, a, c0:c0 + bw], banks[bk][:, 0:bw], [("bk", bk)], [("uT", s, a)])
                P.flush()
            with ExitStack() as phb:
                NX = NS * 64
                BU = phb.enter_context(sbt(nc, "BU", [128, NX, TB], F32))
                cur = phb.enter_context(sbt(nc, "Hh", [128, NX, TB], F32))
                Hc = phb.enter_context(sbt(nc, "Hc", [128, NX], F32))
                Hb = phb.enter_context(sbt(nc, "Hb", [128, NX, TB], BF16))
                P1 = phb.enter_context(sbt(nc, "P1", [128, NX], F32))
                Q1 = phb.enter_context(sbt(nc, "Q1", [128, NX], F32))
                ytmp = [phb.enter_context(sbt(nc, "ytmp%d" % i, [128, TB], F32)) for i in range(4)]
                CH = 1024
                cjobs = []
                if env.get("cast_next") and l + 1 < env["n_layers"]:
                    cstg = [phb.enter_context(sbt(nc, "cstg%d" % i, [128, CH], F32)) for i in range(2)]
                    cstgb = [phb.enter_context(sbt(nc, "cstgb%d" % i, [128, CH], BF16)) for i in range(2)]
                    cjobs = [(c0, min(CH, NW - c0)) for c0 in range(0, NW, CH)]
                P.ms("pool", Hc[:], 0.0, ["Hc"])
                LAd = [LA[:].rearrange("p (d q) t -> p d (q t)", d=2)[:, d, :].unsqueeze(1).broadcast_to([128, NS, 32]) for d in range(2)]
                LBd = [LB[:].rearrange("p (d q) t -> p d q t", d=2)[:, d, :, :].unsqueeze(1).broadcast_to([128, NS, 16, 2]) for d in range(2)]
                P1d = [P1[:, d * NS * 32:(d + 1) * NS * 32].rearrange("p (s x) -> p s x", s=NS) for d in range(2)]
                Q1d = [Q1[:, d * NS * 32:(d + 1) * NS * 32].rearrange("p (s x) -> p s x", s=NS) for d in range(2)]
                Q1d4 = [Q1[:, d * NS * 32:(d + 1) * NS * 32].rearrange("p (s q t) -> p s q t", s=NS, q=16) for d in range(2)]
                BUf = BU[:].rearrange("p x t -> p (x t)")
                nyt = 0
                for tb in range(NB):
                    tbd = (tb, NB - 1 - tb)
                    g = 0
                    for s in range(NS):
                        for d in range(2):
                            for kk in range(4):
                                bk = g % 8
                                g += 1
                                for j8 in range(8):
                                    pr, part = 4 * kk + j8 // 2, j8 % 2
                                    a, q0 = pr // 4, pr % 4
                                    col = ((d * 4 + a) * 4 + q0) * 128
                                    P.mm(banks[bk][:, j8 * TB:(j8 + 1) * TB], WB[:, part, col:col + 128],
                                         uT[s][:, a, tbd[d] * TB:(tbd[d] + 1) * TB], True, True, [("uT", s, a)], [("bk", bk)])
                                x0 = s * 64 + d * 32 + kk * 8
                                P.cp("act", BUf[:, x0 * TB:(x0 + 8) * TB], banks[bk][:, :], [("bk", bk)], ["BU"])
                    for _ in range(3):
                        if cjobs:
                            c0_, cw_ = cjobs.pop(0)
                            ci_ = len(cjobs) % 2
                            P.dma("sp", cstg[ci_][:, 0:cw_], env["wall_in"][l + 1, :, c0_:c0_ + cw_], w=[("cstg", ci_)])
                            P.cp("pool", cstgb[ci_][:, 0:cw_], cstg[ci_][:, 0:cw_], [("cstg", ci_)], [("cstgb", ci_)])
                            P.dma("sp", env["wallb"][l + 1, :, c0_:c0_ + cw_], cstgb[ci_][:, 0:cw_], r=[("cstgb", ci_)])
                    P.relaxed = True
                    for j in range(TB):
                        tok = (j, TB - 1 - j)
                        hp, hps, hn_, bu_, pk = [], [], [], [], []
                        for d in range(2):
                            base = d * 32 * TB
                            if j == 0:
                                hp.append(rawap(Hc[:], d * 32, [[64, NS], [1, 32]]))
                                hps.append(rawap(Hc[:], d * 32 + 1, [[64, NS], [2, 16], [-1, 2]]))
                                pk.append("Hc")
                            else:
                                tp_ = tok[d] - 1 if d == 0 else tok[d] + 1
                                hp.append(rawap(cur[:], base + tp_, [[64 * TB, NS], [TB, 32]]))
                                hps.append(rawap(cur[:], base + tp_ + TB, [[64 * TB, NS], [2 * TB, 16], [-TB, 2]]))
                                pk.append(("Hh", d))
                            hn_.append(rawap(cur[:], base + tok[d], [[64 * TB, NS], [TB, 32]]))
                            bu_.append(rawap(BU[:], base + tok[d], [[64 * TB, NS], [TB, 32]]))
                        sg_ = bool(DBG.get("strict"))
                        for d in range(2):
                            P.tt("dve", P1d[d], LAd[d], hp[d], ALU.mult, [pk[d]], [("P1", d)], sig=sg_)
                        for d in range(2):
                            P.tt("dve", Q1d4[d], LBd[d], hps[d], ALU.mult, [pk[d]], [("Q1", d)], sig=sg_)
                        for d in range(2):
                            P.tt("dve", P1d[d], P1d[d], Q1d[d], ALU.add, [("P1", d), ("Q1", d)], [("P1", d)], sig=sg_)
                        for d in range(2):
                            P.tt("dve", hn_[d], P1d[d], bu_[d], ALU.add, [("P1", d), "BU"], [("Hh", d)], sig=sg_)
                    P.relaxed = False
                    if tb == NB - 1:
                        assert not cjobs
                    for d in range(2):
                        last = TB - 1 if d == 0 else 0
                        P.cp("dve", rawap(Hc[:], d * 32, [[64, NS], [1, 32]]), rawap(cur[:], d * 32 * TB + last, [[64 * TB, NS], [TB, 32]]),
                             [("Hh", d)], ["Hc"])
                    for s in range(NS):
                        P.cp("pool" if s % 2 == 0 else "act", Hb[:, s * 64:(s + 1) * 64, :], cur[:, s * 64:(s + 1) * 64, :], [("Hh", 0), ("Hh", 1)], [("Hb", s)])
                    for s in range(NS):
                        for d in range(2):
                            b_ = tbd[d]
                            first = (d == 0 and b_ <= NB // 2 - 1) or (d == 1 and b_ >= NB // 2)
                            for a in range(4):
                                bk = (s * 8 + d * 4 + a) % 8
                                n_ = 0
                                for q0 in range(4):
                                    for part in range(2):
                                        pr = 4 * a + q0
                                        col = (d * 16 + pr) * 128
                                        P.mm(banks[bk][:, 0:TB], WO[:, part, col:col + 128], Hb[:, s * 64 + d * 32 + pr * 2 + part, :],
                                             n_ == 0, n_ == 7, [("Hb", s)], [("bk", bk)])
                                        n_ += 1
                                ysl = ybuf[s][:, a, b_ * TB:(b_ + 1) * TB]
                                yk = ("yb", s, a, b_)
                                if first:
                                    P.cp("act", ysl, banks[bk][:, 0:TB], [("bk", bk)], [yk])
                                else:
                                    yt = ytmp[nyt % 4]
                                    P.cp("act", yt[:], banks[bk][:, 0:TB], [("bk", bk)], [("ytmp", nyt % 4)])
                                    P.tt("pool", ysl, ysl, yt[:], ALU.add, [("ytmp", nyt % 4), yk], [yk])
                                    nyt += 1
                P.flush()
            with ExitStack() as pc:
                hnb = [pc.enter_context(sbt(nc, "s5hnc%d" % i, [128, 8, 512], BF16)) for i in range(2)]
                wzs = pc.enter_context(sbt(nc, "wzs", [128, 4, 8, 128], BF16))
                wglu = pc.enter_context(sbt(nc, "wglu", [128, 4, 512], BF16))
                P.dma("sp", wzs[:], fm_tile(env, l, T_ZS5, 4), w=["wzs"])
                P.dma("sp", wglu[:], env["wallb"][l, :, W_GLU:W_GLU + 2048].rearrange("p (k c) -> p k c", k=4), w=["wglu"])
                yv = pc.enter_context(sbt(nc, "yv", [128, 4, 512], F32))
                ygb = pc.enter_context(sbt(nc, "ygb", [128, 4, 512], BF16))
                sg = pc.enter_context(sbt(nc, "sg", [128, 512], F32))
                zs = pc.enter_context(sbt(nc, "zs", [128, 512], F32))
                yo = [pc.enter_context(sbt(nc, "yo%d" % i, [128, 4, 512], BF16)) for i in range(2)]
                it = 0
                for s in range(NS):
                    for (c0, bw) in BLOCKS:
                        i = it % 2
                        it += 1
                        P.dma("sp", hnb[i][:, :, 0:bw], env["hnT"][s, :, :, c0:c0 + bw], w=[("hn", i)])
                        for a in range(4):
                            P.stt(yv[:, a, 0:bw], uT[s][:, a, c0:c0 + bw], s5d[:, a:a + 1], ybuf[s][:, a, c0:c0 + bw], ALU.mult, ALU.add,
                                  [], [("yv", a)])
                            P.act(yv[:, a, 0:bw], yv[:, a, 0:bw], AF.Gelu_apprx_tanh, [("yv", a)], [("yv", a)])
                            P.cp("pool", ygb[:, a, 0:bw], yv[:, a, 0:bw], [("yv", a)], [("ygb", a)])
                        for ao in range(4):
                            b0, b1 = ao % 2, 2 + ao % 2
                            for kt in range(4):
                                P.mm(banks[b0][:, 0:bw], wglu[:, kt, ao * 128:(ao + 1) * 128], ygb[:, kt, 0:bw], kt == 0, kt == 3,
                                     [("ygb", k) for k in range(4)] + ["wglu"], [("bk", b0)])
                            P.act(sg[:, 0:bw], banks[b0][:, 0:bw], AF.Sigmoid, [("bk", b0)], ["sg"], bias=s5d[:, 4 + ao:5 + ao])
                            for kt in range(8):
                                P.mm(banks[b1][:, 0:bw], wzs[:, ao, kt, :], hnb[i][:, kt, 0:bw], kt == 0, kt == 7, [("hn", i), "wzs"], [("bk", b1)])
                            P.act(zs[:, 0:bw], banks[b1][:, 0:bw], AF.Silu, [("bk", b1)], ["zs"])
                            P.tt("dve", sg[:, 0:bw], sg[:, 0:bw], yv[:, ao, 0:bw], ALU.mult, ["sg", ("yv", ao)], ["sg"])
                            P.tt("dve", yo[i][:, ao, 0:bw], sg[:, 0:bw], zs[:, 0:bw], ALU.mult, ["sg", "zs"], [("yo", i)])
                        P.dma("sp", env["ys5T"][s, :, :, c0:c0 + bw], yo[i][:, :, 0:bw], r=[("yo", i)])
                P.flush()


def phase_ssd(env, l):
    nc, P, banks = env["nc"], env["P"], env["banks"]
    small, K2 = env["small_in"], env["const2"]
    identf, identb = env["identf"], env["identb"]
    with ExitStack() as lay:
        scw = lay.enter_context(sbt(nc, "scw", [128, 72], F32))
        sal = lay.enter_context(sbt(nc, "sal", [64, 2], F32))
        nega = lay.enter_context(sbt(nc, "nega", [64, 1], F32))
        sdn = lay.enter_context(sbt(nc, "sdn", [128, 1040], F32))
        wz = lay.enter_context(sbt(nc, "wz", [128, 8, 1024], BF16))
        selb = lay.enter_context(sbt(nc, "selb", [64, 32, 128], BF16))
        negm = lay.enter_context(sbt(nc, "negm", [128, 2, 128], BF16))
        lastm = lay.enter_context(sbt(nc, "lastm", [128, 2, 128], F32))
        with ExitStack() as p0:
            stg = p0.enter_context(sbt(nc, "kstg", [128, 4096 + 256], F32))
            P.dma("sp", scw[:], small[l, :, S_SCW:S_SCW + 72], w=["scw"])
            P.dma("sp", sal[:], small[l, 0:64, S_SALOG:S_SALOG + 2], w=["sal"])
            P.dma("sp", sdn[:], small[l, :, S_SD:S_SD + 1040], w=["sdn"])
            P.dma("sp", wz[:], env["wallb"][l, :, W_ZSSD:W_ZSSD + 8192].rearrange("p (k c) -> p k c", k=8), w=["wz"])
            P.dma("sp", stg[:], K2[:, K_SEL:K_SEL + 4096 + 256], w=["kstg"])
            P.dma("sp", lastm[:], K2[:, K_LASTF:K_LASTF + 256].rearrange("p (a c) -> p a c", a=2), w=["lastm"])
            P.cp("dve", selb[:].rearrange("p a c -> p (a c)"), stg[0:64, 0:4096], ["kstg"], ["selb"])
            P.cp("dve", negm[:].rearrange("p a c -> p (a c)"), stg[:, 4096:4352], ["kstg"], ["negm"])
            P.act(nega[:], sal[:, 0:1], AF.Exp, ["sal"], ["nega"])
            P.ts("dve", nega[:], nega[:], -1.0, None, ALU.mult, None, ["nega"], ["nega"])
            P.flush()

        for s in range(env["nseq"]):
            with ExitStack() as ph:
                hn = ph.enter_context(sbt(nc, "shn", [128, 8, LP], BF16))
                xtok = ph.enter_context(sbt(nc, "xtok", [128, NT, 1024], BF16))
                Btok = ph.enter_context(sbt(nc, "Btok", [128, NT, 256], BF16))
                BT = ph.enter_context(sbt(nc, "BT", [128, 2, LP], BF16))
                CT = ph.enter_context(sbt(nc, "CT", [128, 2, LP], BF16))
                atok = ph.enter_context(sbt(nc, "atok", [128, NT, 4, 64], F32))
                ahl = ph.enter_context(sbt(nc, "ahl", [64, 2, LP], BF16))
                for kt in range(8):
                    P.dma("sp", hn[:, kt, :], env["hnT"][s, :, kt, :], w=[("hn", kt)])
                hnk = [("hn", kt) for kt in range(8)]
                with ExitStack() as p1:
                    wcv = p1.enter_context(sbt(nc, "wcv", [128, 12, 8, 128], BF16))
                    raw = [p1.enter_context(sbt(nc, "raw%d" % i, [128, LP + 4], F32)) for i in range(2)]
                    acc1 = p1.enter_context(sbt(nc, "acc", [128, LP], F32))
                    acc = [acc1, acc1]
                    xc1 = p1.enter_context(sbt(nc, "xc", [128, LP], BF16))
                    xc = [xc1, xc1]
                    P.dma("sp", wcv[:], fm_tile(env, l, T_X, 12), w=["wcv"])
                    for i in range(2):
                        P.ms("pool", raw[i][:, 0:2], 0.0, [("raw", i)])
                        P.ms("pool", raw[i][:, LP + 2:LP + 4], 0.0, [("raw", i)])
                    nb_ = 0
                    for ct in range(12):
                        i = ct % 2
                        for (c0, bw) in BLOCKS:
                            bk = nb_ % 4
                            nb_ += 1
                            for kt in range(8):
                                P.mm(banks[bk][:, 0:bw], wcv[:, ct, kt, :], hn[:, kt, c0:c0 + bw], kt == 0, kt == 7, ["wcv"] + hnk, [("bk", bk)])
                            P.cp("act", raw[i][:, 2 + c0:2 + c0 + bw], banks[bk][:, 0:bw], [("bk", bk)], [("raw", i)])
                        P.ts("dve", acc[i][:], raw[i][:, 0:LP], scw[:, ct * 5:ct * 5 + 1], None, ALU.mult, None, [("raw", i), "scw"], ["acc"])
                        for k in range(1, 5):
                            P.stt(acc[i][:], raw[i][:, k:k + LP], scw[:, ct * 5 + k:ct * 5 + k + 1], acc[i][:], ALU.mult, ALU.add,
                                  [("raw", i), "acc", "scw"], ["acc"])
                        if ct < 8:
                            dst, dk = xc[i][:], "xc"
                        elif ct < 10:
                            dst, dk = BT[:, ct - 8, :], ("BT", ct - 8)
                        else:
                            dst, dk = CT[:, ct - 10, :], ("CT", ct - 10)
                        P.act(dst, acc[i][:], AF.Silu, ["acc", "scw"], [dk], bias=scw[:, 60 + ct:61 + ct])
                        P.ms("pool", dst[:, 0:PAD], 0.0, [dk])
                        if ct < 10:
                            for t0 in range(0, NT, 4):
                                n = min(4, NT - t0)
                                bk = 4 + (t0 // 4) % 4
                                bkb = banks[bk][:].bitcast(BF16)
                                for q in range(n):
                                    P.tr(bkb[:, q * 128:(q + 1) * 128], dst[:, (t0 + q) * 128:(t0 + q + 1) * 128], identb, [dk, "cstb"], [("bk", bk)])
                                if ct < 8:
                                    dd = xtok[:, t0:t0 + n, ct * 128:(ct + 1) * 128]
                                    dkk = "xtok"
                                else:
                                    dd = Btok[:, t0:t0 + n, (ct - 8) * 128:(ct - 7) * 128]
                                    dkk = "Btok"
                                P.cp("dve", dd, bkb[:, 0:n * 128].rearrange("p (a c) -> p a c", a=n), [("bk", bk)], [dkk])
                    P.flush()
                with ExitStack() as p2:
                    wdt = p2.enter_context(sbt(nc, "wdt", [128, 8, 128], BF16))
                    dtT = p2.enter_context(sbt(nc, "dtT", [64, LP], F32))
                    dtaT = p2.enter_context(sbt(nc, "dtaT", [64, LP], F32))
                    acT = p2.enter_context(sbt(nc, "acT", [64, LP], F32))
                    rmf = p2.enter_context(sbt(nc, "rmf", [64, LP], F32))
                    rmb = p2.enter_context(sbt(nc, "rmb", [64, LP], F32))
                    P.dma("sp", wdt[:], fm_tile(env, l, T_DT, 1)[:, 0], w=["wdt"])
                    for bi, (c0, bw) in enumerate(BLOCKS):
                        bk = bi % 4
                        for kt in range(8):
                            P.mm(banks[bk][0:64, 0:bw], wdt[:, kt, 0:64], hn[:, kt, c0:c0 + bw], kt == 0, kt == 7, ["wdt"], [("bk", bk)])
                        P.act(dtT[:, c0:c0 + bw], banks[bk][0:64, 0:bw], AF.Exp, [("bk", bk)], ["dtT"], bias=sal[:, 1:2])
                        P.act(dtT[:, c0:c0 + bw], dtT[:, c0:c0 + bw], AF.Ln, ["dtT"], ["dtT"], bias=1.0)
                    P.ms("dve", dtT[:, 0:PAD], 0.0, ["dtT"])
                    P.ts("dve", dtaT[:], dtT[:], nega[:, 0:1], None, ALU.mult, None, ["dtT"], ["dtaT"])
                    P.ms("pool", rmf[:], 1.0, ["rmf"])
                    P.ms("pool", rmf[:, 0:LP:128], 0.0, ["rmf"])
                    P.ms("pool", rmb[:], 1.0, ["rmb"])
                    P.ms("pool", rmb[:, 127:LP:128], 0.0, ["rmb"])
                    P.scan(acT[0:32, :], rmf[0:32, :], dtaT[0:32, :], 0.0, ALU.mult, ALU.add, ["rmf", "dtaT"], ["acT0"])
                    P.scan(acT[32:64, ::-1], rmb[32:64, ::-1], dtaT[32:64, ::-1], 0.0, ALU.mult, ALU.add, ["rmb", "dtaT"], ["acT1"])
                    P.cp("dve", ahl[:, 0, :], acT[:], ["acT0", "acT1"], ["ahl0"])
                    P.tt("dve", ahl[:, 1, :], acT[:], ahl[:, 0, :], ALU.subtract, ["acT0", "acT1", "ahl0"], ["ahl1"])
                    for tt_ in range(NT):
                        bk = 4 + tt_ % 4
                        P.tr(banks[bk][:, 0:64], acT[:, tt_ * 128:(tt_ + 1) * 128], identf[0:64, 0:64], ["acT0", "acT1", "cst"], [("bk", bk)])
                        P.tr(banks[bk][:, 64:128], dtT[:, tt_ * 128:(tt_ + 1) * 128], identf[0:64, 0:64], ["dtT", "cst"], [("bk", bk)])
                        P.cp("act", atok[:, tt_, 0, :], banks[bk][:, 0:64], [("bk", bk)], ["atok0"])
                        P.cp("act", atok[:, tt_, 3, :], banks[bk][:, 64:128], [("bk", bk)], ["atok3"])
                    P.ts("dve", atok[:, :, 1, :], atok[:, :, 0, :], -1.0, None, ALU.mult, None, ["atok0"], ["atok1"])
                    P.act(atok[:, :, 2, :], atok[:, :, 0, :], AF.Exp, ["atok0"], ["atok2"])
                    P.flush()
                with ExitStack() as p3:
                    S = p3.enter_context(sbt(nc, "S", [128, 1024], F32))
                    Sb = p3.enter_context(sbt(nc, "Sb", [128, 1024], BF16))
                    xdt = p3.enter_context(sbt(nc, "xdt", [128, 1024], BF16))
                    xdtt = p3.enter_context(sbt(nc, "xdtt", [128, 1024], BF16))
                    E = [p3.enter_context(sbt(nc, "E%d" % i, [128, 128], BF16)) for i in range(2)]
                    M = [p3.enter_context(sbt(nc, "M%d" % i, [128, 128], BF16)) for i in range(2)]
                    tl = p3.enter_context(sbt(nc, "tl", [128, 2, 16], F32))
                    tY = p3.enter_context(sbt(nc, "tY", [128, 1024], F32))
                    yv = p3.enter_context(sbt(nc, "yv", [128, 1024], F32))
                    yfb = p3.enter_context(sbt(nc, "yfb", [128, 1024], BF16))
                    zs = p3.enter_context(sbt(nc, "zs", [128, 1024], F32))
                    junk = p3.enter_context(sbt(nc, "junk", [128, 512], F32))
                    ss = p3.enter_context(sbt(nc, "ss", [128, 2], F32))
                    yn = p3.enter_context(sbt(nc, "yn", [128, 1024], BF16))
                    yTs = p3.enter_context(sbt(nc, "yTs", [128, 8, 128], BF16))
                    h3 = lambda ap, nh: ap.rearrange("p (h c) -> p h c", h=nh)
                    bc = lambda ap, nh: ap.unsqueeze(2).broadcast_to([128, nh, 64])
                    for d in range(2):
                        rb = d * 32
                        P.ms("pool", S[:], 0.0, ["S0", "S1"])
                        P.ms("pool", Sb[:], 0.0, ["Sb0", "Sb1"])
                        order = list(range(NT)) if d == 0 else list(range(NT - 1, -1, -1))
                        for c in order:
                            cs, ce = c * 128, (c + 1) * 128
                            for g in range(2):
                                P.mm(banks[0][:, g * 128:(g + 1) * 128], BT[:, g, cs:ce], CT[:, g, cs:ce], True, True, [], [("bk", 0)])
                            P.mm(banks[0][:, 256:272], lastm[:, d, :], atok[:, c, 0, rb:rb + 16], True, True, [], [("bk", 0)])
                            P.tt("dve", tl[:, 0, :], banks[0][:, 256:272], atok[:, c, 0, rb:rb + 16], ALU.subtract, [("bk", 0)], ["tl0"])
                            P.act(tl[:, 0, :], tl[:, 0, :], AF.Exp, ["tl0"], ["tl0"])
                            P.act(tl[:, 1, :], banks[0][:, 256:272], AF.Exp, [("bk", 0)], ["tl1"])
                            P.tt("dve", h3(xdt[:], 16), h3(xtok[:, c, :], 16), bc(atok[:, c, 3, rb:rb + 16], 16), ALU.mult, [], ["xdt"])
                            for g in range(2):
                                P.mm(banks[4 + g][:, :], CT[:, g, cs:ce], Sb[:, g * 512:(g + 1) * 512], True, True, ["Sb%d" % g], [("bk", 4 + g)])

                            def dp_mm(h):
                                e = h % 2
                                dpb = 1 if e == 0 else 6
                                dp = banks[dpb][:, 0:128]
                                P.mm(dp, selb[:, d * 16 + h, :], ahl[:, 0, cs:ce], True, False, [], [("bk", dpb)])
                                P.mm(dp, selb[:, d * 16 + h, :], ahl[:, 1, cs:ce], False, False, [], [("bk", dpb)])
                                P.mm(dp, identb, negm[:, d, :], False, True, [], [("bk", dpb)])

                            dp_mm(0)
                            dp_mm(1)
                            for h in range(16):
                                g, e = h // 8, h % 2
                                dpb = 1 if e == 0 else 6
                                P.act(E[e][:], banks[dpb][:, 0:128], AF.Exp, [("bk", dpb)], [("E", e)], bias=atok[:, c, 1, rb + h:rb + h + 1])
                                P.tt("dve", M[e][:], E[e][:], banks[0][:, g * 128:(g + 1) * 128], ALU.mult, [("E", e), ("bk", 0)], [("M", e)])
                                if h + 2 < 16:
                                    dp_mm(h + 2)
                                P.mm(banks[2 + g][:, (h % 8) * 64:(h % 8 + 1) * 64], M[e][:], xdt[:, h * 64:(h + 1) * 64], True, True,
                                     [("M", e), "xdt"], [("bk", 2 + g)])
                            P.tt("dve", h3(xdtt[:], 16), h3(xdt[:], 16), bc(tl[:, 0, :], 16), ALU.mult, ["xdt", "tl0"], ["xdtt"])
                            ea = atok[:, c, 2, rb:rb + 16]
                            for g in range(2):
                                gs = slice(g * 512, (g + 1) * 512)
                                P.tt("dve", h3(tY[:, gs], 8), h3(banks[4 + g][:, :], 8), bc(ea[:, g * 8:(g + 1) * 8], 8), ALU.mult,
                                     [("bk", 4 + g)], [("tY", g)])
                                if d == 0:
                                    P.tt("dve", yfb[:, gs], tY[:, gs], banks[2 + g][:, :], ALU.add, [("tY", g), ("bk", 2 + g)], ["yfb"])
                                else:
                                    P.tt("dve", yv[:, gs], tY[:, gs], banks[2 + g][:, :], ALU.add, [("tY", g), ("bk", 2 + g)], [("yv", g)])
                            if d == 0:
                                P.dma("sp", env["yfD"][s, c], yfb[:], r=["yfb"])
                            for g in range(2):
                                gs = slice(g * 512, (g + 1) * 512)
                                P.mm(banks[4 + g][:, :], Btok[:, c, g * 128:(g + 1) * 128], xdtt[:, gs], True, True, ["xdtt"], [("bk", 4 + g)])
                                P.tt("dve", h3(S[:, gs], 8), h3(S[:, gs], 8), bc(tl[:, 1, g * 8:(g + 1) * 8], 8), ALU.mult, ["S%d" % g, "tl1"], ["S%d" % g])
                                P.tt("dve", S[:, gs], S[:, gs], banks[4 + g][:, :], ALU.add, ["S%d" % g, ("bk", 4 + g)], ["S%d" % g])
                                P.cp("pool", Sb[:, gs], S[:, gs], ["S%d" % g], ["Sb%d" % g])
                            if d == 1:
                                P.dma("sp", yfb[:], env["yfD"][s, c], w=["yfb"])
                                P.tt("dve", yv[:], yv[:], yfb[:], ALU.add, [("yv", 0), ("yv", 1), "yfb"], ["yvv"])
                                P.tt("dve", h3(tY[:], 16), h3(xtok[:, c, :], 16), bc(sdn[:, 0:16], 16), ALU.mult, [("tY", 0), ("tY", 1)], ["tYd"])
                                P.tt("dve", yv[:], yv[:], tY[:], ALU.add, ["yvv", "tYd"], ["yvv"])
                                for b in range(2):
                                    for kt in range(8):
                                        P.mm(banks[6 + b][:, :], hn[:, kt, cs:ce], wz[:, kt, b * 512:(b + 1) * 512], kt == 0, kt == 7, [], [("bk", 6 + b)])
                                    P.act(zs[:, b * 512:(b + 1) * 512], banks[6 + b][:, :], AF.Silu, [("bk", 6 + b)], [("zs", b)])
                                P.tt("dve", yv[:], yv[:], zs[:], ALU.mult, ["yvv", ("zs", 0), ("zs", 1)], ["yvv"])
                                for g in range(2):
                                    P.act(junk[:], yv[:, g * 512:(g + 1) * 512], AF.Square, ["yvv"], ["junk"], accum_out=ss[:, g:g + 1])
                                P.act(ss[:], ss[:], AF.Sqrt, ["junk"], ["ss"], bias=EPS, scale=1.0 / 512)
                                P.recip(ss[:], ss[:], ["ss"], ["ss"])
                                for g in range(2):
                                    gs = slice(g * 512, (g + 1) * 512)
                                    P.stt(yn[:, gs], yv[:, gs], ss[:, g:g + 1], sdn[:, 16 + g * 512:16 + (g + 1) * 512], ALU.mult, ALU.mult,
                                          ["yvv", "ss"], ["yn"])
                                bkb = banks[6][:].bitcast(BF16)
                                for q in range(8):
                                    P.tr(bkb[:, q * 128:(q + 1) * 128], yn[:, q * 128:(q + 1) * 128], identb, ["yn"], [("bk", 6)])
                                P.cp("act", yTs[:].rearrange("p a c -> p (a c)"), bkb[:, :], [("bk", 6)], ["yTs"])
                                P.dma("sp", env["yssdT"][s, :, :, cs:ce], yTs[:], r=["yTs"])
                    P.flush()


def phase_gdn(env, l):
    nc, P, banks = env["nc"], env["P"], env["banks"]
    small, K2 = env["small_in"], env["const2"]
    identf, identb, onesb = env["identf"], env["identb"], env["onesb"]
    with ExitStack() as lay:
        gcw = lay.enter_context(sbt(nc, "gcw", [128, 60], F32))
        gal = lay.enter_context(sbt(nc, "gal", [64, 2], F32))
        negg = lay.enter_context(sbt(nc, "negg", [64, 1], F32))
        gnw = lay.enter_context(sbt(nc, "gnw", [128, 128], F32))
        wzg = lay.enter_context(sbt(nc, "wzg", [128, 8, 512], BF16))
        selg = lay.enter_context(sbt(nc, "selg", [64, 8, 128], F32))
        selgb = lay.enter_context(sbt(nc, "selgb", [64, 8, 128], BF16))
        gmk = lay.enter_context(sbt(nc, "gmk", [128, 4, 128], BF16))
        glast = lay.enter_context(sbt(nc, "glast", [128, 2, 128], F32))
        gch = lay.enter_context(sbt(nc, "gch", [128, 4, 128], F32))
        with ExitStack() as p0:
            stg = p0.enter_context(sbt(nc, "gstg", [128, 512], F32))
            P.dma("sp", gcw[:], small[l, :, S_GCW:S_GCW + 60], w=["gcw"])
            P.dma("sp", gal[:], small[l, 0:64, S_GALOG:S_GALOG + 2], w=["gal"])
            P.dma("sp", gnw[:], small[l, :, S_GNW:S_GNW + 128], w=["gnw"])
            P.dma("sp", wzg[:], env["wallb"][l, :, W_ZGDN:W_ZGDN + 4096].rearrange("p (k c) -> p k c", k=8), w=["wzg"])
            P.dma("sp", selg[:].rearrange("p a c -> p (a c)"), K2[0:64, K_SELG:K_SELG + 1024], w=["selg"])
            P.dma("sp", stg[:], K2[:, K_GNEGF:K_GNEGF + 512], w=["gstg"])
            P.dma("sp", glast[:].rearrange("p a c -> p (a c)"), K2[:, K_GLASTF:K_GLASTF + 256], w=["glast"])
            P.dma("sp", gch[:].rearrange("p a c -> p (a c)"), K2[:, K_GCHF:K_GCHF + 512], w=["gch"])
            P.cp("dve", selgb[:], selg[:], ["selg"], ["selgb"])
            P.cp("dve", gmk[:].rearrange("p a c -> p (a c)"), stg[:], ["gstg"], ["gmk"])
            P.act(negg[:], gal[:, 0:1], AF.Exp, ["gal"], ["negg"])
            P.ts("dve", negg[:], negg[:], -1.0, None, ALU.mult, None, ["negg"], ["negg"])
            P.flush()

        for s in range(env["nseq"]):
            with ExitStack() as ph:
                hn = ph.enter_context(sbt(nc, "ghn", [128, 8, LP], BF16))
                qkv = ph.enter_context(sbt(nc, "qkv", [128, 12, LP], BF16))
                gtok = ph.enter_context(sbt(nc, "gtok", [128, NT, 4, 64], F32))
                ghl = ph.enter_context(sbt(nc, "ghl", [64, 2, LP], BF16))
                gcT = ph.enter_context(sbt(nc, "gcT", [64, LP], F32))
                betaT = ph.enter_context(sbt(nc, "betaT", [64, LP], F32))
                betab = ph.enter_context(sbt(nc, "betab", [64, LP], BF16))
                for kt in range(8):
                    P.dma("sp", hn[:, kt, :], env["hnT"][s, :, kt, :], w=[("hn", kt)])
                hnk = [("hn", kt) for kt in range(8)]
                with ExitStack() as p1:
                    wcv = p1.enter_context(sbt(nc, "gwcv", [128, 12, 8, 128], BF16))
                    raw0 = p1.enter_context(sbt(nc, "graw", [128, LP + 4], F32))
                    raw = [raw0, raw0]
                    acc = p1.enter_context(sbt(nc, "gacc", [128, LP], F32))
                    sq = p1.enter_context(sbt(nc, "gsq", [128, LP], BF16))
                    rs = p1.enter_context(sbt(nc, "grs", [128, 512], F32))
                    P.dma("sp", wcv[:], fm_tile(env, l, T_Q, 12), w=["wcv"])
                    for i in range(2):
                        P.ms("pool", raw[i][:, 0:2], 0.0, ["raw"])
                        P.ms("pool", raw[i][:, LP + 2:LP + 4], 0.0, ["raw"])
                    nb_ = 0
                    for ct in range(12):
                        i = ct % 2
                        for (c0, bw) in BLOCKS:
                            bk = nb_ % 4
                            nb_ += 1
                            for kt in range(8):
                                P.mm(banks[bk][:, 0:bw], wcv[:, ct, kt, :], hn[:, kt, c0:c0 + bw], kt == 0, kt == 7, ["wcv"] + hnk, [("bk", bk)])
                            P.cp("act", raw[i][:, 2 + c0:2 + c0 + bw], banks[bk][:, 0:bw], [("bk", bk)], ["raw"])
                        P.ts("dve", acc[:], raw[i][:, 0:LP], gcw[:, ct * 5:ct * 5 + 1], None, ALU.mult, None, ["raw", "gcw"], ["acc"])
                        for k in range(1, 5):
                            P.stt(acc[:], raw[i][:, k:k + LP], gcw[:, ct * 5 + k:ct * 5 + k + 1], acc[:], ALU.mult, ALU.add,
                                  ["raw", "acc", "gcw"], ["acc"])
                        dst, dk = qkv[:, ct, :], ("qkv", ct)
                        P.act(dst, acc[:], AF.Silu, ["acc"], [dk])
                        P.ms("pool", dst[:, 0:PAD], 0.0, [dk])
                        if ct < 8:
                            P.tt("pool", sq[:], dst, dst, ALU.mult, [dk], ["sq"])
                            for bi, (c0, bw) in enumerate(BLOCKS):
                                bk = 4 + bi % 4
                                P.mm(banks[bk][:, 0:bw], onesb, sq[:, c0:c0 + bw], True, True, ["sq", "cstb"], [("bk", bk)])
                                P.act(rs[:, 0:bw], banks[bk][:, 0:bw], AF.Sqrt, [("bk", bk)], ["rs"], bias=EPS)
                                P.recip(rs[:, 0:bw], rs[:, 0:bw], ["rs"], ["rs"])
                                P.stt(dst[:, c0:c0 + bw], dst[:, c0:c0 + bw], float(128 ** -0.5) if ct < 4 else 1.0, rs[:, 0:bw], ALU.mult, ALU.mult,
                                      [dk, "rs"], [dk])
                    P.flush()
                if DBG.get("gdn_stop") == 1:
                    continue
                with ExitStack() as p2:
                    wba = p2.enter_context(sbt(nc, "wba", [128, 2, 8, 128], BF16))
                    gT = p2.enter_context(sbt(nc, "gT", [64, LP], F32))
                    rmf = p2.enter_context(sbt(nc, "grmf", [64, LP], F32))
                    rmb = p2.enter_context(sbt(nc, "grmb", [64, LP], F32))
                    P.dma("sp", wba[:], fm_tile(env, l, T_BETA, 2), w=["wba"])
                    for bi, (c0, bw) in enumerate(BLOCKS):
                        b0, b1 = bi % 2, 2 + bi % 2
                        for kt in range(8):
                            P.mm(banks[b0][0:64, 0:bw], wba[:, 0, kt, 0:64], hn[:, kt, c0:c0 + bw], kt == 0, kt == 7, ["wba"], [("bk", b0)])
                        P.act(betaT[:, c0:c0 + bw], banks[b0][0:64, 0:bw], AF.Sigmoid, [("bk", b0)], ["betaT"])
                        for kt in range(8):
                            P.mm(banks[b1][0:64, 0:bw], wba[:, 1, kt, 0:64], hn[:, kt, c0:c0 + bw], kt == 0, kt == 7, ["wba"], [("bk", b1)])
                        P.act(gT[:, c0:c0 + bw], banks[b1][0:64, 0:bw], AF.Exp, [("bk", b1)], ["gT"], bias=gal[:, 1:2])
                        P.act(gT[:, c0:c0 + bw], gT[:, c0:c0 + bw], AF.Ln, ["gT"], ["gT"], bias=1.0)
                    P.ms("dve", betaT[:, 0:PAD], 0.0, ["betaT"])
                    P.cp("dve", betab[:], betaT[:], ["betaT"], ["betab"])
                    P.ms("dve", gT[:, 0:PAD], 0.0, ["gT"])
                    P.ts("dve", gT[:], gT[:], negg[:, 0:1], None, ALU.mult, None, ["gT"], ["gT"])
                    P.ms("pool", rmf[:], 1.0, ["rmf"])
                    P.ms("pool", rmf[:, 0:LP:64], 0.0, ["rmf"])
                    P.ms("pool", rmb[:], 1.0, ["rmb"])
                    P.ms("pool", rmb[:, 63:LP:64], 0.0, ["rmb"])
                    P.scan(gcT[0:32, :], rmf[0:32, :], gT[0:32, :], 0.0, ALU.mult, ALU.add, ["rmf", "gT"], ["gcT0"])
                    P.scan(gcT[32:64, ::-1], rmb[32:64, ::-1], gT[32:64, ::-1], 0.0, ALU.mult, ALU.add, ["rmb", "gT"], ["gcT1"])
                    P.cp("dve", ghl[:, 0, :], gcT[:], ["gcT0", "gcT1"], ["ghl0"])
                    P.tt("dve", ghl[:, 1, :], gcT[:], ghl[:, 0, :], ALU.subtract, ["gcT0", "gcT1", "ghl0"], ["ghl1"])
                    for tt_ in range(NT):
                        bk = 4 + tt_ % 4
                        P.tr(banks[bk][:, 0:64], gcT[:, tt_ * 128:(tt_ + 1) * 128], identf[0:64, 0:64], ["gcT0", "gcT1"], [("bk", bk)])
                        P.tr(banks[bk][:, 64:128], betaT[:, tt_ * 128:(tt_ + 1) * 128], identf[0:64, 0:64], ["betaT"], [("bk", bk)])
                        P.cp("act", gtok[:, tt_, 0:2, :], banks[bk][:, 0:128].rearrange("p (a c) -> p a c", a=2), [("bk", bk)], ["gtok01"])
                    P.act(gtok[:, :, 2, :], gtok[:, :, 0, :], AF.Exp, ["gtok01"], ["gtok2"])
                    P.tt("dve", gtok[:, :, 2, :], gtok[:, :, 2, :], gtok[:, :, 1, :], ALU.mult, ["gtok2", "gtok01"], ["gtok2"])
                    P.ts("dve", gtok[:, :, 3, :], gtok[:, :, 0, :], -1.0, None, ALU.mult, None, ["gtok01"], ["gtok3"])
                    P.flush()
                if DBG.get("gdn_stop") == 2:
                    continue
                with ExitStack() as p3:
                    obuf = p3.enter_context(sbt(nc, "obuf", [128, NT, 512], BF16))
                    S = p3.enter_context(sbt(nc, "gS", [128, 4, 128], F32))
                    Sb = p3.enter_context(sbt(nc, "gSb", [128, 4, 128], BF16))
                    ekt = p3.enter_context(sbt(nc, "ekt", [128, 4], F32))
                    dec = p3.enter_context(sbt(nc, "dec", [128, 8], F32))
                    osum = p3.enter_context(sbt(nc, "osum", [128, 4, 128], F32))
                    zs = p3.enter_context(sbt(nc, "gzs", [128, 512], F32))
                    junk = p3.enter_context(sbt(nc, "gjunk", [128, 128], F32))
                    ss = p3.enter_context(sbt(nc, "gss", [128, 4], F32))
                    yn = p3.enter_context(sbt(nc, "gyn", [128, 512], BF16))
                    yTs = p3.enter_context(sbt(nc, "gyTs", [128, 4, 128], BF16))
                    T_ = []
                    if DBG.get("gdn_padalloc"):
                        padt = p3.enter_context(sbt(nc, "gpad", [128, DBG["gdn_padalloc"]], F32))
                    for h in range(4):
                        t = {}
                        for nm, dt_ in (("egb", F32), ("kbT", BF16), ("qgT", BF16), ("vb", BF16), ("kbeg", BF16), ("ktok", BF16),
                                        ("Ei", BF16), ("attnT", BF16), ("Es", BF16), ("N0", BF16), ("N1", BF16), ("NT0", BF16), ("NT1", BF16),
                                        ("R32", F32), ("Rb", BF16), ("u32", F32), ("wT", BF16), ("vnew0", BF16), ("vnew1", BF16), ("av", F32)):
                            t[nm] = p3.enter_context(sbt(nc, "g%s%d" % (nm, h), [128, 128], dt_))
                            if DBG.get("addr"):
                                print("ALLOC", nm, h, t[nm], nc.sbuf_bytes_remaining)
                        P.ms("pool", t["vnew0"][:], 0.0, [("vnew0", h)])
                        P.ms("pool", t["vnew1"][:], 0.0, [("vnew1", h)])
                        T_.append(t)

                    def R(h, i):
                        return banks[2 * h + i // 4][:, (i % 4) * 128:(i % 4 + 1) * 128]

                    def Rb16(h, i):
                        return banks[2 * h + i // 4][:].bitcast(BF16)[:, (i % 4) * 256:(i % 4 + 1) * 256]

                    def chain(d, c, h):
                        t = T_[h]
                        rb = d * 32
                        hrow = rb + h
                        cs, ce = c * 128, (c + 1) * 128
                        rk = lambda i: ("bk", 2 * h + i // 4)
                        tk = lambda nm: (nm, h)
                        qT, kT, vT = qkv[:, h, cs:ce], qkv[:, 4 + h, cs:ce], qkv[:, 8 + h, cs:ce]
                        P.mm(R(h, 0), selgb[:, d * 4 + h, :], betab[:, cs:ce], True, True, [], [rk(0)])
                        P.mm(R(h, 1), selgb[:, d * 4 + h, :], ghl[:, 0, cs:ce], True, False, [], [rk(1)])
                        P.mm(R(h, 1), selgb[:, d * 4 + h, :], ghl[:, 1, cs:ce], False, True, [], [rk(1)])
                        r2b = Rb16(h, 2)
                        P.tr(r2b[:, 0:128], kT, identb, [], [rk(2)])
                        P.tr(r2b[:, 128:256], vT, identb, [], [rk(2)])
                        yield
                        nsub = DBG.get("gdn_sub", 6)
                        if nsub > 0:
                            P.act(t["egb"][:], R(h, 1), AF.Exp, [rk(1)], [tk("egb")])
                        if nsub > 1:
                            P.cp("act", t["av"][:], R(h, 0), [rk(0)], [tk("av")])
                            P.tt("dve", t["kbT"][:], kT, t["av"][:], ALU.mult, [tk("av")], [tk("kbT")])
                        if nsub > 2:
                            P.tt("dve", t["qgT"][:], qT, t["egb"][:], ALU.mult, [tk("egb")], [tk("qgT")])
                        if nsub > 3:
                            P.ts("dve", t["vb"][:], r2b[:, 128:256], gtok[:, c, 1, hrow:hrow + 1], None, ALU.mult, None, [rk(2)], [tk("vb")])
                        if nsub > 4:
                            P.ts("dve", t["kbeg"][:], r2b[:, 0:128], gtok[:, c, 2, hrow:hrow + 1], None, ALU.mult, None, [rk(2)], [tk("kbeg")])
                        if nsub > 5:
                            P.ts("dve", t["ktok"][:], r2b[:, 0:128], ekt[:, h:h + 1], None, ALU.mult, None, [rk(2), "ekt"], [tk("ktok")])
                        yield
                        P.mm(R(h, 3), kT, t["kbT"][:], True, True, [tk("kbT")], [rk(3)])
                        P.mm(R(h, 4), kT, qT, True, True, [], [rk(4)])
                        P.mm(R(h, 5), selgb[:, d * 4 + h, :], ghl[:, 0, cs:ce], True, False, [], [rk(5)])
                        P.mm(R(h, 5), selgb[:, d * 4 + h, :], ghl[:, 1, cs:ce], False, False, [], [rk(5)])
                        P.mm(R(h, 5), identb, gmk[:, d, :], False, True, [], [rk(5)])
                        yield
                        P.act(t["Ei"][:], R(h, 5), AF.Exp, [rk(5)], [tk("Ei")], bias=gtok[:, c, 3, hrow:hrow + 1])
                        P.tt("dve", t["attnT"][:], R(h, 4), t["Ei"][:], ALU.mult, [rk(4), tk("Ei")], [tk("attnT")])
                        P.tt("pool", t["Es"][:], t["Ei"][:], gmk[:, 2 + d, :], ALU.mult, [tk("Ei")], [tk("Es")])
                        P.stt(t["N0"][:], R(h, 3), -1.0, t["Es"][:], ALU.mult, ALU.mult, [rk(3), tk("Es")], [tk("N0")])
                        yield
                        r0b = Rb16(h, 0)
                        P.tr(r0b[:, 0:128], t["N0"][:], identb, [tk("N0")], [rk(0)])
                        P.tt("dve", t["R32"][:], t["N0"][:], identf, ALU.add, [tk("N0")], [tk("R32")])
                        P.cp("pool", t["Rb"][:], t["R32"][:], [tk("R32")], [tk("Rb")])
                        yield
                        P.cp("act", t["NT0"][:], r0b[:, 0:128], [rk(0)], [tk("NT0")])
                        yield
                        P.mm(R(h, 1), t["NT0"][:], t["N0"][:], True, True, [tk("N0"), tk("NT0")], [rk(1)])
                        P.mm(R(h, 2), t["N0"][:], t["NT0"][:], True, True, [tk("N0"), tk("NT0")], [rk(2)])
                        yield
                        P.cp("act", t["N1"][:], R(h, 1), [rk(1)], [tk("N1")])
                        P.cp("dve", t["NT1"][:], R(h, 2), [rk(2)], [tk("NT1")])
                        yield
                        for lvl in range(1, 6):
                            a, b = lvl % 2, (lvl + 1) % 2
                            Na, NTa, Nb, NTb = t["N%d" % a], t["NT%d" % a], t["N%d" % b], t["NT%d" % b]
                            P.mm(R(h, 6), NTa[:], t["Rb"][:], True, True, [tk("NT%d" % a), tk("Rb")], [rk(6)])
                            if lvl < 5:
                                if lvl < 4:
                                    P.mm(R(h, 1), NTa[:], Na[:], True, True, [tk("N%d" % a), tk("NT%d" % a)], [rk(1)])
                                P.mm(R(h, 2), Na[:], NTa[:], True, True, [tk("N%d" % a), tk("NT%d" % a)], [rk(2)])
                            yield
                            P.tt("dve", t["R32"][:], t["R32"][:], R(h, 6), ALU.add, [rk(6), tk("R32")], [tk("R32")])
                            P.cp("act", t["Rb"][:], t["R32"][:], [tk("R32")], [tk("Rb")])
                            if lvl < 5:
                                if lvl < 4:
                                    P.cp("act", Nb[:], R(h, 1), [rk(1)], [tk("N%d" % b)])
                                P.cp("dve", NTb[:], R(h, 2), [rk(2)], [tk("NT%d" % b)])
                            yield
                        P.mm(R(h, 7), t["Rb"][:], t["vb"][:], True, True, [tk("Rb"), tk("vb")], [rk(7)])
                        P.mm(R(h, 0), t["kbeg"][:], t["Rb"][:], True, True, [tk("Rb"), tk("kbeg")], [rk(0)])
                        yield
                        P.cp("act", t["u32"][:], R(h, 7), [rk(7)], [tk("u32")])
                        P.cp("dve", t["wT"][:], R(h, 0), [rk(0)], [tk("wT")])
                        yield
                        for sc in ((0, 1) if d == 0 else (1, 0)):
                            rows = slice(64 * sc, 64 * sc + 64)
                            P.mm(R(h, 1), t["wT"][:], Sb[:, h, :], True, True, [tk("wT"), ("Sb", h)], [rk(1)])
                            P.mm(R(h, 2 + sc), t["qgT"][:], Sb[:, h, :], True, True, [tk("qgT"), ("Sb", h)], [rk(2 + sc)])
                            yield
                            vn = "vnew%d" % sc
                            P.tt("dve", t[vn][rows, :], t["u32"][rows, :], R(h, 1)[rows, :], ALU.subtract, [tk("u32"), rk(1)], [tk(vn)])
                            yield
                            P.mm(R(h, 4), t["ktok"][:], t[vn][:], True, True, [tk("ktok"), tk(vn)], [rk(4)])
                            yield
                            P.stt(S[:, h, :], S[:, h, :], dec[:, sc * 4 + h:sc * 4 + h + 1], R(h, 4), ALU.mult, ALU.add, [rk(4), ("S", h), "dec"], [("S", h)])
                            P.cp("act", Sb[:, h, :], S[:, h, :], [("S", h)], [("Sb", h)])
                            yield
                        P.mm(R(h, 5), t["attnT"][:], t["vnew0"][:], True, False, [tk("attnT"), tk("vnew0")], [rk(5)])
                        P.mm(R(h, 5), t["attnT"][:], t["vnew1"][:], False, True, [tk("attnT"), tk("vnew1")], [rk(5)])
                        yield
                        P.cp("act", t["av"][:], R(h, 5), [rk(5)], [tk("av")])
                        for sc in range(2):
                            rows = slice(64 * sc, 64 * sc + 64)
                            if d == 0:
                                P.tt("dve", obuf[rows, c, h * 128:(h + 1) * 128], R(h, 2 + sc)[rows, :], t["av"][rows, :], ALU.add,
                                     [rk(2 + sc), tk("av")], [("obuf", h)])
                            else:
                                P.tt("dve", osum[rows, h, :], R(h, 2 + sc)[rows, :], t["av"][rows, :], ALU.add, [rk(2 + sc), tk("av")], [("osum", h)])
                        if d == 1:
                            P.tt("dve", osum[:, h, :], osum[:, h, :], obuf[:, c, h * 128:(h + 1) * 128], ALU.add, [("osum", h)], [("osum", h)])
                        yield

                    for d in range(2):
                        rb = d * 32
                        P.ms("pool", S[:], 0.0, [("S", h) for h in range(4)])
                        P.ms("pool", Sb[:], 0.0, [("Sb", h) for h in range(4)])
                        order = list(range(NT)) if d == 0 else list(range(NT - 1, -1, -1))
                        for c in order[:DBG.get("gdn_tiles", NT)]:
                            cs, ce = c * 128, (c + 1) * 128
                            r06 = R(0, 6)
                            P.mm(r06[:, 0:16], glast[:, d, :], gtok[:, c, 0, rb:rb + 16], True, True, [], [("bk", 1)])
                            for sc in range(2):
                                P.mm(r06[:, 16 + 16 * sc:32 + 16 * sc], gch[:, d * 2 + sc, :], gtok[:, c, 0, rb:rb + 16], True, True, [], [("bk", 1)])
                            P.tt("dve", ekt[:], r06[:, 0:4], gtok[:, c, 0, rb:rb + 4], ALU.subtract, [("bk", 1)], ["ekt"])
                            P.act(ekt[:], ekt[:], AF.Exp, ["ekt"], ["ekt"])
                            P.act(dec[:].rearrange("p (a c) -> p a c", a=2), r06[:, 16:48].rearrange("p (a c) -> p a c", a=2)[:, :, 0:4], AF.Exp, [("bk", 1)], ["dec"])
                            gens = [chain(d, c, h) for h in range(DBG.get("gdn_heads", 4))]
                            nst = 0
                            while gens and nst < DBG.get("gdn_stage", 10 ** 9):
                                nst += 1
                                for g in list(gens):
                                    try:
                                        next(g)
                                    except StopIteration:
                                        gens.remove(g)
                            if d == 1 and not DBG.get("gdn_noepi"):
                                r0k = [("bk", 0)]
                                for kt in range(8):
                                    P.mm(banks[0][:, :], hn[:, kt, cs:ce], wzg[:, kt, :], kt == 0, kt == 7, [], r0k)
                                P.act(zs[:], banks[0][:, :], AF.Silu, r0k, ["zs"])
                                for h in range(4):
                                    P.act(junk[:], osum[:, h, :], AF.Square, [("osum", h)], ["junk"], accum_out=ss[:, h:h + 1])
                                P.act(ss[:], ss[:], AF.Sqrt, ["junk"], ["ss"], bias=EPS, scale=1.0 / 128)
                                P.recip(ss[:], ss[:], ["ss"], ["ss"])
                                for h in range(4):
                                    P.stt(osum[:, h, :], osum[:, h, :], ss[:, h:h + 1], gnw[:], ALU.mult, ALU.mult, [("osum", h), "ss"], [("osum", h)])
                                P.tt("dve", yn[:], osum[:].rearrange("p a c -> p (a c)"), zs[:], ALU.mult, [("osum", h) for h in range(4)] + ["zs"], ["yn"])
                                bkb = banks[2][:].bitcast(BF16)
                                r1k = [("bk", 2)]
                                for q in range(4):
                                    P.tr(bkb[:, q * 128:(q + 1) * 128], yn[:, q * 128:(q + 1) * 128], identb, ["yn"], r1k)
                                P.cp("act", yTs[:].rearrange("p a c -> p (a c)"), bkb[:, 0:512], r1k, ["yTs"])
                                P.dma("sp", env["ygdnT"][s, :, :, cs:ce], yTs[:], r=["yTs"])
                    P.flush()


def phase_merge(env, l):
    nc, P, banks = env["nc"], env["P"], env["banks"]
    wallb = env["wallb"]
    with ExitStack() as lay:
        wg = lay.enter_context(sbt(nc, "wg", [128, 24, 8, 128], BF16))
        wo5 = lay.enter_context(sbt(nc, "wo5", [128, 4, 1024], BF16))
        wos = lay.enter_context(sbt(nc, "wos", [128, 8, 1024], BF16))
        wog = lay.enter_context(sbt(nc, "wog", [128, 4, 1024], BF16))
        wout = lay.enter_context(sbt(nc, "wout", [128, 8, 1024], BF16))
        for t0 in range(0, 24, 8):
            P.dma("sp", wg[:, t0:t0 + 8], fm_tile(env, l, T_GATE + t0, 8), w=["wg"])
        P.dma("sp", wo5[:], wallb[l, :, W_S5O:W_S5O + 4096].rearrange("p (k c) -> p k c", k=4), w=["wo5"])
        P.dma("sp", wos[:], wallb[l, :, W_SSDO:W_SSDO + 8192].rearrange("p (k c) -> p k c", k=8), w=["wos"])
        P.dma("sp", wog[:], wallb[l, :, W_GDNO:W_GDNO + 4096].rearrange("p (k c) -> p k c", k=4), w=["wog"])
        P.dma("sp", wout[:], wallb[l, :, W_OUT:W_OUT + 8192].rearrange("p (k c) -> p k c", k=8), w=["wout"])
        hnb = [lay.enter_context(sbt(nc, "mhn%d" % i, [128, 8, 512], BF16)) for i in range(2)]
        yb = [lay.enter_context(sbt(nc, "myb%d" % i, [128, 16, 512], BF16)) for i in range(2)]
        hb = [lay.enter_context(sbt(nc, "mh%d" % i, [128, 8, 512], F32)) for i in range(2)]
        mg = lay.enter_context(sbt(nc, "mg", [128, 8, 512], BF16))
        gt = [lay.enter_context(sbt(nc, "gt%d" % i, [128, 512], F32)) for i in range(2)]
        tmp = [lay.enter_context(sbt(nc, "mtmp%d" % i, [128, 512], F32)) for i in range(2)]
        msum = lay.enter_context(sbt(nc, "msum", [128, 512], F32))
        it = 0
        nb_ = 0
        for s in range(env["nseq"]):
            for (c0, bw) in BLOCKS:
                i = it % 2
                it += 1
                P.dma("sp", hnb[i][:, :, 0:bw], env["hnT"][s, :, :, c0:c0 + bw], w=[("hn", i)])
                P.dma("sp", yb[i][:, 0:4, 0:bw], env["ys5T"][s, :, :, c0:c0 + bw], w=[("y5", i)])
                P.dma("sp", yb[i][:, 4:12, 0:bw], env["yssdT"][s, :, :, c0:c0 + bw], w=[("ys", i)])
                P.dma("sp", yb[i][:, 12:16, 0:bw], env["ygdnT"][s, :, :, c0:c0 + bw], w=[("yg", i)])
                P.dma("sp", hb[i][:, :, 0:bw], env["hT"][s, :, :, c0:c0 + bw], w=[("h", i)])
                srcs = [(0, 4, wo5, ("y5", i), "wo5"), (4, 8, wos, ("ys", i), "wos"), (12, 4, wog, ("yg", i), "wog")]
                for dtl in range(8):
                    for b, (y0, nk, wsrc, yk, wk) in enumerate(srcs):
                        ba, bb = nb_ % 4, 4 + nb_ % 4
                        nb_ += 1
                        for kt in range(8):
                            P.mm(banks[ba][:, 0:bw], wg[:, b * 8 + dtl, kt, :], hnb[i][:, kt, 0:bw], kt == 0, kt == 7, ["wg", ("hn", i)], [("bk", ba)])
                        P.act(gt[b % 2][:, 0:bw], banks[ba][:, 0:bw], AF.Sigmoid, [("bk", ba)], [("gt", b % 2)])
                        for kt in range(nk):
                            P.mm(banks[bb][:, 0:bw], wsrc[:, kt, dtl * 128:(dtl + 1) * 128], yb[i][:, y0 + kt, 0:bw], kt == 0, kt == nk - 1,
                                 [wk, yk], [("bk", bb)])
                        if b == 0:
                            P.tt("dve", msum[:, 0:bw], gt[0][:, 0:bw], banks[bb][:, 0:bw], ALU.mult, [("gt", 0), ("bk", bb)], ["msum"])
                        else:
                            P.tt("dve", tmp[b % 2][:, 0:bw], gt[b % 2][:, 0:bw], banks[bb][:, 0:bw], ALU.mult, [("gt", b % 2), ("bk", bb)], [("tmp", b % 2)])
                            if b == 1:
                                P.tt("dve", msum[:, 0:bw], msum[:, 0:bw], tmp[1][:, 0:bw], ALU.add, ["msum", ("tmp", 1)], ["msum"])
                            else:
                                P.tt("dve", mg[:, dtl, 0:bw], msum[:, 0:bw], tmp[0][:, 0:bw], ALU.add, ["msum", ("tmp", 0)], [("mg", dtl)])
                mgk = [("mg", k) for k in range(8)]
                for dtl in range(8):
                    bk = nb_ % 4
                    nb_ += 1
                    for kt in range(8):
                        P.mm(banks[bk][:, 0:bw], wout[:, kt, dtl * 128:(dtl + 1) * 128], mg[:, kt, 0:bw], kt == 0, kt == 7, ["wout"] + mgk, [("bk", bk)])
                    P.tt("dve", hb[i][:, dtl, 0:bw], hb[i][:, dtl, 0:bw], banks[bk][:, 0:bw], ALU.add, [("h", i), ("bk", bk)], [("h", i)])
                off = PAD if c0 == 0 else 0
                P.dma("sp", env["hT"][s, :, :, c0 + off:c0 + bw], hb[i][:, :, off:bw], r=[("h", i)])
        P.flush()


def build(n_layers=DEPTH, nseq=2, dbg=False):
    nc = bass.Bass("TRN2", target_bir_lowering=False)
    x_in = nc.dram_tensor("x", [nseq, SEQ, D], F32, kind="ExternalInput").ap()
    meta_in = nc.dram_tensor("meta", [NMETA, D], F32, kind="ExternalInput").ap()
    wall_in = nc.dram_tensor("wall", [max(n_layers, 1), 128, NW], F32, kind="ExternalInput").ap()
    small_in = nc.dram_tensor("small", [max(n_layers, 1), 128, NS], F32, kind="ExternalInput").ap()
    fnw_in = nc.dram_tensor("fnw", [128, 8], F32, kind="ExternalInput").ap()
    const_in = nc.dram_tensor("consts", [128, NC_CONST], F32, kind="ExternalInput").ap()
    const2_in = nc.dram_tensor("consts2", [128, NK], F32, kind="ExternalInput").ap()
    y_out = nc.dram_tensor("y", [nseq, SEQ, D], F32, kind="ExternalOutput").ap()
    hT = nc.dram_tensor("hT", [nseq, 128, 8, LP], F32, kind="Internal").ap()
    hnT = nc.dram_tensor("hnT", [nseq, 128, 8, LP], BF16, kind="Internal").ap()
    wallb = nc.dram_tensor("wallb", [max(n_layers, 1), 128, NW], BF16, kind="Internal").ap()

    top = ExitStack()
    P = Prog(nc, top)
    banks = [top.enter_context(nc.psum_tensor("bank%d" % b, [128, 512], F32)) for b in range(8)]
    cst = top.enter_context(sbt(nc, "cst", [128, NC_CONST], F32))
    cstb = top.enter_context(sbt(nc, "cstb", [128, NC_CONST], BF16))
    fnw = top.enter_context(sbt(nc, "fnw_s", [128, 8], F32))
    identf = cst[:, C_IDENT:C_IDENT + 128]
    identb = cstb[:, C_IDENT:C_IDENT + 128]
    onesb = cstb[:, C_ONES:C_ONES + 128]

    with ExitStack() as ph:
        P.dma("sp", cst[:], const_in, w=["cst"])
        P.dma("sp", fnw[:], fnw_in, w=["fnw"])
        P.cp("dve", cstb[:], cst[:], ["cst"], ["cstb"])
        CH = 2048
        stg = [ph.enter_context(sbt(nc, "wstg%d" % i, [128, CH], F32)) for i in range(3)]
        stgb = [ph.enter_context(sbt(nc, "wstgb%d" % i, [128, CH], BF16)) for i in range(3)]
        engs = ["dve", "pool", "act"]
        ci = 0
        for l in range(min(n_layers, 1)):
            for c0 in range(0, NW, CH):
                cw = min(CH, NW - c0)
                i = ci % 3
                P.dma("sp", stg[i][:, 0:cw], wall_in[l, :, c0:c0 + cw], w=[("stg", i)])
                P.cp(engs[i], stgb[i][:, 0:cw], stg[i][:, 0:cw], [("stg", i)], [("stgb", i)])
                P.dma("sp", wallb[l, :, c0:c0 + cw], stgb[i][:, 0:cw], r=[("stgb", i)])
                ci += 1
        xt = [ph.enter_context(sbt(nc, "xt%d" % i, [128, D], F32)) for i in range(2)]
        hs = [ph.enter_context(sbt(nc, "hs%d" % i, [128, 8, 128], F32)) for i in range(2)]
        it = 0
        for s in range(nseq):
            for tt in range(NT):
                i = it % 2
                it += 1
                if tt == 0:
                    P.ms("pool", hs[i][:], 0.0, [("hs", i)])
                    P.dma("sp", xt[i][0:NMETA, :], meta_in, w=[("xt", i)])
                    np_ = NMETA
                else:
                    P.dma("sp", xt[i][:], x_in[s, (tt - 1) * 128:tt * 128, :], w=[("xt", i)])
                    np_ = 128
                for kt in range(8):
                    bk = banks[kt % 2]
                    P.tr(bk[:, 0:np_], xt[i][0:np_, kt * 128:(kt + 1) * 128], identf[0:np_, 0:np_], [("xt", i), "cst"], [("bk", kt % 2)])
                    P.cp("act" if kt % 2 else "dve", hs[i][:, kt, 128 - np_:128], bk[:, 0:np_], [("bk", kt % 2)], [("hs", i)])
                P.dma("sp", hT[s, :, :, tt * 128:(tt + 1) * 128], hs[i][:], r=[("hs", i)])
        P.flush()

    ys5T = nc.dram_tensor("ys5T", [nseq, 128, 4, LP], BF16, kind="ExternalOutput" if dbg else "Internal").ap()
    yssdT = nc.dram_tensor("yssdT", [nseq, 128, 8, LP], BF16, kind="ExternalOutput" if dbg else "Internal").ap()
    ygdnT = nc.dram_tensor("ygdnT", [nseq, 128, 4, LP], BF16, kind="ExternalOutput" if dbg else "Internal").ap()
    yfD = nc.dram_tensor("yfD", [nseq, NT, 128, 1024], BF16, kind="Internal").ap()
    env = dict(cast_next=True, n_layers=n_layers, wall_in=wall_in, yfD=yfD, nc=nc, P=P, banks=banks, hT=hT, hnT=hnT, wallb=wallb, small_in=small_in, identf=identf, identb=identb,
               onesb=onesb, const2=const2_in, ys5T=ys5T, yssdT=yssdT, ygdnT=ygdnT, nseq=nseq, cst=cst, cstb=cstb)
    for l in range(n_layers):
        phase_norm(env, l)
        if dbg in (False, "s5"):
            phase_s5(env, l)
        if dbg in (False, "ssd"):
            phase_ssd(env, l)
        if dbg in (False, "gdn"):
            phase_gdn(env, l)
        if dbg:
            break
        phase_merge(env, l)

    def rms_block(ph_tiles, s, c0, bw, hblk, key):
        sq, rstd = ph_tiles
        P.dma("sp", hblk[:, :, 0:bw], hT[s, :, :, c0:c0 + bw], w=[key + "h"])
        P.act(sq[:, :, 0:bw], hblk[:, :, 0:bw], AF.Square, [key + "h"], [key + "sq"])
        for kt in range(8):
            P.mm(banks[7][:, 0:bw], onesb, sq[:, kt, 0:bw], kt == 0, kt == 7, ["cstb", key + "sq"], ["b7"])
        P.act(rstd[:, 0:bw], banks[7][:, 0:bw], AF.Sqrt, ["b7"], [key + "rs"], bias=EPS, scale=1.0 / D)
        P.recip(rstd[:, 0:bw], rstd[:, 0:bw], [key + "rs"], [key + "rs"])

    with ExitStack() as ph:
        hb = [ph.enter_context(sbt(nc, "fh%d" % i, [128, 8, 512], F32)) for i in range(2)]
        sqb = [ph.enter_context(sbt(nc, "fsq%d" % i, [128, 8, 512], BF16)) for i in range(2)]
        rsb = [ph.enter_context(sbt(nc, "frs%d" % i, [128, 512], F32)) for i in range(2)]
        hnf = [ph.enter_context(sbt(nc, "fhn%d" % i, [128, 8, 512], F32)) for i in range(2)]
        ot = [ph.enter_context(sbt(nc, "fot%d" % i, [128, D], F32)) for i in range(2)]
        it = 0
        oi = 0
        for s in range(nseq):
            for (c0, bw) in BLOCKS:
                i = it % 2
                it += 1
                key = "f%d" % i
                rms_block((sqb[i], rsb[i]), s, c0, bw, hb[i], key)
                for kt in range(8):
                    P.stt(hnf[i][:, kt, 0:bw], hb[i][:, kt, 0:bw], fnw[:, kt:kt + 1], rsb[i][:, 0:bw], ALU.mult, ALU.mult,
                          [key + "h", key + "rs", "fnw"], [key + "hn"])
                for t0 in range(0, bw, 128):
                    tt = (c0 + t0) // 128
                    if tt == 0:
                        continue
                    j = oi % 2
                    oi += 1
                    for kt in range(8):
                        bk = banks[kt % 4]
                        P.tr(bk[:, 0:128], hnf[i][:, kt, t0:t0 + 128], identf, [key + "hn", "cst"], [("bk", kt % 4)])
                        P.cp("act" if kt % 2 else "dve", ot[j][:, kt * 128:(kt + 1) * 128], bk[:, 0:128], [("bk", kt % 4)], [("ot", j)])
                    P.dma("sp", y_out[s, (tt - 1) * 128:tt * 128, :], ot[j][:], r=[("ot", j)])
        P.flush()
    top.close()
    return nc


_CACHE = {}


def kernel(**inputs):
    n_layers = inputs.pop("_n_layers", DEPTH)
    dbg = inputs.pop("_dbg", False)
    x = np.asarray(inputs["x"], np.float32)
    nseq = x.shape[0] // N_CORES
    walls, smalls = [], []
    for i in range(max(n_layers, 1)):
        w, s = _arrange_layer(i, inputs)
        walls.append(w)
        smalls.append(s)
    wall = np.stack(walls)
    small = np.stack(smalls)
    fnw = np.ascontiguousarray(np.asarray(inputs["final_norm_w"], np.float32).reshape(8, 128).T)
    meta = np.asarray(inputs["meta_tokens"], np.float32)
    consts = _consts()
    consts2 = _consts2()
    key = (n_layers, nseq, dbg)
    if key not in _CACHE:
        _CACHE[key] = build(n_layers, nseq, dbg)
    nc = _CACHE[key]
    in_maps = []
    for c in range(N_CORES):
        in_maps.append({"x": np.ascontiguousarray(x[c * nseq:(c + 1) * nseq]), "meta": meta, "wall": wall,
                        "small": small, "fnw": fnw, "consts": consts, "consts2": consts2})
    res = run_bass_kernel_spmd(nc, in_maps, core_ids=list(range(N_CORES)))
    if dbg:
        return res.results
    return np.concatenate([r["y"] for r in res.results], axis=0)
```

```python
import numpy as np
import concourse.bass as bass
import concourse.mybir as mybir
from concourse.bass_utils import run_bass_kernel_spmd
from contextlib import ExitStack

F32 = mybir.dt.float32
BF16 = mybir.dt.bfloat16
ALU = mybir.AluOpType
AF = mybir.ActivationFunctionType
AX = mybir.AxisListType

N_CORES = 8
DBG = {}
D = 1024
DEPTH = 4
SEQ = 2048
NMETA = 16
LP = 2176
PAD = 112
NT = 17
EPS = 1e-6
BLOCKS = [(0, 512), (512, 512), (1024, 512), (1536, 512), (2048, 128)]
N_DMA_SEMS = 40
MAGIC = 12582912.0
TWO_PI = float(2 * np.pi)


class Prog:
    ENGS = ("pe", "act", "dve", "pool", "sp")

    def __init__(self, nc, stack):
        self.nc = nc
        self.ops = {e: [] for e in self.ENGS}
        self.cnt = {e: 0 for e in self.ENGS}
        self.seen = {e: {} for e in self.ENGS}
        self.state = {}
        self.dma_i = 0
        self.dma_tot = [0] * N_DMA_SEMS
        self.pending_dma = {e: [] for e in self.ENGS}
        self.sems = {e: stack.enter_context(nc.semaphore("s_" + e)) for e in self.ENGS}
        for i in range(N_DMA_SEMS):
            self.sems[("d", i)] = stack.enter_context(nc.semaphore("s_d%d" % i))
        self.nblocks = 0
        self.ninstr = 0
        self.relaxed = False

    def _need(self, eng, tok, waits):
        if tok is None:
            return
        sk, val = tok
        if sk == "pe" and eng == "pe":
            return
        if sk == eng and val > self.cnt[eng] and not DBG.get("strict"):
            return
        if (self.relaxed or eng == "dve") and not DBG.get("strict") and sk == eng and val <= self.cnt[eng] - 1:
            return
        if self.seen[eng].get(sk, 0) >= val:
            return
        self.seen[eng][sk] = val
        waits.append(tok)

    def _deps(self, eng, reads, writes):
        waits = []
        for k in reads:
            st = self.state.get(k)
            if st is not None:
                self._need(eng, st[0], waits)
        for k in writes:
            st = self.state.get(k)
            if st is not None:
                self._need(eng, st[0], waits)
                for t in st[1]:
                    self._need(eng, t, waits)
        return waits

    def _commit(self, tok, reads, writes):
        for k in reads:
            st = self.state.setdefault(k, [None, []])
            st[1].append(tok)
        for k in writes:
            self.state[k] = [tok, []]

    @staticmethod
    def _excl(r, w):
        r2 = [k for k in r if not (isinstance(k, tuple) and k[0] == "bk")]
        if len(r2) == len(r):
            return r, w
        return r2, list(w) + [k for k in r if isinstance(k, tuple) and k[0] == "bk"]

    def op(self, eng, fn, r=(), w=(), sig=True):
        r, w = self._excl(r, w)
        waits = self._deps(eng, r, w)
        tok = (eng, self.cnt[eng] + 1)
        if sig:
            self.cnt[eng] += 1
        self.ops[eng].append((waits, fn, eng if sig else None))
        self._commit(tok, r, w)
        return tok

    def dma(self, eng, out, in_, r=(), w=(), **kw):
        waits = self._deps(eng, r, w)
        si = self.dma_i % N_DMA_SEMS
        self.dma_i += 1
        prev = self.dma_tot[si]
        if prev > 0:
            self._need(eng, (("d", si), prev), waits)
        self.dma_tot[si] = prev + 16
        tok = (("d", si), prev + 16)
        fn = lambda e, out=out, in_=in_, kw=kw: e.dma_start(out=out, in_=in_, **kw)
        self.ops[eng].append((waits, fn, ("d", si)))
        self._commit(tok, r, w)
        self.pending_dma[eng].append(tok)
        return tok

    def flush(self):
        nc = self.nc
        for e in self.ENGS:
            fw = []
            for t in self.pending_dma[e]:
                self._need(e, t, fw)
            self.pending_dma[e] = []
            if fw:
                self.ops[e].append((fw, None, None))
        sems = self.sems
        if DBG.get("dump") and self.nblocks >= DBG["dump"]:
            for e in self.ENGS:
                print("ENGINE", e, "cnt_end", self.cnt[e])
                c = None
                for waits, fn, sg in self.ops[e]:
                    print("   waits", waits, "sig", sg, "fn", None if fn is None else fn.__code__.co_names[-1] if fn.__code__.co_names else "?")
        with nc.Block() as block:
            def run(engname):
                def body(e):
                    for waits, fn, sg in self.ops[engname]:
                        for sk, val in waits:
                            e.wait_ge(sems[sk], val)
                            self.ninstr += 1
                        if fn is None:
                            continue
                        ins = fn(e)
                        self.ninstr += 1
                        if sg is not None:
                            ins.then_inc(sems[sg], 16 if isinstance(sg, tuple) else 1)
                return body
            block.tensor(run("pe"))
            block.scalar(run("act"))
            block.vector(run("dve"))
            block.gpsimd(run("pool"))
            block.sync(run("sp"))
        self.ops = {e: [] for e in self.ENGS}
        self.state = {}
        self.nblocks += 1

    def mm(self, out, lhsT, rhs, start, stop, r, w):
        return self.op("pe", lambda e: e.matmul(out, lhsT, rhs, start=start, stop=stop), r, w, sig=stop)

    def tr(self, out, in_, ident, r, w):
        return self.op("pe", lambda e: e.transpose(out, in_, ident), r, w)

    def act(self, out, in_, func, r, w, bias=None, scale=1.0, accum_out=None):
        kw = {}
        if bias is not None:
            kw["bias"] = bias
        if accum_out is not None:
            kw["accum_out"] = accum_out
        return self.op("act", lambda e: e.activation(out=out, in_=in_, func=func, scale=scale, **kw), r, w)

    def tt(self, eng, out, a, b, op, r, w, sig=True):
        return self.op(eng, lambda e: e.tensor_tensor(out=out, in0=a, in1=b, op=op), r, w, sig=sig)

    def ts(self, eng, out, a, s1, s2, op0, op1, r, w):
        if s2 is None:
            return self.op(eng, lambda e: e.tensor_single_scalar(out=out, in_=a, scalar=s1, op=op0), r, w)
        return self.op(eng, lambda e: e.tensor_scalar(out=out, in0=a, scalar1=s1, scalar2=s2, op0=op0, op1=op1), r, w)

    def stt(self, out, in0, scalar, in1, op0, op1, r, w):
        return self.op("dve", lambda e: e.scalar_tensor_tensor(out=out, in0=in0, scalar=scalar, in1=in1, op0=op0, op1=op1), r, w)

    def cp(self, eng, out, in_, r, w):
        if eng == "act":
            return self.op("act", lambda e: e.copy(out, in_), r, w)
        return self.op(eng, lambda e: e.tensor_copy(out, in_), r, w)

    def ms(self, eng, ap, val, w):
        return self.op(eng, lambda e: e.memset(ap, val), (), w)

    def recip(self, out, in_, r, w):
        return self.op("dve", lambda e: e.reciprocal(out, in_), r, w)

    def scan(self, out, d0, d1, init, op0, op1, r, w):
        return self.op("dve", lambda e: e.tensor_tensor_scan(out=out, data0=d0, data1=d1, initial=init, op0=op0, op1=op1), r, w)


N_FM = 59
T_U, T_ZS5, T_X, T_B, T_C, T_DT, T_Q, T_K, T_V, T_BETA, T_ARAW, T_GATE = 0, 4, 8, 16, 18, 20, 21, 25, 29, 33, 34, 35
W_FM = 0
W_ZSSD = W_FM + N_FM * 1024
W_ZGDN = W_ZSSD + 8 * 1024
W_GLU = W_ZGDN + 8 * 512
W_S5O = W_GLU + 4 * 512
W_SSDO = W_S5O + 4 * 1024
W_GDNO = W_SSDO + 8 * 1024
W_OUT = W_GDNO + 4 * 1024
NW = W_OUT + 8 * 1024


def _kt_layout(w):
    k, c = w.shape
    return np.ascontiguousarray(w.reshape(k // 128, 128, c).transpose(1, 0, 2)).reshape(128, -1)


def _fm_cols():
    tiles = []
    for base, n in ((0, 4), (512, 4), (1024, 8), (3072, 2), (3328, 2)):
        for t in range(n):
            tiles.append(list(range(base + 128 * t, base + 128 * (t + 1))))
    dt = [-1] * 128
    dt[0:16] = range(3584, 3600)
    dt[32:48] = range(3600, 3616)
    tiles.append(dt)
    for base in (3616, 4128, 4640):
        for t in range(4):
            tiles.append(list(range(base + 128 * t, base + 128 * (t + 1))))
    be = [-1] * 128
    be[0:4] = range(5664, 5668)
    be[32:36] = range(5668, 5672)
    tiles.append(be)
    ar = [-1] * 128
    ar[0:4] = range(5672, 5676)
    ar[32:36] = range(5676, 5680)
    tiles.append(ar)
    for b in range(3):
        for t in range(8):
            tiles.append(list(range(5680 + b * 1024 + 128 * t, 5680 + b * 1024 + 128 * (t + 1))))
    assert len(tiles) == N_FM
    return tiles


S_NORMW = 0
S_ARE_A = 8
S_AIM_A = 40
S_LDT_A = 72
S_ARE_B = 104
S_AIM_B = 616
S_LDT_B = 1128
S_BRE = 1640
S_BIM = S_BRE + 4096
S_CRE = S_BIM + 4096
S_CIM = S_CRE + 4096
S_S5D = S_CIM + 4096
S_BGLU = S_S5D + 4
S_SCW = S_BGLU + 4
S_SCB = S_SCW + 60
S_SALOG = S_SCB + 12
S_SDTB = S_SALOG + 1
S_SD = S_SDTB + 1
S_SNW = S_SD + 16
S_GCW = S_SNW + 1024
S_GALOG = S_GCW + 60
S_GDTB = S_GALOG + 1
S_GNW = S_GDTB + 1
NS = S_GNW + 128


def _arrange_layer(i, inp):
    f = np.float32
    wall = np.zeros((128, NW), f)
    win = np.asarray(inp["w_in"][i], f)
    winp = np.concatenate([win, np.zeros((D, 1), f)], axis=1)
    for t, cols in enumerate(_fm_cols()):
        wall[:, W_FM + t * 1024:W_FM + (t + 1) * 1024] = _kt_layout(winp[:, cols])
    wall[:, W_ZSSD:W_ZGDN] = _kt_layout(win[:, 2048:3072])
    wall[:, W_ZGDN:W_GLU] = _kt_layout(win[:, 5152:5664])
    wall[:, W_GLU:W_S5O] = _kt_layout(np.asarray(inp["s5_w_glu"][i], f))
    wall[:, W_S5O:W_SSDO] = _kt_layout(np.asarray(inp["w_s5_out"][i], f))
    wall[:, W_SSDO:W_GDNO] = _kt_layout(np.asarray(inp["w_ssd_out"][i], f))
    wall[:, W_GDNO:W_OUT] = _kt_layout(np.asarray(inp["w_gdn_out"][i], f))
    wall[:, W_OUT:NW] = _kt_layout(np.asarray(inp["w_out"][i], f))

    sm = np.zeros((128, NS), f)
    sm[:, S_NORMW:S_NORMW + 8] = np.asarray(inp["norm_w"][i], f).reshape(8, 128).T
    are = np.asarray(inp["s5_a_re"][i], f)
    aim = np.asarray(inp["s5_a_im"][i], f)
    ldt = np.asarray(inp["s5_log_dt"][i], f)
    for nm, off in ((are, S_ARE_A), (aim, S_AIM_A)):
        a4 = nm.reshape(2, 16, 2, 64)
        sm[:, off:off + 32] = a4.transpose(2, 3, 0, 1).reshape(128, 32)
    l4 = np.broadcast_to(ldt.reshape(2, 16, 2, 1), (2, 16, 2, 64))
    sm[:, S_LDT_A:S_LDT_A + 32] = l4.transpose(2, 3, 0, 1).reshape(128, 32)
    for nm, off in ((are, S_ARE_B), (aim, S_AIM_B)):
        a5 = nm.reshape(2, 4, 4, 2, 1, 64)
        a5 = np.broadcast_to(a5, (2, 4, 4, 2, 16, 64))
        sm[:, off:off + 512] = a5.transpose(2, 3, 4, 0, 1, 5).reshape(128, 512)
    l5 = np.broadcast_to(ldt.reshape(2, 4, 4, 2, 1, 1), (2, 4, 4, 2, 16, 64))
    sm[:, S_LDT_B:S_LDT_B + 512] = l5.transpose(2, 3, 4, 0, 1, 5).reshape(128, 512)
    for nm, off in ((inp["s5_b_re"], S_BRE), (inp["s5_b_im"], S_BIM)):
        b = np.asarray(nm[i], f).reshape(2, 4, 4, 2, 64, 16)
        z = np.zeros((4, 2, 16, 2, 4, 4, 2, 64), f)
        for q in range(4):
            for m in range(2):
                z[q, m, :, :, :, q, m, :] = b[:, :, q, m].transpose(3, 0, 1, 2)
        sm[:, off:off + 4096] = z.reshape(128, 4096)
    for nm, off in ((inp["s5_c_re"], S_CRE), (inp["s5_c_im"], S_CIM)):
        c = np.asarray(nm[i], f).reshape(2, 16, 2, 16, 64)
        z = np.zeros((2, 64, 2, 16, 4, 2, 16), f)
        for pr in range(16):
            for m in range(2):
                z[m, :, :, pr, pr % 4, m, :] = c[:, pr, m].transpose(2, 0, 1)
        sm[:, off:off + 4096] = z.reshape(128, 4096)
    sm[:, S_S5D:S_S5D + 4] = np.asarray(inp["s5_d"][i], f).reshape(4, 128).T
    sm[:, S_BGLU:S_BGLU + 4] = np.asarray(inp["s5_b_glu"][i], f).reshape(4, 128).T
    scw = np.asarray(inp["ssd_conv_w"][i], f)
    sm[:, S_SCW:S_SCW + 60] = scw.reshape(5, 12, 128).transpose(2, 1, 0).reshape(128, 60)
    sm[:, S_SCB:S_SCB + 12] = np.asarray(inp["ssd_conv_b"][i], f).reshape(12, 128).T
    for nm, off in ((inp["ssd_a_log"], S_SALOG), (inp["ssd_dt_bias"], S_SDTB)):
        v = np.asarray(nm[i], f)
        sm[0:16, off] = v[0]
        sm[32:48, off] = v[1]
    sm[:, S_SD:S_SD + 16] = np.asarray(inp["ssd_d"][i], f)[None, :]
    sm[:, S_SNW:S_SNW + 1024] = np.asarray(inp["ssd_norm_w"][i], f)[None, :]
    gcw = np.asarray(inp["gdn_conv_w"][i], f)
    sm[:, S_GCW:S_GCW + 60] = gcw.reshape(5, 12, 128).transpose(2, 1, 0).reshape(128, 60)
    for nm, off in ((inp["gdn_a_log"], S_GALOG), (inp["gdn_dt_bias"], S_GDTB)):
        v = np.asarray(nm[i], f)
        sm[0:4, off] = v[0]
        sm[32:36, off] = v[1]
    sm[:, S_GNW:S_GNW + 128] = np.asarray(inp["gdn_norm_w"][i], f)[None, :]
    return wall, sm


C_IDENT = 0
C_ONES = 128
NC_CONST = 256
K_SEL = 0
K_NEGF = K_SEL + 32 * 128
K_NEGB = K_NEGF + 128
K_LASTF = K_NEGB + 128
K_LASTB = K_LASTF + 128
K_SELG = K_LASTB + 128
K_GNEGF = K_SELG + 8 * 128
K_GNEGB = K_GNEGF + 128
K_GSTRF = K_GNEGB + 128
K_GSTRB = K_GSTRF + 128
K_GLASTF = K_GSTRB + 128
K_GLASTB = K_GLASTF + 128
K_GCHF = K_GLASTB + 128
K_GCHB = K_GCHF + 256
NK = K_GCHB + 256
NEG = -30000.0


def _consts():
    c = np.zeros((128, NC_CONST), np.float32)
    c[:, C_IDENT:C_IDENT + 128] = np.eye(128, dtype=np.float32)
    c[:, C_ONES:C_ONES + 128] = 1.0
    return c


def _consts2():
    k = np.zeros((128, NK), np.float32)
    for d in range(2):
        for h in range(16):
            k[d * 32 + h, K_SEL + (d * 16 + h) * 128:K_SEL + (d * 16 + h + 1) * 128] = 1.0
        for h in range(4):
            k[d * 32 + h, K_SELG + (d * 4 + h) * 128:K_SELG + (d * 4 + h + 1) * 128] = 1.0
    j = np.arange(128)[:, None]
    i = np.arange(128)[None, :]
    same = (j // 64) == (i // 64)
    k[:, K_NEGF:K_NEGF + 128] = np.where(j <= i, 0.0, NEG)
    k[:, K_NEGB:K_NEGB + 128] = np.where(j >= i, 0.0, NEG)
    k[127, K_LASTF:K_LASTF + 128] = 1.0
    k[0, K_LASTB:K_LASTB + 128] = 1.0
    k[:, K_GNEGF:K_GNEGF + 128] = np.where(same & (j <= i), 0.0, NEG)
    k[:, K_GNEGB:K_GNEGB + 128] = np.where(same & (j >= i), 0.0, NEG)
    k[:, K_GSTRF:K_GSTRF + 128] = np.where(same & (j < i), 1.0, 0.0)
    k[:, K_GSTRB:K_GSTRB + 128] = np.where(same & (j > i), 1.0, 0.0)
    k[:, K_GLASTF:K_GLASTF + 128] = np.where(j == 64 * (i // 64) + 63, 1.0, 0.0)
    k[:, K_GLASTB:K_GLASTB + 128] = np.where(j == 64 * (i // 64), 1.0, 0.0)
    for sc in range(2):
        k[64 * sc + 63, K_GCHF + sc * 128:K_GCHF + (sc + 1) * 128] = 1.0
        k[64 * sc, K_GCHB + sc * 128:K_GCHB + (sc + 1) * 128] = 1.0
    return k


_UID = [0]


def sbt(nc, name, shape, dt):
    _UID[0] += 1
    return nc.sbuf_tensor("%s_%d" % (name, _UID[0]), shape, dt)


def rawap(t_ap, off, dims):
    return bass.AP(t_ap.tensor, t_ap.offset + off, [list(t_ap.ap[0])] + [list(d) for d in dims])


def fm_tile(env, l, t, n=1):
    return env["wallb"][l, :, W_FM + t * 1024:W_FM + (t + n) * 1024].rearrange("p (n k c) -> p n k c", n=n, k=8)


def phase_norm(env, l):
    nc, P, banks = env["nc"], env["P"], env["banks"]
    with ExitStack() as ph:
        nw = ph.enter_context(sbt(nc, "nw", [128, 8], F32))
        P.dma("sp", nw[:], env["small_in"][l, :, S_NORMW:S_NORMW + 8], w=["nw"])
        hb = [ph.enter_context(sbt(nc, "nh%d" % i, [128, 8, 512], F32)) for i in range(2)]
        sqb = [ph.enter_context(sbt(nc, "nsq%d" % i, [128, 8, 512], BF16)) for i in range(2)]
        rsb = [ph.enter_context(sbt(nc, "nrs%d" % i, [128, 512], F32)) for i in range(2)]
        hnb = [ph.enter_context(sbt(nc, "nhn%d" % i, [128, 8, 512], BF16)) for i in range(2)]
        it = 0
        for s in range(env["nseq"]):
            for (c0, bw) in BLOCKS:
                i = it % 2
                it += 1
                key = "n%d" % i
                P.dma("sp", hb[i][:, :, 0:bw], env["hT"][s, :, :, c0:c0 + bw], w=[key + "h"])
                P.act(sqb[i][:, :, 0:bw], hb[i][:, :, 0:bw], AF.Square, [key + "h"], [key + "sq"])
                for kt in range(8):
                    P.mm(banks[it % 2][:, 0:bw], env["onesb"], sqb[i][:, kt, 0:bw], kt == 0, kt == 7, ["cstb", key + "sq"], [("bk", it % 2)])
                P.act(rsb[i][:, 0:bw], banks[it % 2][:, 0:bw], AF.Sqrt, [("bk", it % 2)], [key + "rs"], bias=EPS, scale=1.0 / D)
                P.recip(rsb[i][:, 0:bw], rsb[i][:, 0:bw], [key + "rs"], [key + "rs"])
                for kt in range(8):
                    P.stt(hnb[i][:, kt, 0:bw], hb[i][:, kt, 0:bw], nw[:, kt:kt + 1], rsb[i][:, 0:bw], ALU.mult, ALU.mult,
                          [key + "h", key + "rs", "nw"], [key + "hn"])
                P.dma("sp", env["hnT"][s, :, :, c0:c0 + bw], hnb[i][:, :, 0:bw], r=[key + "hn"])
        P.flush()


def rr_sin(P, out, x, shift, tmp, rk, wk, tk):
    P.ts("dve", out, x, float(shift), None, ALU.add, None, rk, [wk])
    P.ts("dve", tmp, out, 1.0 / TWO_PI, MAGIC, ALU.mult, ALU.add, [wk], [tk])
    P.ts("dve", tmp, tmp, -MAGIC, -TWO_PI, ALU.add, ALU.mult, [tk], [tk])
    P.tt("dve", tmp, tmp, out, ALU.add, [tk, wk], [tk])
    P.act(out, tmp, AF.Sin, [tk], [wk])


def s5_lambda(P, nc, ph, src, n, tag):
    t = {}
    for nm in ("dt", "xr", "th", "mag", "sn", "cs", "lr", "li", "tmp"):
        t[nm] = ph.enter_context(sbt(nc, tag + nm, [128, n], F32))
    k = tag
    P.act(t["dt"][:], src[:, 2, :], AF.Exp, [k + "src"], [k + "dt"])
    P.tt("dve", t["xr"][:], src[:, 0, :], t["dt"][:], ALU.mult, [k + "src", k + "dt"], [k + "xr"])
    P.tt("dve", t["th"][:], src[:, 1, :], t["dt"][:], ALU.mult, [k + "src", k + "dt"], [k + "th"])
    P.act(t["mag"][:], t["xr"][:], AF.Exp, [k + "xr"], [k + "mag"])
    rr_sin(P, t["sn"][:], t["th"][:], 0.0, t["tmp"][:], [k + "th"], k + "sn", k + "tmp")
    rr_sin(P, t["cs"][:], t["th"][:], float(np.pi / 2), t["tmp"][:], [k + "th"], k + "cs", k + "tmp")
    P.tt("dve", t["lr"][:], t["mag"][:], t["cs"][:], ALU.mult, [k + "mag", k + "cs"], [k + "lr"])
    P.tt("dve", t["li"][:], t["mag"][:], t["sn"][:], ALU.mult, [k + "mag", k + "sn"], [k + "li"])
    return t


def phase_s5(env, l):
    nc, P, banks = env["nc"], env["P"], env["banks"]
    small = env["small_in"]
    with ExitStack() as lay:
        LA = lay.enter_context(sbt(nc, "LA", [128, 32, 2], F32))
        LB = lay.enter_context(sbt(nc, "LB", [128, 32, 2], F32))
        WB = lay.enter_context(sbt(nc, "WB", [128, 2, 4096], BF16))
        WO = lay.enter_context(sbt(nc, "WO", [128, 2, 4096], BF16))
        s5d = lay.enter_context(sbt(nc, "s5d", [128, 8], F32))
        P.dma("sp", s5d[:], small[l, :, S_S5D:S_S5D + 8], w=["s5d"])
        with ExitStack() as ph:
            srcA = ph.enter_context(sbt(nc, "srcA", [128, 3, 32], F32))
            srcB = ph.enter_context(sbt(nc, "srcB", [128, 3, 512], F32))
            P.dma("sp", srcA[:], small[l, :, S_ARE_A:S_ARE_A + 96].rearrange("p (a n) -> p a n", a=3), w=["Asrc"])
            P.dma("sp", srcB[:], small[l, :, S_ARE_B:S_ARE_B + 1536].rearrange("p (a n) -> p a n", a=3), w=["Bsrc"])
            tA = s5_lambda(P, nc, ph, srcA, 32, "A")
            P.cp("dve", LA[:, :, 0], tA["lr"][:], ["Alr"], ["LA0"])
            P.cp("dve", LA[:, :, 1], tA["lr"][:], ["Alr"], ["LA1"])
            P.ts("dve", LB[:, :, 0], tA["li"][:], -1.0, None, ALU.mult, None, ["Ali"], ["LB0"])
            P.cp("dve", LB[:, :, 1], tA["li"][:], ["Ali"], ["LB1"])
            tB = s5_lambda(P, nc, ph, srcB, 512, "B")
            den = ph.enter_context(sbt(nc, "den", [128, 512], F32))
            t1 = ph.enter_context(sbt(nc, "pt1", [128, 512], F32))
            t2 = ph.enter_context(sbt(nc, "pt2", [128, 512], F32))
            fre = ph.enter_context(sbt(nc, "fre", [128, 512], F32))
            fim = ph.enter_context(sbt(nc, "fim", [128, 512], F32))
            are, aim = srcB[:, 0, :], srcB[:, 1, :]
            P.tt("dve", den[:], are, are, ALU.mult, ["Bsrc"], ["den"])
            P.tt("dve", t1[:], aim, aim, ALU.mult, ["Bsrc"], ["t1"])
            P.tt("dve", den[:], den[:], t1[:], ALU.add, ["den", "t1"], ["den"])
            P.recip(den[:], den[:], ["den"], ["den"])
            lm1 = tB["tmp"]
            P.ts("dve", lm1[:], tB["lr"][:], -1.0, None, ALU.add, None, ["Blr"], ["Btmp"])
            P.tt("dve", t1[:], lm1[:], are, ALU.mult, ["Btmp", "Bsrc", "den"], ["t1"])
            P.tt("dve", t2[:], tB["li"][:], aim, ALU.mult, ["Bli", "Bsrc"], ["t2"])
            P.tt("dve", t1[:], t1[:], t2[:], ALU.add, ["t1", "t2"], ["t1"])
            P.tt("dve", fre[:], t1[:], den[:], ALU.mult, ["t1", "den"], ["fre"])
            P.tt("dve", t1[:], tB["li"][:], are, ALU.mult, ["Bli", "Bsrc", "fre"], ["t1"])
            P.tt("dve", t2[:], lm1[:], aim, ALU.mult, ["Btmp", "Bsrc"], ["t2"])
            P.tt("dve", t1[:], t1[:], t2[:], ALU.subtract, ["t1", "t2"], ["t1"])
            P.tt("dve", fim[:], t1[:], den[:], ALU.mult, ["t1", "den"], ["fim"])
            braw = ph.enter_context(sbt(nc, "braw", [128, 2, 4096], F32))
            P.dma("sp", braw[:], small[l, :, S_BRE:S_BRE + 8192].rearrange("p (a n) -> p a n", a=2), w=["braw"])
            bt1 = ph.enter_context(sbt(nc, "bt1", [128, 4096], F32))
            bt2 = ph.enter_context(sbt(nc, "bt2", [128, 4096], F32))
            v4 = lambda ap: ap.rearrange("p (d q n) -> p d q n", d=8, q=8)
            fb = lambda f: f[:].rearrange("p (d n) -> p d n", d=8).unsqueeze(2).broadcast_to([128, 8, 8, 64])
            P.tt("dve", v4(bt1[:]), v4(braw[:, 0, :]), fb(fre), ALU.mult, ["braw", "fre"], ["bt1"])
            P.tt("dve", v4(bt2[:]), v4(braw[:, 1, :]), fb(fim), ALU.mult, ["braw", "fim"], ["bt2"])
            P.tt("dve", WB[:, 0, :], bt1[:], bt2[:], ALU.subtract, ["bt1", "bt2"], ["WB0"])
            P.tt("dve", v4(bt1[:]), v4(braw[:, 1, :]), fb(fre), ALU.mult, ["braw", "fre", "WB0"], ["bt1"])
            P.tt("dve", v4(bt2[:]), v4(braw[:, 0, :]), fb(fim), ALU.mult, ["braw", "fim", "WB0"], ["bt2"])
            P.tt("dve", WB[:, 1, :], bt1[:], bt2[:], ALU.add, ["bt1", "bt2"], ["WB1"])
            P.dma("sp", braw[:], small[l, :, S_CRE:S_CRE + 8192].rearrange("p (a n) -> p a n", a=2), r=["WB0", "WB1"], w=["craw"])
            P.cp("act", WO[:, 0, :], braw[:, 0, :], ["craw"], ["WO0"])
            P.ts("dve", WO[:, 1, :], braw[:, 1, :], -1.0, None, ALU.mult, None, ["craw"], ["WO1"])
            P.flush()

        NS = env["nseq"]
        TB = 64
        NB = LP // TB
        with ExitStack() as ph:
            uT = [ph.enter_context(sbt(nc, "uT", [128, 4, LP], BF16)) for s in range(NS)]
            ybuf = [ph.enter_context(sbt(nc, "ybuf", [128, 4, LP], BF16)) for s in range(NS)]
            with ExitStack() as pa:
                hnb = [pa.enter_context(sbt(nc, "s5hn%d" % i, [128, 8, 512], BF16)) for i in range(2)]
                wu = pa.enter_context(sbt(nc, "wu", [128, 4, 8, 128], BF16))
                P.dma("sp", wu[:], fm_tile(env, l, T_U, 4), w=["wu"])
                it = 0
                for s in range(NS):
                    for (c0, bw) in BLOCKS:
                        i = it % 2
                        it += 1
                        P.dma("sp", hnb[i][:, :, 0:bw], env["hnT"][s, :, :, c0:c0 + bw], w=[("hn", i)])
                        for a in range(4):
                            bk = (it * 4 + a) % 8
                            for kt in range(8):
                                P.mm(banks[bk][:, 0:bw], wu[:, a, kt, :], hnb[i][:, kt, 0:bw], kt == 0, kt == 7, ["wu", ("hn", i)], [("bk", bk)])
                            P.cp("act" if a % 2 else "dve", uT[s][:, a, c0:c0 + bw], banks[bk][:, 0:bw], [("bk", bk)], [("uT", s, a)])
                P.flush()
            with ExitStack() as phb:
                NX = NS * 64
                BU = phb.enter_context(sbt(nc, "BU", [128, TB, NX], F32))
                cur = phb.enter_context(sbt(nc, "Hh", [128, TB, NX], F32))
                Hc = phb.enter_context(sbt(nc, "Hc", [128, NX], F32))
                Hb = phb.enter_context(sbt(nc, "Hb", [128, TB, NX], BF16))
                P1 = phb.enter_context(sbt(nc, "P1", [128, NX], F32))
                Q1 = phb.enter_context(sbt(nc, "Q1", [128, NX], F32))
                ytmp = [phb.enter_context(sbt(nc, "ytmp%d" % i, [128, TB], F32)) for i in range(4)]
                CH = 1024
                cjobs = []
                if env.get("cast_next") and l + 1 < env["n_layers"]:
                    cstg = [phb.enter_context(sbt(nc, "cstg%d" % i, [128, CH], F32)) for i in range(2)]
                    cstgb = [phb.enter_context(sbt(nc, "cstgb%d" % i, [128, CH], BF16)) for i in range(2)]
                    cjobs = [(c0, min(CH, NW - c0)) for c0 in range(0, NW, CH)]
                P.ms("pool", Hc[:], 0.0, ["Hc"])
                LAd = [LA[:].rearrange("p (d q) t -> p d (q t)", d=2)[:, d, :].unsqueeze(1).broadcast_to([128, NS, 32]) for d in range(2)]
                LBd = [LB[:].rearrange("p (d q) t -> p d q t", d=2)[:, d, :, :].unsqueeze(1).broadcast_to([128, NS, 16, 2]) for d in range(2)]
                P1d = [P1[:, d * NS * 32:(d + 1) * NS * 32].rearrange("p (s x) -> p s x", s=NS) for d in range(2)]
                Q1d = [Q1[:, d * NS * 32:(d + 1) * NS * 32].rearrange("p (s x) -> p s x", s=NS) for d in range(2)]
                Q1d4 = [Q1[:, d * NS * 32:(d + 1) * NS * 32].rearrange("p (s q t) -> p s q t", s=NS, q=16) for d in range(2)]
                BUf = BU[:].rearrange("p x t -> p (x t)")
                nyt = 0
                for tb in range(NB):
                    tbd = (tb, NB - 1 - tb)
                    g = 0
                    for s in range(NS):
                        for d in range(2):
                            for kk in range(4):
                                bk = g % 8
                                g += 1
                                for j8 in range(8):
                                    pr, part = 4 * kk + j8 // 2, j8 % 2
                                    a, q0 = pr // 4, pr % 4
                                    col = ((d * 4 + a) * 4 + q0) * 128
                                    P.mm(banks[bk][:, j8 * TB:(j8 + 1) * TB], WB[:, part, col:col + 128],
                                         uT[s][:, a, tbd[d] * TB:(tbd[d] + 1) * TB], True, True, [("uT", s, a)], [("bk", bk)])
                                x0 = s * 64 + d * 32 + kk * 8
                                P.cp("act", BU[:, :, x0:x0 + 8].rearrange("p t x -> p x t"), banks[bk][:, :].rearrange("p (x t) -> p x t", x=8),
                                     [("bk", bk)], ["BU"])
                    for _ in range(3):
                        if cjobs:
                            c0_, cw_ = cjobs.pop(0)
                            ci_ = len(cjobs) % 2
                            P.dma("sp", cstg[ci_][:, 0:cw_], env["wall_in"][l + 1, :, c0_:c0_ + cw_], w=[("cstg", ci_)])
                            P.cp("pool", cstgb[ci_][:, 0:cw_], cstg[ci_][:, 0:cw_], [("cstg", ci_)], [("cstgb", ci_)])
                            P.dma("sp", env["wallb"][l + 1, :, c0_:c0_ + cw_], cstgb[ci_][:, 0:cw_], r=[("cstgb", ci_)])
                    P.relaxed = True
                    for j in range(TB):
                        tok = (j, TB - 1 - j)
                        hp, hps, hn_, bu_, pk = [], [], [], [], []
                        for d in range(2):
                            base = d * 32 * TB
                            if j == 0:
                                hp.append(rawap(Hc[:], d * 32, [[64, NS], [1, 32]]))
                                hps.append(rawap(Hc[:], d * 32 + 1, [[64, NS], [2, 16], [-1, 2]]))
                                pk.append("Hc")
                            else:
                                tp_ = tok[d] - 1 if d == 0 else tok[d] + 1
                                hp.append(rawap(cur[:], tp_ * NX + d * 32, [[64, NS], [1, 32]]))
                                hps.append(rawap(cur[:], tp_ * NX + d * 32 + 1, [[64, NS], [2, 16], [-1, 2]]))
                                pk.append(("Hh", d))
                            hn_.append(rawap(cur[:], tok[d] * NX + d * 32, [[64, NS], [1, 32]]))
                            bu_.append(rawap(BU[:], tok[d] * NX + d * 32, [[64, NS], [1, 32]]))
                        sg_ = bool(DBG.get("strict"))
                        for d in range(2):
                            P.tt("dve", P1d[d], LAd[d], hp[d], ALU.mult, [pk[d]], [("P1", d)], sig=sg_)
                        for d in range(2):
                            P.tt("dve", Q1d4[d], LBd[d], hps[d], ALU.mult, [pk[d]], [("Q1", d)], sig=sg_)
                        for d in range(2):
                            P.tt("dve", P1d[d], P1d[d], Q1d[d], ALU.add, [("P1", d), ("Q1", d)], [("P1", d)], sig=sg_)
                        for d in range(2):
                            P.tt("dve", hn_[d], P1d[d], bu_[d], ALU.add, [("P1", d), "BU"], [("Hh", d)], sig=sg_)
                    P.relaxed = False
                    if tb == NB - 1:
                        assert not cjobs
                    for d in range(2):
                        last = TB - 1 if d == 0 else 0
                        P.cp("dve", rawap(Hc[:], d * 32, [[64, NS], [1, 32]]), rawap(cur[:], last * NX + d * 32, [[64, NS], [1, 32]]),
                             [("Hh", d)], ["Hc"])
                    for s in range(NS):
                        P.cp("pool" if s % 2 == 0 else "act", Hb[:, :, s * 64:(s + 1) * 64], cur[:, :, s * 64:(s + 1) * 64], [("Hh", 0), ("Hh", 1)], [("Hb", s)])
                    for s in range(NS):
                        for d in range(2):
                            b_ = tbd[d]
                            first = (d == 0 and b_ <= NB // 2 - 1) or (d == 1 and b_ >= NB // 2)
                            for a in range(4):
                                bk = (s * 8 + d * 4 + a) % 8
                                n_ = 0
                                for q0 in range(4):
                                    for part in range(2):
                                        pr = 4 * a + q0
                                        col = (d * 16 + pr) * 128
                                        P.mm(banks[bk][:, 0:TB], WO[:, part, col:col + 128], Hb[:, :, s * 64 + d * 32 + pr * 2 + part],
                                             n_ == 0, n_ == 7, [("Hb", s)], [("bk", bk)])
                                        n_ += 1
                                ysl = ybuf[s][:, a, b_ * TB:(b_ + 1) * TB]
                                yk = ("yb", s, a, b_)
                                if first:
                                    P.cp("act", ysl, banks[bk][:, 0:TB], [("bk", bk)], [yk])
                                else:
                                    yt = ytmp[nyt % 4]
                                    P.cp("act", yt[:], banks[bk][:, 0:TB], [("bk", bk)], [("ytmp", nyt % 4)])
                                    P.tt("pool", ysl, ysl, yt[:], ALU.add, [("ytmp", nyt % 4), yk], [yk])
                                    nyt += 1
                P.flush()
            with ExitStack() as pc:
                hnb = [pc.enter_context(sbt(nc, "s5hnc%d" % i, [128, 8, 512], BF16)) for i in range(2)]
                wzs = pc.enter_context(sbt(nc, "wzs", [128, 4, 8, 128], BF16))
                wglu = pc.enter_context(sbt(nc, "wglu", [128, 4, 512], BF16))
                P.dma("sp", wzs[:], fm_tile(env, l, T_ZS5, 4), w=["wzs"])
                P.dma("sp", wglu[:], env["wallb"][l, :, W_GLU:W_GLU + 2048].rearrange("p (k c) -> p k c", k=4), w=["wglu"])
                yv = pc.enter_context(sbt(nc, "yv", [128, 4, 512], F32))
                ygb = pc.enter_context(sbt(nc, "ygb", [128, 4, 512], BF16))
                sg = pc.enter_context(sbt(nc, "sg", [128, 512], F32))
                zs = pc.enter_context(sbt(nc, "zs", [128, 512], F32))
                yo = [pc.enter_context(sbt(nc, "yo%d" % i, [128, 4, 512], BF16)) for i in range(2)]
                it = 0
                for s in range(NS):
                    for (c0, bw) in BLOCKS:
                        i = it % 2
                        it += 1
                        P.dma("sp", hnb[i][:, :, 0:bw], env["hnT"][s, :, :, c0:c0 + bw], w=[("hn", i)])
                        for a in range(4):
                            P.stt(yv[:, a, 0:bw], uT[s][:, a, c0:c0 + bw], s5d[:, a:a + 1], ybuf[s][:, a, c0:c0 + bw], ALU.mult, ALU.add,
                                  [], [("yv", a)])
                            P.act(yv[:, a, 0:bw], yv[:, a, 0:bw], AF.Gelu_apprx_tanh, [("yv", a)], [("yv", a)])
                            P.cp("pool", ygb[:, a, 0:bw], yv[:, a, 0:bw], [("yv", a)], [("ygb", a)])
                        for ao in range(4):
                            b0, b1 = ao % 2, 2 + ao % 2
                            for kt in range(4):
                                P.mm(banks[b0][:, 0:bw], wglu[:, kt, ao * 128:(ao + 1) * 128], ygb[:, kt, 0:bw], kt == 0, kt == 3,
                                     [("ygb", k) for k in range(4)] + ["wglu"], [("bk", b0)])
                            P.act(sg[:, 0:bw], banks[b0][:, 0:bw], AF.Sigmoid, [("bk", b0)], ["sg"], bias=s5d[:, 4 + ao:5 + ao])
                            for kt in range(8):
                                P.mm(banks[b1][:, 0:bw], wzs[:, ao, kt, :], hnb[i][:, kt, 0:bw], kt == 0, kt == 7, [("hn", i), "wzs"], [("bk", b1)])
                            P.act(zs[:, 0:bw], banks[b1][:, 0:bw], AF.Silu, [("bk", b1)], ["zs"])
                            P.tt("dve", sg[:, 0:bw], sg[:, 0:bw], yv[:, ao, 0:bw], ALU.mult, ["sg", ("yv", ao)], ["sg"])
                            P.tt("dve", yo[i][:, ao, 0:bw], sg[:, 0:bw], zs[:, 0:bw], ALU.mult, ["sg", "zs"], [("yo", i)])
                        P.dma("sp", env["ys5T"][s, :, :, c0:c0 + bw], yo[i][:, :, 0:bw], r=[("yo", i)])
                P.flush()


def phase_ssd(env, l):
    nc, P, banks = env["nc"], env["P"], env["banks"]
    small, K2 = env["small_in"], env["const2"]
    identf, identb = env["identf"], env["identb"]
    with ExitStack() as lay:
        scw = lay.enter_context(sbt(nc, "scw", [128, 72], F32))
        sal = lay.enter_context(sbt(nc, "sal", [64, 2], F32))
        nega = lay.enter_context(sbt(nc, "nega", [64, 1], F32))
        sdn = lay.enter_context(sbt(nc, "sdn", [128, 1040], F32))
        wz = lay.enter_context(sbt(nc, "wz", [128, 8, 1024], BF16))
        selb = lay.enter_context(sbt(nc, "selb", [64, 32, 128], BF16))
        negm = lay.enter_context(sbt(nc, "negm", [128, 2, 128], BF16))
        lastm = lay.enter_context(sbt(nc, "lastm", [128, 2, 128], F32))
        with ExitStack() as p0:
            stg = p0.enter_context(sbt(nc, "kstg", [128, 4096 + 256], F32))
            P.dma("sp", scw[:], small[l, :, S_SCW:S_SCW + 72], w=["scw"])
            P.dma("sp", sal[:], small[l, 0:64, S_SALOG:S_SALOG + 2], w=["sal"])
            P.dma("sp", sdn[:], small[l, :, S_SD:S_SD + 1040], w=["sdn"])
            P.dma("sp", wz[:], env["wallb"][l, :, W_ZSSD:W_ZSSD + 8192].rearrange("p (k c) -> p k c", k=8), w=["wz"])
            P.dma("sp", stg[:], K2[:, K_SEL:K_SEL + 4096 + 256], w=["kstg"])
            P.dma("sp", lastm[:], K2[:, K_LASTF:K_LASTF + 256].rearrange("p (a c) -> p a c", a=2), w=["lastm"])
            P.cp("dve", selb[:].rearrange("p a c -> p (a c)"), stg[0:64, 0:4096], ["kstg"], ["selb"])
            P.cp("dve", negm[:].rearrange("p a c -> p (a c)"), stg[:, 4096:4352], ["kstg"], ["negm"])
            P.act(nega[:], sal[:, 0:1], AF.Exp, ["sal"], ["nega"])
            P.ts("dve", nega[:], nega[:], -1.0, None, ALU.mult, None, ["nega"], ["nega"])
            P.flush()

        for s in range(env["nseq"]):
            with ExitStack() as ph:
                hn = ph.enter_context(sbt(nc, "shn", [128, 8, LP], BF16))
                xtok = ph.enter_context(sbt(nc, "xtok", [128, NT, 1024], BF16))
                Btok = ph.enter_context(sbt(nc, "Btok", [128, NT, 256], BF16))
                BT = ph.enter_context(sbt(nc, "BT", [128, 2, LP], BF16))
                CT = ph.enter_context(sbt(nc, "CT", [128, 2, LP], BF16))
                atok = ph.enter_context(sbt(nc, "atok", [128, NT, 4, 64], F32))
                ahl = ph.enter_context(sbt(nc, "ahl", [64, 2, LP], BF16))
                for kt in range(8):
                    P.dma("sp", hn[:, kt, :], env["hnT"][s, :, kt, :], w=[("hn", kt)])
                hnk = [("hn", kt) for kt in range(8)]
                with ExitStack() as p1:
                    wcv = p1.enter_context(sbt(nc, "wcv", [128, 12, 8, 128], BF16))
                    raw = [p1.enter_context(sbt(nc, "raw%d" % i, [128, LP + 4], F32)) for i in range(2)]
                    acc1 = p1.enter_context(sbt(nc, "acc", [128, LP], F32))
                    acc = [acc1, acc1]
                    xc1 = p1.enter_context(sbt(nc, "xc", [128, LP], BF16))
                    xc = [xc1, xc1]
                    P.dma("sp", wcv[:], fm_tile(env, l, T_X, 12), w=["wcv"])
                    for i in range(2):
                        P.ms("pool", raw[i][:, 0:2], 0.0, [("raw", i)])
                        P.ms("pool", raw[i][:, LP + 2:LP + 4], 0.0, [("raw", i)])
                    nb_ = 0
                    for ct in range(12):
                        i = ct % 2
                        for (c0, bw) in BLOCKS:
                            bk = nb_ % 4
                            nb_ += 1
                            for kt in range(8):
                                P.mm(banks[bk][:, 0:bw], wcv[:, ct, kt, :], hn[:, kt, c0:c0 + bw], kt == 0, kt == 7, ["wcv"] + hnk, [("bk", bk)])
                            P.cp("act", raw[i][:, 2 + c0:2 + c0 + bw], banks[bk][:, 0:bw], [("bk", bk)], [("raw", i)])
                        P.ts("dve", acc[i][:], raw[i][:, 0:LP], scw[:, ct * 5:ct * 5 + 1], None, ALU.mult, None, [("raw", i), "scw"], ["acc"])
                        for k in range(1, 5):
                            P.stt(acc[i][:], raw[i][:, k:k + LP], scw[:, ct * 5 + k:ct * 5 + k + 1], acc[i][:], ALU.mult, ALU.add,
                                  [("raw", i), "acc", "scw"], ["acc"])
                        if ct < 8:
                            dst, dk = xc[i][:], "xc"
                        elif ct < 10:
                            dst, dk = BT[:, ct - 8, :], ("BT", ct - 8)
                        else:
                            dst, dk = CT[:, ct - 10, :], ("CT", ct - 10)
                        P.act(dst, acc[i][:], AF.Silu, ["acc", "scw"], [dk], bias=scw[:, 60 + ct:61 + ct])
                        P.ms("pool", dst[:, 0:PAD], 0.0, [dk])
                        if ct < 10:
                            for t0 in range(0, NT, 4):
                                n = min(4, NT - t0)
                                bk = 4 + (t0 // 4) % 4
                                bkb = banks[bk][:].bitcast(BF16)
                                for q in range(n):
                                    P.tr(bkb[:, q * 128:(q + 1) * 128], dst[:, (t0 + q) * 128:(t0 + q + 1) * 128], identb, [dk, "cstb"], [("bk", bk)])
                                if ct < 8:
                                    dd = xtok[:, t0:t0 + n, ct * 128:(ct + 1) * 128]
                                    dkk = "xtok"
                                else:
                                    dd = Btok[:, t0:t0 + n, (ct - 8) * 128:(ct - 7) * 128]
                                    dkk = "Btok"
                                P.cp("dve", dd, bkb[:, 0:n * 128].rearrange("p (a c) -> p a c", a=n), [("bk", bk)], [dkk])
                    P.flush()
                with ExitStack() as p2:
                    wdt = p2.enter_context(sbt(nc, "wdt", [128, 8, 128], BF16))
                    dtT = p2.enter_context(sbt(nc, "dtT", [64, LP], F32))
                    dtaT = p2.enter_context(sbt(nc, "dtaT", [64, LP], F32))
                    acT = p2.enter_context(sbt(nc, "acT", [64, LP], F32))
                    rmf = p2.enter_context(sbt(nc, "rmf", [64, LP], F32))
                    rmb = p2.enter_context(sbt(nc, "rmb", [64, LP], F32))
                    P.dma("sp", wdt[:], fm_tile(env, l, T_DT, 1)[:, 0], w=["wdt"])
                    for bi, (c0, bw) in enumerate(BLOCKS):
                        bk = bi % 4
                        for kt in range(8):
                            P.mm(banks[bk][0:64, 0:bw], wdt[:, kt, 0:64], hn[:, kt, c0:c0 + bw], kt == 0, kt == 7, ["wdt"], [("bk", bk)])
                        P.act(dtT[:, c0:c0 + bw], banks[bk][0:64, 0:bw], AF.Exp, [("bk", bk)], ["dtT"], bias=sal[:, 1:2])
                        P.act(dtT[:, c0:c0 + bw], dtT[:, c0:c0 + bw], AF.Ln, ["dtT"], ["dtT"], bias=1.0)
                    P.ms("dve", dtT[:, 0:PAD], 0.0, ["dtT"])
                    P.ts("dve", dtaT[:], dtT[:], nega[:, 0:1], None, ALU.mult, None, ["dtT"], ["dtaT"])
                    P.ms("pool", rmf[:], 1.0, ["rmf"])
                    P.ms("pool", rmf[:, 0:LP:128], 0.0, ["rmf"])
                    P.ms("pool", rmb[:], 1.0, ["rmb"])
                    P.ms("pool", rmb[:, 127:LP:128], 0.0, ["rmb"])
                    P.scan(acT[0:32, :], rmf[0:32, :], dtaT[0:32, :], 0.0, ALU.mult, ALU.add, ["rmf", "dtaT"], ["acT0"])
                    P.scan(acT[32:64, ::-1], rmb[32:64, ::-1], dtaT[32:64, ::-1], 0.0, ALU.mult, ALU.add, ["rmb", "dtaT"], ["acT1"])
                    P.cp("dve", ahl[:, 0, :], acT[:], ["acT0", "acT1"], ["ahl0"])
                    P.tt("dve", ahl[:, 1, :], acT[:], ahl[:, 0, :], ALU.subtract, ["acT0", "acT1", "ahl0"], ["ahl1"])
                    for tt_ in range(NT):
                        bk = 4 + tt_ % 4
                        P.tr(banks[bk][:, 0:64], acT[:, tt_ * 128:(tt_ + 1) * 128], identf[0:64, 0:64], ["acT0", "acT1", "cst"], [("bk", bk)])
                        P.tr(banks[bk][:, 64:128], dtT[:, tt_ * 128:(tt_ + 1) * 128], identf[0:64, 0:64], ["dtT", "cst"], [("bk", bk)])
                        P.cp("act", atok[:, tt_, 0, :], banks[bk][:, 0:64], [("bk", bk)], ["atok0"])
                        P.cp("act", atok[:, tt_, 3, :], banks[bk][:, 64:128], [("bk", bk)], ["atok3"])
                    P.ts("dve", atok[:, :, 1, :], atok[:, :, 0, :], -1.0, None, ALU.mult, None, ["atok0"], ["atok1"])
                    P.act(atok[:, :, 2, :], atok[:, :, 0, :], AF.Exp, ["atok0"], ["atok2"])
                    P.flush()
                with ExitStack() as p3:
                    S = p3.enter_context(sbt(nc, "S", [128, 1024], F32))
                    Sb = p3.enter_context(sbt(nc, "Sb", [128, 1024], BF16))
                    xdt = p3.enter_context(sbt(nc, "xdt", [128, 1024], BF16))
                    xdtt = p3.enter_context(sbt(nc, "xdtt", [128, 1024], BF16))
                    E = [p3.enter_context(sbt(nc, "E%d" % i, [128, 128], BF16)) for i in range(2)]
                    M = [p3.enter_context(sbt(nc, "M%d" % i, [128, 128], BF16)) for i in range(2)]
                    tl = p3.enter_context(sbt(nc, "tl", [128, 2, 16], F32))
                    tY = p3.enter_context(sbt(nc, "tY", [128, 1024], F32))
                    yv = p3.enter_context(sbt(nc, "yv", [128, 1024], F32))
                    yfb = p3.enter_context(sbt(nc, "yfb", [128, 1024], BF16))
                    zs = p3.enter_context(sbt(nc, "zs", [128, 1024], F32))
                    junk = p3.enter_context(sbt(nc, "junk", [128, 512], F32))
                    ss = p3.enter_context(sbt(nc, "ss", [128, 2], F32))
                    yn = p3.enter_context(sbt(nc, "yn", [128, 1024], BF16))
                    yTs = p3.enter_context(sbt(nc, "yTs", [128, 8, 128], BF16))
                    h3 = lambda ap, nh: ap.rearrange("p (h c) -> p h c", h=nh)
                    bc = lambda ap, nh: ap.unsqueeze(2).broadcast_to([128, nh, 64])
                    for d in range(2):
                        rb = d * 32
                        P.ms("pool", S[:], 0.0, ["S0", "S1"])
                        P.ms("pool", Sb[:], 0.0, ["Sb0", "Sb1"])
                        order = list(range(NT)) if d == 0 else list(range(NT - 1, -1, -1))
                        for c in order:
                            cs, ce = c * 128, (c + 1) * 128
                            for g in range(2):
                                P.mm(banks[0][:, g * 128:(g + 1) * 128], BT[:, g, cs:ce], CT[:, g, cs:ce], True, True, [], [("bk", 0)])
                            P.mm(banks[0][:, 256:272], lastm[:, d, :], atok[:, c, 0, rb:rb + 16], True, True, [], [("bk", 0)])
                            P.tt("dve", tl[:, 0, :], banks[0][:, 256:272], atok[:, c, 0, rb:rb + 16], ALU.subtract, [("bk", 0)], ["tl0"])
                            P.act(tl[:, 0, :], tl[:, 0, :], AF.Exp, ["tl0"], ["tl0"])
                            P.act(tl[:, 1, :], banks[0][:, 256:272], AF.Exp, [("bk", 0)], ["tl1"])
                            P.tt("dve", h3(xdt[:], 16), h3(xtok[:, c, :], 16), bc(atok[:, c, 3, rb:rb + 16], 16), ALU.mult, [], ["xdt"])
                            for g in range(2):
                                P.mm(banks[4 + g][:, :], CT[:, g, cs:ce], Sb[:, g * 512:(g + 1) * 512], True, True, ["Sb%d" % g], [("bk", 4 + g)])

                            def dp_mm(h):
                                e = h % 2
                                dpb = 1 if e == 0 else 6
                                dp = banks[dpb][:, 0:128]
                                P.mm(dp, selb[:, d * 16 + h, :], ahl[:, 0, cs:ce], True, False, [], [("bk", dpb)])
                                P.mm(dp, selb[:, d * 16 + h, :], ahl[:, 1, cs:ce], False, False, [], [("bk", dpb)])
                                P.mm(dp, identb, negm[:, d, :], False, True, [], [("bk", dpb)])

                            dp_mm(0)
                            dp_mm(1)
                            for h in range(16):
                                g, e = h // 8, h % 2
                                dpb = 1 if e == 0 else 6
                                P.act(E[e][:], banks[dpb][:, 0:128], AF.Exp, [("bk", dpb)], [("E", e)], bias=atok[:, c, 1, rb + h:rb + h + 1])
                                P.tt("dve", M[e][:], E[e][:], banks[0][:, g * 128:(g + 1) * 128], ALU.mult, [("E", e), ("bk", 0)], [("M", e)])
                                if h + 2 < 16:
                                    dp_mm(h + 2)
                                P.mm(banks[2 + g][:, (h % 8) * 64:(h % 8 + 1) * 64], M[e][:], xdt[:, h * 64:(h + 1) * 64], True, True,
                                     [("M", e), "xdt"], [("bk", 2 + g)])
                            P.tt("dve", h3(xdtt[:], 16), h3(xdt[:], 16), bc(tl[:, 0, :], 16), ALU.mult, ["xdt", "tl0"], ["xdtt"])
                            ea = atok[:, c, 2, rb:rb + 16]
                            for g in range(2):
                                gs = slice(g * 512, (g + 1) * 512)
                                P.tt("dve", h3(tY[:, gs], 8), h3(banks[4 + g][:, :], 8), bc(ea[:, g * 8:(g + 1) * 8], 8), ALU.mult,
                                     [("bk", 4 + g)], [("tY", g)])
                                if d == 0:
                                    P.tt("dve", yfb[:, gs], tY[:, gs], banks[2 + g][:, :], ALU.add, [("tY", g), ("bk", 2 + g)], ["yfb"])
                                else:
                                    P.tt("dve", yv[:, gs], tY[:, gs], banks[2 + g][:, :], ALU.add, [("tY", g), ("bk", 2 + g)], [("yv", g)])
                            if d == 0:
                                P.dma("sp", env["yfD"][s, c], yfb[:], r=["yfb"])
                            for g in range(2):
                                gs = slice(g * 512, (g + 1) * 512)
                                P.mm(banks[4 + g][:, :], Btok[:, c, g * 128:(g + 1) * 128], xdtt[:, gs], True, True, ["xdtt"], [("bk", 4 + g)])
                                P.tt("dve", h3(S[:, gs], 8), h3(S[:, gs], 8), bc(tl[:, 1, g * 8:(g + 1) * 8], 8), ALU.mult, ["S%d" % g, "tl1"], ["S%d" % g])
                                P.tt("dve", S[:, gs], S[:, gs], banks[4 + g][:, :], ALU.add, ["S%d" % g, ("bk", 4 + g)], ["S%d" % g])
                                P.cp("pool", Sb[:, gs], S[:, gs], ["S%d" % g], ["Sb%d" % g])
                            if d == 1:
                                P.dma("sp", yfb[:], env["yfD"][s, c], w=["yfb"])
                                P.tt("dve", yv[:], yv[:], yfb[:], ALU.add, [("yv", 0), ("yv", 1), "yfb"], ["yvv"])
                                P.tt("dve", h3(tY[:], 16), h3(xtok[:, c, :], 16), bc(sdn[:, 0:16], 16), ALU.mult, [("tY", 0), ("tY", 1)], ["tYd"])
                                P.tt("dve", yv[:], yv[:], tY[:], ALU.add, ["yvv", "tYd"], ["yvv"])
                                for b in range(2):
                                    for kt in range(8):
                                        P.mm(banks[6 + b][:, :], hn[:, kt, cs:ce], wz[:, kt, b * 512:(b + 1) * 512], kt == 0, kt == 7, [], [("bk", 6 + b)])
                                    P.act(zs[:, b * 512:(b + 1) * 512], banks[6 + b][:, :], AF.Silu, [("bk", 6 + b)], [("zs", b)])
                                P.tt("dve", yv[:], yv[:], zs[:], ALU.mult, ["yvv", ("zs", 0), ("zs", 1)], ["yvv"])
                                for g in range(2):
                                    P.act(junk[:], yv[:, g * 512:(g + 1) * 512], AF.Square, ["yvv"], ["junk"], accum_out=ss[:, g:g + 1])
                                P.act(ss[:], ss[:], AF.Sqrt, ["junk"], ["ss"], bias=EPS, scale=1.0 / 512)
                                P.recip(ss[:], ss[:], ["ss"], ["ss"])
                                for g in range(2):
                                    gs = slice(g * 512, (g + 1) * 512)
                                    P.stt(yn[:, gs], yv[:, gs], ss[:, g:g + 1], sdn[:, 16 + g * 512:16 + (g + 1) * 512], ALU.mult, ALU.mult,
                                          ["yvv", "ss"], ["yn"])
                                bkb = banks[6][:].bitcast(BF16)
                                for q in range(8):
                                    P.tr(bkb[:, q * 128:(q + 1) * 128], yn[:, q * 128:(q + 1) * 128], identb, ["yn"], [("bk", 6)])
                                P.cp("act", yTs[:].rearrange("p a c -> p (a c)"), bkb[:, :], [("bk", 6)], ["yTs"])
                                P.dma("sp", env["yssdT"][s, :, :, cs:ce], yTs[:], r=["yTs"])
                    P.flush()


def phase_gdn(env, l):
    nc, P, banks = env["nc"], env["P"], env["banks"]
    small, K2 = env["small_in"], env["const2"]
    identf, identb, onesb = env["identf"], env["identb"], env["onesb"]
    with ExitStack() as lay:
        gcw = lay.enter_context(sbt(nc, "gcw", [128, 60], F32))
        gal = lay.enter_context(sbt(nc, "gal", [64, 2], F32))
        negg = lay.enter_context(sbt(nc, "negg", [64, 1], F32))
        gnw = lay.enter_context(sbt(nc, "gnw", [128, 128], F32))
        wzg = lay.enter_context(sbt(nc, "wzg", [128, 8, 512], BF16))
        selg = lay.enter_context(sbt(nc, "selg", [64, 8, 128], F32))
        selgb = lay.enter_context(sbt(nc, "selgb", [64, 8, 128], BF16))
        gmk = lay.enter_context(sbt(nc, "gmk", [128, 4, 128], BF16))
        glast = lay.enter_context(sbt(nc, "glast", [128, 2, 128], F32))
        gch = lay.enter_context(sbt(nc, "gch", [128, 4, 128], F32))
        with ExitStack() as p0:
            stg = p0.enter_context(sbt(nc, "gstg", [128, 512], F32))
            P.dma("sp", gcw[:], small[l, :, S_GCW:S_GCW + 60], w=["gcw"])
            P.dma("sp", gal[:], small[l, 0:64, S_GALOG:S_GALOG + 2], w=["gal"])
            P.dma("sp", gnw[:], small[l, :, S_GNW:S_GNW + 128], w=["gnw"])
            P.dma("sp", wzg[:], env["wallb"][l, :, W_ZGDN:W_ZGDN + 4096].rearrange("p (k c) -> p k c", k=8), w=["wzg"])
            P.dma("sp", selg[:].rearrange("p a c -> p (a c)"), K2[0:64, K_SELG:K_SELG + 1024], w=["selg"])
            P.dma("sp", stg[:], K2[:, K_GNEGF:K_GNEGF + 512], w=["gstg"])
            P.dma("sp", glast[:].rearrange("p a c -> p (a c)"), K2[:, K_GLASTF:K_GLASTF + 256], w=["glast"])
            P.dma("sp", gch[:].rearrange("p a c -> p (a c)"), K2[:, K_GCHF:K_GCHF + 512], w=["gch"])
            P.cp("dve", selgb[:], selg[:], ["selg"], ["selgb"])
            P.cp("dve", gmk[:].rearrange("p a c -> p (a c)"), stg[:], ["gstg"], ["gmk"])
            P.act(negg[:], gal[:, 0:1], AF.Exp, ["gal"], ["negg"])
            P.ts("dve", negg[:], negg[:], -1.0, None, ALU.mult, None, ["negg"], ["negg"])
            P.flush()

        for s in range(env["nseq"]):
            with ExitStack() as ph:
                hn = ph.enter_context(sbt(nc, "ghn", [128, 8, LP], BF16))
                qkv = ph.enter_context(sbt(nc, "qkv", [128, 12, LP], BF16))
                gtok = ph.enter_context(sbt(nc, "gtok", [128, NT, 4, 64], F32))
                ghl = ph.enter_context(sbt(nc, "ghl", [64, 2, LP], BF16))
                gcT = ph.enter_context(sbt(nc, "gcT", [64, LP], F32))
                betaT = ph.enter_context(sbt(nc, "betaT", [64, LP], F32))
                betab = ph.enter_context(sbt(nc, "betab", [64, LP], BF16))
                for kt in range(8):
                    P.dma("sp", hn[:, kt, :], env["hnT"][s, :, kt, :], w=[("hn", kt)])
                hnk = [("hn", kt) for kt in range(8)]
                with ExitStack() as p1:
                    wcv = p1.enter_context(sbt(nc, "gwcv", [128, 12, 8, 128], BF16))
                    raw0 = p1.enter_context(sbt(nc, "graw", [128, LP + 4], F32))
                    raw = [raw0, raw0]
                    acc = p1.enter_context(sbt(nc, "gacc", [128, LP], F32))
                    sq = p1.enter_context(sbt(nc, "gsq", [128, LP], BF16))
                    rs = p1.enter_context(sbt(nc, "grs", [128, 512], F32))
                    P.dma("sp", wcv[:], fm_tile(env, l, T_Q, 12), w=["wcv"])
                    for i in range(2):
                        P.ms("pool", raw[i][:, 0:2], 0.0, ["raw"])
                        P.ms("pool", raw[i][:, LP + 2:LP + 4], 0.0, ["raw"])
                    nb_ = 0
                    for ct in range(12):
                        i = ct % 2
                        for (c0, bw) in BLOCKS:
                            bk = nb_ % 4
                            nb_ += 1
                            for kt in range(8):
                                P.mm(banks[bk][:, 0:bw], wcv[:, ct, kt, :], hn[:, kt, c0:c0 + bw], kt == 0, kt == 7, ["wcv"] + hnk, [("bk", bk)])
                            P.cp("act", raw[i][:, 2 + c0:2 + c0 + bw], banks[bk][:, 0:bw], [("bk", bk)], ["raw"])
                        P.ts("dve", acc[:], raw[i][:, 0:LP], gcw[:, ct * 5:ct * 5 + 1], None, ALU.mult, None, ["raw", "gcw"], ["acc"])
                        for k in range(1, 5):
                            P.stt(acc[:], raw[i][:, k:k + LP], gcw[:, ct * 5 + k:ct * 5 + k + 1], acc[:], ALU.mult, ALU.add,
                                  ["raw", "acc", "gcw"], ["acc"])
                        dst, dk = qkv[:, ct, :], ("qkv", ct)
                        P.act(dst, acc[:], AF.Silu, ["acc"], [dk])
                        P.ms("pool", dst[:, 0:PAD], 0.0, [dk])
                        if ct < 8:
                            P.tt("pool", sq[:], dst, dst, ALU.mult, [dk], ["sq"])
                            for bi, (c0, bw) in enumerate(BLOCKS):
                                bk = 4 + bi % 4
                                P.mm(banks[bk][:, 0:bw], onesb, sq[:, c0:c0 + bw], True, True, ["sq", "cstb"], [("bk", bk)])
                                P.act(rs[:, 0:bw], banks[bk][:, 0:bw], AF.Sqrt, [("bk", bk)], ["rs"], bias=EPS)
                                P.recip(rs[:, 0:bw], rs[:, 0:bw], ["rs"], ["rs"])
                                P.stt(dst[:, c0:c0 + bw], dst[:, c0:c0 + bw], float(128 ** -0.5) if ct < 4 else 1.0, rs[:, 0:bw], ALU.mult, ALU.mult,
                                      [dk, "rs"], [dk])
                    P.flush()
                if DBG.get("gdn_stop") == 1:
                    continue
                with ExitStack() as p2:
                    wba = p2.enter_context(sbt(nc, "wba", [128, 2, 8, 128], BF16))
                    gT = p2.enter_context(sbt(nc, "gT", [64, LP], F32))
                    rmf = p2.enter_context(sbt(nc, "grmf", [64, LP], F32))
                    rmb = p2.enter_context(sbt(nc, "grmb", [64, LP], F32))
                    P.dma("sp", wba[:], fm_tile(env, l, T_BETA, 2), w=["wba"])
                    for bi, (c0, bw) in enumerate(BLOCKS):
                        b0, b1 = bi % 2, 2 + bi % 2
                        for kt in range(8):
                            P.mm(banks[b0][0:64, 0:bw], wba[:, 0, kt, 0:64], hn[:, kt, c0:c0 + bw], kt == 0, kt == 7, ["wba"], [("bk", b0)])
                        P.act(betaT[:, c0:c0 + bw], banks[b0][0:64, 0:bw], AF.Sigmoid, [("bk", b0)], ["betaT"])
                        for kt in range(8):
                            P.mm(banks[b1][0:64, 0:bw], wba[:, 1, kt, 0:64], hn[:, kt, c0:c0 + bw], kt == 0, kt == 7, ["wba"], [("bk", b1)])
                        P.act(gT[:, c0:c0 + bw], banks[b1][0:64, 0:bw], AF.Exp, [("bk", b1)], ["gT"], bias=gal[:, 1:2])
                        P.act(gT[:, c0:c0 + bw], gT[:, c0:c0 + bw], AF.Ln, ["gT"], ["gT"], bias=1.0)
                    P.ms("dve", betaT[:, 0:PAD], 0.0, ["betaT"])
                    P.cp("dve", betab[:], betaT[:], ["betaT"], ["betab"])
                    P.ms("dve", gT[:, 0:PAD], 0.0, ["gT"])
                    P.ts("dve", gT[:], gT[:], negg[:, 0:1], None, ALU.mult, None, ["gT"], ["gT"])
                    P.ms("pool", rmf[:], 1.0, ["rmf"])
                    P.ms("pool", rmf[:, 0:LP:64], 0.0, ["rmf"])
                    P.ms("pool", rmb[:], 1.0, ["rmb"])
                    P.ms("pool", rmb[:, 63:LP:64], 0.0, ["rmb"])
                    P.scan(gcT[0:32, :], rmf[0:32, :], gT[0:32, :], 0.0, ALU.mult, ALU.add, ["rmf", "gT"], ["gcT0"])
                    P.scan(gcT[32:64, ::-1], rmb[32:64, ::-1], gT[32:64, ::-1], 0.0, ALU.mult, ALU.add, ["rmb", "gT"], ["gcT1"])
                    P.cp("dve", ghl[:, 0, :], gcT[:], ["gcT0", "gcT1"], ["ghl0"])
                    P.tt("dve", ghl[:, 1, :], gcT[:], ghl[:, 0, :], ALU.subtract, ["gcT0", "gcT1", "ghl0"], ["ghl1"])
                    for tt_ in range(NT):
                        bk = 4 + tt_ % 4
                        P.tr(banks[bk][:, 0:64], gcT[:, tt_ * 128:(tt_ + 1) * 128], identf[0:64, 0:64], ["gcT0", "gcT1"], [("bk", bk)])
                        P.tr(banks[bk][:, 64:128], betaT[:, tt_ * 128:(tt_ + 1) * 128], identf[0:64, 0:64], ["betaT"], [("bk", bk)])
                        P.cp("act", gtok[:, tt_, 0:2, :], banks[bk][:, 0:128].rearrange("p (a c) -> p a c", a=2), [("bk", bk)], ["gtok01"])
                    P.act(gtok[:, :, 2, :], gtok[:, :, 0, :], AF.Exp, ["gtok01"], ["gtok2"])
                    P.tt("dve", gtok[:, :, 2, :], gtok[:, :, 2, :], gtok[:, :, 1, :], ALU.mult, ["gtok2", "gtok01"], ["gtok2"])
                    P.ts("dve", gtok[:, :, 3, :], gtok[:, :, 0, :], -1.0, None, ALU.mult, None, ["gtok01"], ["gtok3"])
                    P.flush()
                if DBG.get("gdn_stop") == 2:
                    continue
                with ExitStack() as p3:
                    obuf = p3.enter_context(sbt(nc, "obuf", [128, NT, 512], BF16))
                    S = p3.enter_context(sbt(nc, "gS", [128, 4, 128], F32))
                    Sb = p3.enter_context(sbt(nc, "gSb", [128, 4, 128], BF16))
                    ekt = p3.enter_context(sbt(nc, "ekt", [128, 4], F32))
                    dec = p3.enter_context(sbt(nc, "dec", [128, 8], F32))
                    osum = p3.enter_context(sbt(nc, "osum", [128, 4, 128], F32))
                    zs = p3.enter_context(sbt(nc, "gzs", [128, 512], F32))
                    junk = p3.enter_context(sbt(nc, "gjunk", [128, 128], F32))
                    ss = p3.enter_context(sbt(nc, "gss", [128, 4], F32))
                    yn = p3.enter_context(sbt(nc, "gyn", [128, 512], BF16))
                    yTs = p3.enter_context(sbt(nc, "gyTs", [128, 4, 128], BF16))
                    T_ = []
                    if DBG.get("gdn_padalloc"):
                        padt = p3.enter_context(sbt(nc, "gpad", [128, DBG["gdn_padalloc"]], F32))
                    for h in range(4):
                        t = {}
                        for nm, dt_ in (("egb", F32), ("kbT", BF16), ("qgT", BF16), ("vb", BF16), ("kbeg", BF16), ("ktok", BF16),
                                        ("Ei", BF16), ("attnT", BF16), ("Es", BF16), ("N0", BF16), ("N1", BF16), ("NT0", BF16), ("NT1", BF16),
                                        ("R32", F32), ("Rb", BF16), ("u32", F32), ("wT", BF16), ("vnew0", BF16), ("vnew1", BF16), ("av", F32)):
                            t[nm] = p3.enter_context(sbt(nc, "g%s%d" % (nm, h), [128, 128], dt_))
                            if DBG.get("addr"):
                                print("ALLOC", nm, h, t[nm], nc.sbuf_bytes_remaining)
                        P.ms("pool", t["vnew0"][:], 0.0, [("vnew0", h)])
                        P.ms("pool", t["vnew1"][:], 0.0, [("vnew1", h)])
                        T_.append(t)

                    def R(h, i):
                        return banks[2 * h + i // 4][:, (i % 4) * 128:(i % 4 + 1) * 128]

                    def Rb16(h, i):
                        return banks[2 * h + i // 4][:].bitcast(BF16)[:, (i % 4) * 256:(i % 4 + 1) * 256]

                    def chain(d, c, h):
                        t = T_[h]
                        rb = d * 32
                        hrow = rb + h
                        cs, ce = c * 128, (c + 1) * 128
                        rk = lambda i: ("bk", 2 * h + i // 4)
                        tk = lambda nm: (nm, h)
                        qT, kT, vT = qkv[:, h, cs:ce], qkv[:, 4 + h, cs:ce], qkv[:, 8 + h, cs:ce]
                        P.mm(R(h, 0), selgb[:, d * 4 + h, :], betab[:, cs:ce], True, True, [], [rk(0)])
                        P.mm(R(h, 1), selgb[:, d * 4 + h, :], ghl[:, 0, cs:ce], True, False, [], [rk(1)])
                        P.mm(R(h, 1), selgb[:, d * 4 + h, :], ghl[:, 1, cs:ce], False, True, [], [rk(1)])
                        r2b = Rb16(h, 2)
                        P.tr(r2b[:, 0:128], kT, identb, [], [rk(2)])
                        P.tr(r2b[:, 128:256], vT, identb, [], [rk(2)])
                        yield
                        nsub = DBG.get("gdn_sub", 6)
                        if nsub > 0:
                            P.act(t["egb"][:], R(h, 1), AF.Exp, [rk(1)], [tk("egb")])
                        if nsub > 1:
                            P.cp("act", t["av"][:], R(h, 0), [rk(0)], [tk("av")])
                            P.tt("dve", t["kbT"][:], kT, t["av"][:], ALU.mult, [tk("av")], [tk("kbT")])
                        if nsub > 2:
                            P.tt("dve", t["qgT"][:], qT, t["egb"][:], ALU.mult, [tk("egb")], [tk("qgT")])
                        if nsub > 3:
                            P.ts("dve", t["vb"][:], r2b[:, 128:256], gtok[:, c, 1, hrow:hrow + 1], None, ALU.mult, None, [rk(2)], [tk("vb")])
                        if nsub > 4:
                            P.ts("dve", t["kbeg"][:], r2b[:, 0:128], gtok[:, c, 2, hrow:hrow + 1], None, ALU.mult, None, [rk(2)], [tk("kbeg")])
                        if nsub > 5:
                            P.ts("dve", t["ktok"][:], r2b[:, 0:128], ekt[:, h:h + 1], None, ALU.mult, None, [rk(2), "ekt"], [tk("ktok")])
                        yield
                        P.mm(R(h, 3), kT, t["kbT"][:], True, True, [tk("kbT")], [rk(3)])
                        P.mm(R(h, 4), kT, qT, True, True, [], [rk(4)])
                        P.mm(R(h, 5), selgb[:, d * 4 + h, :], ghl[:, 0, cs:ce], True, False, [], [rk(5)])
                        P.mm(R(h, 5), selgb[:, d * 4 + h, :], ghl[:, 1, cs:ce], False, False, [], [rk(5)])
                        P.mm(R(h, 5), identb, gmk[:, d, :], False, True, [], [rk(5)])
                        yield
                        P.act(t["Ei"][:], R(h, 5), AF.Exp, [rk(5)], [tk("Ei")], bias=gtok[:, c, 3, hrow:hrow + 1])
                        P.tt("dve", t["attnT"][:], R(h, 4), t["Ei"][:], ALU.mult, [rk(4), tk("Ei")], [tk("attnT")])
                        P.tt("pool", t["Es"][:], t["Ei"][:], gmk[:, 2 + d, :], ALU.mult, [tk("Ei")], [tk("Es")])
                        P.stt(t["N0"][:], R(h, 3), -1.0, t["Es"][:], ALU.mult, ALU.mult, [rk(3), tk("Es")], [tk("N0")])
                        yield
                        r0b = Rb16(h, 0)
                        P.tr(r0b[:, 0:128], t["N0"][:], identb, [tk("N0")], [rk(0)])
                        P.tt("dve", t["R32"][:], t["N0"][:], identf, ALU.add, [tk("N0")], [tk("R32")])
                        P.cp("pool", t["Rb"][:], t["R32"][:], [tk("R32")], [tk("Rb")])
                        yield
                        P.cp("act", t["NT0"][:], r0b[:, 0:128], [rk(0)], [tk("NT0")])
                        yield
                        for lvl in range(1, 6):
                            a, b = (lvl - 1) % 2, lvl % 2
                            Na, NTa, Nb, NTb = t["N%d" % a], t["NT%d" % a], t["N%d" % b], t["NT%d" % b]
                            if lvl < 5:
                                P.mm(R(h, 1), NTa[:], Na[:], True, True, [tk("N%d" % a), tk("NT%d" % a)], [rk(1)])
                            P.mm(R(h, 2), Na[:], NTa[:], True, True, [tk("N%d" % a), tk("NT%d" % a)], [rk(2)])
                            yield
                            if lvl < 5:
                                P.cp("act", Nb[:], R(h, 1), [rk(1)], [tk("N%d" % b)])
                            P.cp("dve", NTb[:], R(h, 2), [rk(2)], [tk("NT%d" % b)])
                            yield
                            P.mm(R(h, 6), NTb[:], t["Rb"][:], True, True, [tk("NT%d" % b), tk("Rb")], [rk(6)])
                            yield
                            P.tt("dve", t["R32"][:], t["R32"][:], R(h, 6), ALU.add, [rk(6), tk("R32")], [tk("R32")])
                            P.cp("act", t["Rb"][:], t["R32"][:], [tk("R32")], [tk("Rb")])
                            yield
                        P.mm(R(h, 7), t["Rb"][:], t["vb"][:], True, True, [tk("Rb"), tk("vb")], [rk(7)])
                        P.mm(R(h, 0), t["kbeg"][:], t["Rb"][:], True, True, [tk("Rb"), tk("kbeg")], [rk(0)])
                        yield
                        P.cp("act", t["u32"][:], R(h, 7), [rk(7)], [tk("u32")])
                        P.cp("dve", t["wT"][:], R(h, 0), [rk(0)], [tk("wT")])
                        yield
                        for sc in ((0, 1) if d == 0 else (1, 0)):
                            rows = slice(64 * sc, 64 * sc + 64)
                            P.mm(R(h, 1), t["wT"][:], Sb[:, h, :], True, True, [tk("wT"), ("Sb", h)], [rk(1)])
                            P.mm(R(h, 2 + sc), t["qgT"][:], Sb[:, h, :], True, True, [tk("qgT"), ("Sb", h)], [rk(2 + sc)])
                            yield
                            vn = "vnew%d" % sc
                            P.tt("dve", t[vn][rows, :], t["u32"][rows, :], R(h, 1)[rows, :], ALU.subtract, [tk("u32"), rk(1)], [tk(vn)])
                            yield
                            P.mm(R(h, 4), t["ktok"][:], t[vn][:], True, True, [tk("ktok"), tk(vn)], [rk(4)])
                            yield
                            P.stt(S[:, h, :], S[:, h, :], dec[:, sc * 4 + h:sc * 4 + h + 1], R(h, 4), ALU.mult, ALU.add, [rk(4), ("S", h), "dec"], [("S", h)])
                            P.cp("act", Sb[:, h, :], S[:, h, :], [("S", h)], [("Sb", h)])
                            yield
                        P.mm(R(h, 5), t["attnT"][:], t["vnew0"][:], True, False, [tk("attnT"), tk("vnew0")], [rk(5)])
                        P.mm(R(h, 5), t["attnT"][:], t["vnew1"][:], False, True, [tk("attnT"), tk("vnew1")], [rk(5)])
                        yield
                        P.cp("act", t["av"][:], R(h, 5), [rk(5)], [tk("av")])
                        for sc in range(2):
                            rows = slice(64 * sc, 64 * sc + 64)
                            if d == 0:
                                P.tt("dve", obuf[rows, c, h * 128:(h + 1) * 128], R(h, 2 + sc)[rows, :], t["av"][rows, :], ALU.add,
                                     [rk(2 + sc), tk("av")], [("obuf", h)])
                            else:
                                P.tt("dve", osum[rows, h, :], R(h, 2 + sc)[rows, :], t["av"][rows, :], ALU.add, [rk(2 + sc), tk("av")], [("osum", h)])
                        if d == 1:
                            P.tt("dve", osum[:, h, :], osum[:, h, :], obuf[:, c, h * 128:(h + 1) * 128], ALU.add, [("osum", h)], [("osum", h)])
                        yield

                    for d in range(2):
                        rb = d * 32
                        P.ms("pool", S[:], 0.0, [("S", h) for h in range(4)])
                        P.ms("pool", Sb[:], 0.0, [("Sb", h) for h in range(4)])
                        order = list(range(NT)) if d == 0 else list(range(NT - 1, -1, -1))
                        for c in order[:DBG.get("gdn_tiles", NT)]:
                            cs, ce = c * 128, (c + 1) * 128
                            r06 = R(0, 6)
                            P.mm(r06[:, 0:16], glast[:, d, :], gtok[:, c, 0, rb:rb + 16], True, True, [], [("bk", 1)])
                            for sc in range(2):
                                P.mm(r06[:, 16 + 16 * sc:32 + 16 * sc], gch[:, d * 2 + sc, :], gtok[:, c, 0, rb:rb + 16], True, True, [], [("bk", 1)])
                            P.tt("dve", ekt[:], r06[:, 0:4], gtok[:, c, 0, rb:rb + 4], ALU.subtract, [("bk", 1)], ["ekt"])
                            P.act(ekt[:], ekt[:], AF.Exp, ["ekt"], ["ekt"])
                            P.act(dec[:].rearrange("p (a c) -> p a c", a=2), r06[:, 16:48].rearrange("p (a c) -> p a c", a=2)[:, :, 0:4], AF.Exp, [("bk", 1)], ["dec"])
                            gens = [chain(d, c, h) for h in range(DBG.get("gdn_heads", 4))]
                            nst = 0
                            while gens and nst < DBG.get("gdn_stage", 10 ** 9):
                                nst += 1
                                for g in list(gens):
                                    try:
                                        next(g)
                                    except StopIteration:
                                        gens.remove(g)
                            if d == 1 and not DBG.get("gdn_noepi"):
                                r0k = [("bk", 0)]
                                for kt in range(8):
                                    P.mm(banks[0][:, :], hn[:, kt, cs:ce], wzg[:, kt, :], kt == 0, kt == 7, [], r0k)
                                P.act(zs[:], banks[0][:, :], AF.Silu, r0k, ["zs"])
                                for h in range(4):
                                    P.act(junk[:], osum[:, h, :], AF.Square, [("osum", h)], ["junk"], accum_out=ss[:, h:h + 1])
                                P.act(ss[:], ss[:], AF.Sqrt, ["junk"], ["ss"], bias=EPS, scale=1.0 / 128)
                                P.recip(ss[:], ss[:], ["ss"], ["ss"])
                                for h in range(4):
                                    P.stt(osum[:, h, :], osum[:, h, :], ss[:, h:h + 1], gnw[:], ALU.mult, ALU.mult, [("osum", h), "ss"], [("osum", h)])
                                P.tt("dve", yn[:], osum[:].rearrange("p a c -> p (a c)"), zs[:], ALU.mult, [("osum", h) for h in range(4)] + ["zs"], ["yn"])
                                bkb = banks[2][:].bitcast(BF16)
                                r1k = [("bk", 2)]
                                for q in range(4):
                                    P.tr(bkb[:, q * 128:(q + 1) * 128], yn[:, q * 128:(q + 1) * 128], identb, ["yn"], r1k)
                                P.cp("act", yTs[:].rearrange("p a c -> p (a c)"), bkb[:, 0:512], r1k, ["yTs"])
                                P.dma("sp", env["ygdnT"][s, :, :, cs:ce], yTs[:], r=["yTs"])
                    P.flush()


def phase_merge(env, l):
    nc, P, banks = env["nc"], env["P"], env["banks"]
    wallb = env["wallb"]
    with ExitStack() as lay:
        wg = lay.enter_context(sbt(nc, "wg", [128, 24, 8, 128], BF16))
        wo5 = lay.enter_context(sbt(nc, "wo5", [128, 4, 1024], BF16))
        wos = lay.enter_context(sbt(nc, "wos", [128, 8, 1024], BF16))
        wog = lay.enter_context(sbt(nc, "wog", [128, 4, 1024], BF16))
        wout = lay.enter_context(sbt(nc, "wout", [128, 8, 1024], BF16))
        for t0 in range(0, 24, 8):
            P.dma("sp", wg[:, t0:t0 + 8], fm_tile(env, l, T_GATE + t0, 8), w=["wg"])
        P.dma("sp", wo5[:], wallb[l, :, W_S5O:W_S5O + 4096].rearrange("p (k c) -> p k c", k=4), w=["wo5"])
        P.dma("sp", wos[:], wallb[l, :, W_SSDO:W_SSDO + 8192].rearrange("p (k c) -> p k c", k=8), w=["wos"])
        P.dma("sp", wog[:], wallb[l, :, W_GDNO:W_GDNO + 4096].rearrange("p (k c) -> p k c", k=4), w=["wog"])
        P.dma("sp", wout[:], wallb[l, :, W_OUT:W_OUT + 8192].rearrange("p (k c) -> p k c", k=8), w=["wout"])
        hnb = [lay.enter_context(sbt(nc, "mhn%d" % i, [128, 8, 512], BF16)) for i in range(2)]
        yb = [lay.enter_context(sbt(nc, "myb%d" % i, [128, 16, 512], BF16)) for i in range(2)]
        hb = [lay.enter_context(sbt(nc, "mh%d" % i, [128, 8, 512], F32)) for i in range(2)]
        mg = lay.enter_context(sbt(nc, "mg", [128, 8, 512], BF16))
        gt = [lay.enter_context(sbt(nc, "gt%d" % i, [128, 512], F32)) for i in range(2)]
        tmp = [lay.enter_context(sbt(nc, "mtmp%d" % i, [128, 512], F32)) for i in range(2)]
        msum = lay.enter_context(sbt(nc, "msum", [128, 512], F32))
        it = 0
        nb_ = 0
        for s in range(env["nseq"]):
            for (c0, bw) in BLOCKS:
                i = it % 2
                it += 1
                P.dma("sp", hnb[i][:, :, 0:bw], env["hnT"][s, :, :, c0:c0 + bw], w=[("hn", i)])
                P.dma("sp", yb[i][:, 0:4, 0:bw], env["ys5T"][s, :, :, c0:c0 + bw], w=[("y5", i)])
                P.dma("sp", yb[i][:, 4:12, 0:bw], env["yssdT"][s, :, :, c0:c0 + bw], w=[("ys", i)])
                P.dma("sp", yb[i][:, 12:16, 0:bw], env["ygdnT"][s, :, :, c0:c0 + bw], w=[("yg", i)])
                P.dma("sp", hb[i][:, :, 0:bw], env["hT"][s, :, :, c0:c0 + bw], w=[("h", i)])
                srcs = [(0, 4, wo5, ("y5", i), "wo5"), (4, 8, wos, ("ys", i), "wos"), (12, 4, wog, ("yg", i), "wog")]
                for dtl in range(8):
                    for b, (y0, nk, wsrc, yk, wk) in enumerate(srcs):
                        ba, bb = nb_ % 4, 4 + nb_ % 4
                        nb_ += 1
                        for kt in range(8):
                            P.mm(banks[ba][:, 0:bw], wg[:, b * 8 + dtl, kt, :], hnb[i][:, kt, 0:bw], kt == 0, kt == 7, ["wg", ("hn", i)], [("bk", ba)])
                        P.act(gt[b % 2][:, 0:bw], banks[ba][:, 0:bw], AF.Sigmoid, [("bk", ba)], [("gt", b % 2)])
                        for kt in range(nk):
                            P.mm(banks[bb][:, 0:bw], wsrc[:, kt, dtl * 128:(dtl + 1) * 128], yb[i][:, y0 + kt, 0:bw], kt == 0, kt == nk - 1,
                                 [wk, yk], [("bk", bb)])
                        if b == 0:
                            P.tt("dve", msum[:, 0:bw], gt[0][:, 0:bw], banks[bb][:, 0:bw], ALU.mult, [("gt", 0), ("bk", bb)], ["msum"])
                        else:
                            P.tt("dve", tmp[b % 2][:, 0:bw], gt[b % 2][:, 0:bw], banks[bb][:, 0:bw], ALU.mult, [("gt", b % 2), ("bk", bb)], [("tmp", b % 2)])
                            if b == 1:
                                P.tt("dve", msum[:, 0:bw], msum[:, 0:bw], tmp[1][:, 0:bw], ALU.add, ["msum", ("tmp", 1)], ["msum"])
                            else:
                                P.tt("dve", mg[:, dtl, 0:bw], msum[:, 0:bw], tmp[0][:, 0:bw], ALU.add, ["msum", ("tmp", 0)], [("mg", dtl)])
                mgk = [("mg", k) for k in range(8)]
                for dtl in range(8):
                    bk = nb_ % 4
                    nb_ += 1
                    for kt in range(8):
                        P.mm(banks[bk][:, 0:bw], wout[:, kt, dtl * 128:(dtl + 1) * 128], mg[:, kt, 0:bw], kt == 0, kt == 7, ["wout"] + mgk, [("bk", bk)])
                    P.tt("dve", hb[i][:, dtl, 0:bw], hb[i][:, dtl, 0:bw], banks[bk][:, 0:bw], ALU.add, [("h", i), ("bk", bk)], [("h", i)])
                off = PAD if c0 == 0 else 0
                P.dma("sp", env["hT"][s, :, :, c0 + off:c0 + bw], hb[i][:, :, off:bw], r=[("h", i)])
        P.flush()


def build(n_layers=DEPTH, nseq=2, dbg=False):
    nc = bass.Bass("TRN2", target_bir_lowering=False)
    x_in = nc.dram_tensor("x", [nseq, SEQ, D], F32, kind="ExternalInput").ap()
    meta_in = nc.dram_tensor("meta", [NMETA, D], F32, kind="ExternalInput").ap()
    wall_in = nc.dram_tensor("wall", [max(n_layers, 1), 128, NW], F32, kind="ExternalInput").ap()
    small_in = nc.dram_tensor("small", [max(n_layers, 1), 128, NS], F32, kind="ExternalInput").ap()
    fnw_in = nc.dram_tensor("fnw", [128, 8], F32, kind="ExternalInput").ap()
    const_in = nc.dram_tensor("consts", [128, NC_CONST], F32, kind="ExternalInput").ap()
    const2_in = nc.dram_tensor("consts2", [128, NK], F32, kind="ExternalInput").ap()
    y_out = nc.dram_tensor("y", [nseq, SEQ, D], F32, kind="ExternalOutput").ap()
    hT = nc.dram_tensor("hT", [nseq, 128, 8, LP], F32, kind="Internal").ap()
    hnT = nc.dram_tensor("hnT", [nseq, 128, 8, LP], BF16, kind="Internal").ap()
    wallb = nc.dram_tensor("wallb", [max(n_layers, 1), 128, NW], BF16, kind="Internal").ap()

    top = ExitStack()
    P = Prog(nc, top)
    banks = [top.enter_context(nc.psum_tensor("bank%d" % b, [128, 512], F32)) for b in range(8)]
    cst = top.enter_context(sbt(nc, "cst", [128, NC_CONST], F32))
    cstb = top.enter_context(sbt(nc, "cstb", [128, NC_CONST], BF16))
    fnw = top.enter_context(sbt(nc, "fnw_s", [128, 8], F32))
    identf = cst[:, C_IDENT:C_IDENT + 128]
    identb = cstb[:, C_IDENT:C_IDENT + 128]
    onesb = cstb[:, C_ONES:C_ONES + 128]

    with ExitStack() as ph:
        P.dma("sp", cst[:], const_in, w=["cst"])
        P.dma("sp", fnw[:], fnw_in, w=["fnw"])
        P.cp("dve", cstb[:], cst[:], ["cst"], ["cstb"])
        CH = 2048
        stg = [ph.enter_context(sbt(nc, "wstg%d" % i, [128, CH], F32)) for i in range(3)]
        stgb = [ph.enter_context(sbt(nc, "wstgb%d" % i, [128, CH], BF16)) for i in range(3)]
        engs = ["dve", "pool", "act"]
        ci = 0
        for l in range(min(n_layers, 1)):
            for c0 in range(0, NW, CH):
                cw = min(CH, NW - c0)
                i = ci % 3
                P.dma("sp", stg[i][:, 0:cw], wall_in[l, :, c0:c0 + cw], w=[("stg", i)])
                P.cp(engs[i], stgb[i][:, 0:cw], stg[i][:, 0:cw], [("stg", i)], [("stgb", i)])
                P.dma("sp", wallb[l, :, c0:c0 + cw], stgb[i][:, 0:cw], r=[("stgb", i)])
                ci += 1
        xt = [ph.enter_context(sbt(nc, "xt%d" % i, [128, D], F32)) for i in range(2)]
        hs = [ph.enter_context(sbt(nc, "hs%d" % i, [128, 8, 128], F32)) for i in range(2)]
        it = 0
        for s in range(nseq):
            for tt in range(NT):
                i = it % 2
                it += 1
                if tt == 0:
                    P.ms("pool", hs[i][:], 0.0, [("hs", i)])
                    P.dma("sp", xt[i][0:NMETA, :], meta_in, w=[("xt", i)])
                    np_ = NMETA
                else:
                    P.dma("sp", xt[i][:], x_in[s, (tt - 1) * 128:tt * 128, :], w=[("xt", i)])
                    np_ = 128
                for kt in range(8):
                    bk = banks[kt % 2]
                    P.tr(bk[:, 0:np_], xt[i][0:np_, kt * 128:(kt + 1) * 128], identf[0:np_, 0:np_], [("xt", i), "cst"], [("bk", kt % 2)])
                    P.cp("act" if kt % 2 else "dve", hs[i][:, kt, 128 - np_:128], bk[:, 0:np_], [("bk", kt % 2)], [("hs", i)])
                P.dma("sp", hT[s, :, :, tt * 128:(tt + 1) * 128], hs[i][:], r=[("hs", i)])
        P.flush()

    ys5T = nc.dram_tensor("ys5T", [nseq, 128, 4, LP], BF16, kind="ExternalOutput" if dbg else "Internal").ap()
    yssdT = nc.dram_tensor("yssdT", [nseq, 128, 8, LP], BF16, kind="ExternalOutput" if dbg else "Internal").ap()
    ygdnT = nc.dram_tensor("ygdnT", [nseq, 128, 4, LP], BF16, kind="ExternalOutput" if dbg else "Internal").ap()
    yfD = nc.dram_tensor("yfD", [nseq, NT, 128, 1024], BF16, kind="Internal").ap()
    env = dict(cast_next=True, n_layers=n_layers, wall_in=wall_in, yfD=yfD, nc=nc, P=P, banks=banks, hT=hT, hnT=hnT, wallb=wallb, small_in=small_in, identf=identf, identb=identb,
               onesb=onesb, const2=const2_in, ys5T=ys5T, yssdT=yssdT, ygdnT=ygdnT, nseq=nseq, cst=cst, cstb=cstb)
    for l in range(n_layers):
        phase_norm(env, l)
        if dbg in (False, "s5"):
            phase_s5(env, l)
        if dbg in (False, "ssd"):
            phase_ssd(env, l)
        if dbg in (False, "gdn"):
            phase_gdn(env, l)
        if dbg:
            break
        phase_merge(env, l)

    def rms_block(ph_tiles, s, c0, bw, hblk, key):
        sq, rstd = ph_tiles
        P.dma("sp", hblk[:, :, 0:bw], hT[s, :, :, c0:c0 + bw], w=[key + "h"])
        P.act(sq[:, :, 0:bw], hblk[:, :, 0:bw], AF.Square, [key + "h"], [key + "sq"])
        for kt in range(8):
            P.mm(banks[7][:, 0:bw], onesb, sq[:, kt, 0:bw], kt == 0, kt == 7, ["cstb", key + "sq"], ["b7"])
        P.act(rstd[:, 0:bw], banks[7][:, 0:bw], AF.Sqrt, ["b7"], [key + "rs"], bias=EPS, scale=1.0 / D)
        P.recip(rstd[:, 0:bw], rstd[:, 0:bw], [key + "rs"], [key + "rs"])

    with ExitStack() as ph:
        hb = [ph.enter_context(sbt(nc, "fh%d" % i, [128, 8, 512], F32)) for i in range(2)]
        sqb = [ph.enter_context(sbt(nc, "fsq%d" % i, [128, 8, 512], BF16)) for i in range(2)]
        rsb = [ph.enter_context(sbt(nc, "frs%d" % i, [128, 512], F32)) for i in range(2)]
        hnf = [ph.enter_context(sbt(nc, "fhn%d" % i, [128, 8, 512], F32)) for i in range(2)]
        ot = [ph.enter_context(sbt(nc, "fot%d" % i, [128, D], F32)) for i in range(2)]
        it = 0
        oi = 0
        for s in range(nseq):
            for (c0, bw) in BLOCKS:
                i = it % 2
                it += 1
                key = "f%d" % i
                rms_block((sqb[i], rsb[i]), s, c0, bw, hb[i], key)
                for kt in range(8):
                    P.stt(hnf[i][:, kt, 0:bw], hb[i][:, kt, 0:bw], fnw[:, kt:kt + 1], rsb[i][:, 0:bw], ALU.mult, ALU.mult,
                          [key + "h", key + "rs", "fnw"], [key + "hn"])
                for t0 in range(0, bw, 128):
                    tt = (c0 + t0) // 128
                    if tt == 0:
                        continue
                    j = oi % 2
                    oi += 1
                    for kt in range(8):
                        bk = banks[kt % 4]
                        P.tr(bk[:, 0:128], hnf[i][:, kt, t0:t0 + 128], identf, [key + "hn", "cst"], [("bk", kt % 4)])
                        P.cp("act" if kt % 2 else "dve", ot[j][:, kt * 128:(kt + 1) * 128], bk[:, 0:128], [("bk", kt % 4)], [("ot", j)])
                    P.dma("sp", y_out[s, (tt - 1) * 128:tt * 128, :], ot[j][:], r=[("ot", j)])
        P.flush()
    top.close()
    return nc


_CACHE = {}


def kernel(**inputs):
    n_layers = inputs.pop("_n_layers", DEPTH)
    dbg = inputs.pop("_dbg", False)
    x = np.asarray(inputs["x"], np.float32)
    nseq = x.shape[0] // N_CORES
    walls, smalls = [], []
    for i in range(max(n_layers, 1)):
        w, s = _arrange_layer(i, inputs)
        walls.append(w)
        smalls.append(s)
    wall = np.stack(walls)
    small = np.stack(smalls)
    fnw = np.ascontiguousarray(np.asarray(inputs["final_norm_w"], np.float32).reshape(8, 128).T)
    meta = np.asarray(inputs["meta_tokens"], np.float32)
    consts = _consts()
    consts2 = _consts2()
    key = (n_layers, nseq, dbg)
    if key not in _CACHE:
        _CACHE[key] = build(n_layers, nseq, dbg)
    nc = _CACHE[key]
    in_maps = []
    for c in range(N_CORES):
        in_maps.append({"x": np.ascontiguousarray(x[c * nseq:(c + 1) * nseq]), "meta": meta, "wall": wall,
                        "small": small, "fnw": fnw, "consts": consts, "consts2": consts2})
    res = run_bass_kernel_spmd(nc, in_maps, core_ids=list(range(N_CORES)))
    if dbg:
        return res.results
    return np.concatenate([r["y"] for r in res.results], axis=0)
```

```python
import numpy as np
import concourse.bass as bass
import concourse.mybir as mybir
from concourse.bass_utils import run_bass_kernel_spmd
from contextlib import ExitStack

F32 = mybir.dt.float32
BF16 = mybir.dt.bfloat16
ALU = mybir.AluOpType
AF = mybir.ActivationFunctionType
AX = mybir.AxisListType

N_CORES = 8
DBG = {}
D = 1024
DEPTH = 4
SEQ = 2048
NMETA = 16
LP = 2176
PAD = 112
NT = 17
EPS = 1e-6
BLOCKS = [(0, 512), (512, 512), (1024, 512), (1536, 512), (2048, 128)]
N_DMA_SEMS = 40
MAGIC = 12582912.0
TWO_PI = float(2 * np.pi)


class Prog:
    ENGS = ("pe", "act", "dve", "pool", "sp")

    def __init__(self, nc, stack):
        self.nc = nc
        self.ops = {e: [] for e in self.ENGS}
        self.cnt = {e: 0 for e in self.ENGS}
        self.seen = {e: {} for e in self.ENGS}
        self.state = {}
        self.dma_i = 0
        self.dma_tot = [0] * N_DMA_SEMS
        self.pending_dma = {e: [] for e in self.ENGS}
        self.sems = {e: stack.enter_context(nc.semaphore("s_" + e)) for e in self.ENGS}
        for i in range(N_DMA_SEMS):
            self.sems[("d", i)] = stack.enter_context(nc.semaphore("s_d%d" % i))
        self.nblocks = 0
        self.ninstr = 0
        self.relaxed = False

    def _need(self, eng, tok, waits):
        if tok is None:
            return
        sk, val = tok
        if sk == "pe" and eng == "pe":
            return
        if sk == eng and val > self.cnt[eng] and not DBG.get("strict"):
            return
        if (self.relaxed or eng == "dve") and not DBG.get("strict") and sk == eng and val <= self.cnt[eng] - 1:
            return
        if self.seen[eng].get(sk, 0) >= val:
            return
        self.seen[eng][sk] = val
        waits.append(tok)

    def _deps(self, eng, reads, writes):
        waits = []
        for k in reads:
            st = self.state.get(k)
            if st is not None:
                self._need(eng, st[0], waits)
        for k in writes:
            st = self.state.get(k)
            if st is not None:
                self._need(eng, st[0], waits)
                for t in st[1]:
                    self._need(eng, t, waits)
        return waits

    def _commit(self, tok, reads, writes):
        for k in reads:
            st = self.state.setdefault(k, [None, []])
            st[1].append(tok)
        for k in writes:
            self.state[k] = [tok, []]

    @staticmethod
    def _excl(r, w):
        r2 = [k for k in r if not (isinstance(k, tuple) and k[0] == "bk")]
        if len(r2) == len(r):
            return r, w
        return r2, list(w) + [k for k in r if isinstance(k, tuple) and k[0] == "bk"]

    def op(self, eng, fn, r=(), w=(), sig=True):
        r, w = self._excl(r, w)
        waits = self._deps(eng, r, w)
        tok = (eng, self.cnt[eng] + 1)
        if sig:
            self.cnt[eng] += 1
        self.ops[eng].append((waits, fn, eng if sig else None))
        self._commit(tok, r, w)
        return tok

    def dma(self, eng, out, in_, r=(), w=(), **kw):
        waits = self._deps(eng, r, w)
        si = self.dma_i % N_DMA_SEMS
        self.dma_i += 1
        prev = self.dma_tot[si]
        if prev > 0:
            self._need(eng, (("d", si), prev), waits)
        self.dma_tot[si] = prev + 16
        tok = (("d", si), prev + 16)
        fn = lambda e, out=out, in_=in_, kw=kw: e.dma_start(out=out, in_=in_, **kw)
        self.ops[eng].append((waits, fn, ("d", si)))
        self._commit(tok, r, w)
        self.pending_dma[eng].append(tok)
        return tok

    def flush(self):
        nc = self.nc
        for e in self.ENGS:
            fw = []
            for t in self.pending_dma[e]:
                self._need(e, t, fw)
            self.pending_dma[e] = []
            if fw:
                self.ops[e].append((fw, None, None))
        sems = self.sems
        if DBG.get("dump") and self.nblocks >= DBG["dump"]:
            for e in self.ENGS:
                print("ENGINE", e, "cnt_end", self.cnt[e])
                c = None
                for waits, fn, sg in self.ops[e]:
                    print("   waits", waits, "sig", sg, "fn", None if fn is None else fn.__code__.co_names[-1] if fn.__code__.co_names else "?")
        with nc.Block() as block:
            def run(engname):
                def body(e):
                    for waits, fn, sg in self.ops[engname]:
                        for sk, val in waits:
                            e.wait_ge(sems[sk], val)
                            self.ninstr += 1
                        if fn is None:
                            continue
                        ins = fn(e)
                        self.ninstr += 1
                        if sg is not None:
                            ins.then_inc(sems[sg], 16 if isinstance(sg, tuple) else 1)
                return body
            block.tensor(run("pe"))
            block.scalar(run("act"))
            block.vector(run("dve"))
            block.gpsimd(run("pool"))
            block.sync(run("sp"))
        self.ops = {e: [] for e in self.ENGS}
        self.state = {}
        self.nblocks += 1

    def mm(self, out, lhsT, rhs, start, stop, r, w):
        return self.op("pe", lambda e: e.matmul(out, lhsT, rhs, start=start, stop=stop), r, w, sig=stop)

    def tr(self, out, in_, ident, r, w):
        return self.op("pe", lambda e: e.transpose(out, in_, ident), r, w)

    def act(self, out, in_, func, r, w, bias=None, scale=1.0, accum_out=None):
        kw = {}
        if bias is not None:
            kw["bias"] = bias
        if accum_out is not None:
            kw["accum_out"] = accum_out
        return self.op("act", lambda e: e.activation(out=out, in_=in_, func=func, scale=scale, **kw), r, w)

    def tt(self, eng, out, a, b, op, r, w, sig=True):
        return self.op(eng, lambda e: e.tensor_tensor(out=out, in0=a, in1=b, op=op), r, w, sig=sig)

    def ts(self, eng, out, a, s1, s2, op0, op1, r, w):
        if s2 is None:
            return self.op(eng, lambda e: e.tensor_single_scalar(out=out, in_=a, scalar=s1, op=op0), r, w)
        return self.op(eng, lambda e: e.tensor_scalar(out=out, in0=a, scalar1=s1, scalar2=s2, op0=op0, op1=op1), r, w)

    def stt(self, out, in0, scalar, in1, op0, op1, r, w):
        return self.op("dve", lambda e: e.scalar_tensor_tensor(out=out, in0=in0, scalar=scalar, in1=in1, op0=op0, op1=op1), r, w)

    def cp(self, eng, out, in_, r, w):
        if eng == "act":
            return self.op("act", lambda e: e.copy(out, in_), r, w)
        return self.op(eng, lambda e: e.tensor_copy(out, in_), r, w)

    def ms(self, eng, ap, val, w):
        return self.op(eng, lambda e: e.memset(ap, val), (), w)

    def recip(self, out, in_, r, w):
        return self.op("dve", lambda e: e.reciprocal(out, in_), r, w)

    def scan(self, out, d0, d1, init, op0, op1, r, w):
        return self.op("dve", lambda e: e.tensor_tensor_scan(out=out, data0=d0, data1=d1, initial=init, op0=op0, op1=op1), r, w)


N_FM = 59
T_U, T_ZS5, T_X, T_B, T_C, T_DT, T_Q, T_K, T_V, T_BETA, T_ARAW, T_GATE = 0, 4, 8, 16, 18, 20, 21, 25, 29, 33, 34, 35
W_FM = 0
W_ZSSD = W_FM + N_FM * 1024
W_ZGDN = W_ZSSD + 8 * 1024
W_GLU = W_ZGDN + 8 * 512
W_S5O = W_GLU + 4 * 512
W_SSDO = W_S5O + 4 * 1024
W_GDNO = W_SSDO + 8 * 1024
W_OUT = W_GDNO + 4 * 1024
NW = W_OUT + 8 * 1024


def _kt_layout(w):
    k, c = w.shape
    return np.ascontiguousarray(w.reshape(k // 128, 128, c).transpose(1, 0, 2)).reshape(128, -1)


def _fm_cols():
    tiles = []
    for base, n in ((0, 4), (512, 4), (1024, 8), (3072, 2), (3328, 2)):
        for t in range(n):
            tiles.append(list(range(base + 128 * t, base + 128 * (t + 1))))
    dt = [-1] * 128
    dt[0:16] = range(3584, 3600)
    dt[32:48] = range(3600, 3616)
    tiles.append(dt)
    for base in (3616, 4128, 4640):
        for t in range(4):
            tiles.append(list(range(base + 128 * t, base + 128 * (t + 1))))
    be = [-1] * 128
    be[0:4] = range(5664, 5668)
    be[32:36] = range(5668, 5672)
    tiles.append(be)
    ar = [-1] * 128
    ar[0:4] = range(5672, 5676)
    ar[32:36] = range(5676, 5680)
    tiles.append(ar)
    for b in range(3):
        for t in range(8):
            tiles.append(list(range(5680 + b * 1024 + 128 * t, 5680 + b * 1024 + 128 * (t + 1))))
    assert len(tiles) == N_FM
    return tiles


S_NORMW = 0
S_ARE_A = 8
S_AIM_A = 40
S_LDT_A = 72
S_ARE_B = 104
S_AIM_B = 616
S_LDT_B = 1128
S_BRE = 1640
S_BIM = S_BRE + 4096
S_CRE = S_BIM + 4096
S_CIM = S_CRE + 4096
S_S5D = S_CIM + 4096
S_BGLU = S_S5D + 4
S_SCW = S_BGLU + 4
S_SCB = S_SCW + 60
S_SALOG = S_SCB + 12
S_SDTB = S_SALOG + 1
S_SD = S_SDTB + 1
S_SNW = S_SD + 16
S_GCW = S_SNW + 1024
S_GALOG = S_GCW + 60
S_GDTB = S_GALOG + 1
S_GNW = S_GDTB + 1
NS = S_GNW + 128


def _arrange_layer(i, inp):
    f = np.float32
    wall = np.zeros((128, NW), f)
    win = np.asarray(inp["w_in"][i], f)
    winp = np.concatenate([win, np.zeros((D, 1), f)], axis=1)
    for t, cols in enumerate(_fm_cols()):
        wall[:, W_FM + t * 1024:W_FM + (t + 1) * 1024] = _kt_layout(winp[:, cols])
    wall[:, W_ZSSD:W_ZGDN] = _kt_layout(win[:, 2048:3072])
    wall[:, W_ZGDN:W_GLU] = _kt_layout(win[:, 5152:5664])
    wall[:, W_GLU:W_S5O] = _kt_layout(np.asarray(inp["s5_w_glu"][i], f))
    wall[:, W_S5O:W_SSDO] = _kt_layout(np.asarray(inp["w_s5_out"][i], f))
    wall[:, W_SSDO:W_GDNO] = _kt_layout(np.asarray(inp["w_ssd_out"][i], f))
    wall[:, W_GDNO:W_OUT] = _kt_layout(np.asarray(inp["w_gdn_out"][i], f))
    wall[:, W_OUT:NW] = _kt_layout(np.asarray(inp["w_out"][i], f))

    sm = np.zeros((128, NS), f)
    sm[:, S_NORMW:S_NORMW + 8] = np.asarray(inp["norm_w"][i], f).reshape(8, 128).T
    are = np.asarray(inp["s5_a_re"][i], f)
    aim = np.asarray(inp["s5_a_im"][i], f)
    ldt = np.asarray(inp["s5_log_dt"][i], f)
    for nm, off in ((are, S_ARE_A), (aim, S_AIM_A)):
        a4 = nm.reshape(2, 16, 2, 64)
        sm[:, off:off + 32] = a4.transpose(2, 3, 0, 1).reshape(128, 32)
    l4 = np.broadcast_to(ldt.reshape(2, 16, 2, 1), (2, 16, 2, 64))
    sm[:, S_LDT_A:S_LDT_A + 32] = l4.transpose(2, 3, 0, 1).reshape(128, 32)
    for nm, off in ((are, S_ARE_B), (aim, S_AIM_B)):
        a5 = nm.reshape(2, 4, 4, 2, 1, 64)
        a5 = np.broadcast_to(a5, (2, 4, 4, 2, 16, 64))
        sm[:, off:off + 512] = a5.transpose(2, 3, 4, 0, 1, 5).reshape(128, 512)
    l5 = np.broadcast_to(ldt.reshape(2, 4, 4, 2, 1, 1), (2, 4, 4, 2, 16, 64))
    sm[:, S_LDT_B:S_LDT_B + 512] = l5.transpose(2, 3, 4, 0, 1, 5).reshape(128, 512)
    for nm, off in ((inp["s5_b_re"], S_BRE), (inp["s5_b_im"], S_BIM)):
        b = np.asarray(nm[i], f).reshape(2, 4, 4, 2, 64, 16)
        z = np.zeros((4, 2, 16, 2, 4, 4, 2, 64), f)
        for q in range(4):
            for m in range(2):
                z[q, m, :, :, :, q, m, :] = b[:, :, q, m].transpose(3, 0, 1, 2)
        sm[:, off:off + 4096] = z.reshape(128, 4096)
    for nm, off in ((inp["s5_c_re"], S_CRE), (inp["s5_c_im"], S_CIM)):
        c = np.asarray(nm[i], f).reshape(2, 16, 2, 16, 64)
        z = np.zeros((2, 64, 2, 16, 4, 2, 16), f)
        for pr in range(16):
            for m in range(2):
                z[m, :, :, pr, pr % 4, m, :] = c[:, pr, m].transpose(2, 0, 1)
        sm[:, off:off + 4096] = z.reshape(128, 4096)
    sm[:, S_S5D:S_S5D + 4] = np.asarray(inp["s5_d"][i], f).reshape(4, 128).T
    sm[:, S_BGLU:S_BGLU + 4] = np.asarray(inp["s5_b_glu"][i], f).reshape(4, 128).T
    scw = np.asarray(inp["ssd_conv_w"][i], f)
    sm[:, S_SCW:S_SCW + 60] = scw.reshape(5, 12, 128).transpose(2, 1, 0).reshape(128, 60)
    sm[:, S_SCB:S_SCB + 12] = np.asarray(inp["ssd_conv_b"][i], f).reshape(12, 128).T
    for nm, off in ((inp["ssd_a_log"], S_SALOG), (inp["ssd_dt_bias"], S_SDTB)):
        v = np.asarray(nm[i], f)
        sm[0:16, off] = v[0]
        sm[32:48, off] = v[1]
    sm[:, S_SD:S_SD + 16] = np.asarray(inp["ssd_d"][i], f)[None, :]
    sm[:, S_SNW:S_SNW + 1024] = np.asarray(inp["ssd_norm_w"][i], f)[None, :]
    gcw = np.asarray(inp["gdn_conv_w"][i], f)
    sm[:, S_GCW:S_GCW + 60] = gcw.reshape(5, 12, 128).transpose(2, 1, 0).reshape(128, 60)
    for nm, off in ((inp["gdn_a_log"], S_GALOG), (inp["gdn_dt_bias"], S_GDTB)):
        v = np.asarray(nm[i], f)
        sm[0:4, off] = v[0]
        sm[32:36, off] = v[1]
    sm[:, S_GNW:S_GNW + 128] = np.asarray(inp["gdn_norm_w"][i], f)[None, :]
    return wall, sm


C_IDENT = 0
C_ONES = 128
NC_CONST = 256
K_SEL = 0
K_NEGF = K_SEL + 32 * 128
K_NEGB = K_NEGF + 128
K_LASTF = K_NEGB + 128
K_LASTB = K_LASTF + 128
K_SELG = K_LASTB + 128
K_GNEGF = K_SELG + 8 * 128
K_GNEGB = K_GNEGF + 128
K_GSTRF = K_GNEGB + 128
K_GSTRB = K_GSTRF + 128
K_GLASTF = K_GSTRB + 128
K_GLASTB = K_GLASTF + 128
K_GCHF = K_GLASTB + 128
K_GCHB = K_GCHF + 256
NK = K_GCHB + 256
NEG = -30000.0


def _consts():
    c = np.zeros((128, NC_CONST), np.float32)
    c[:, C_IDENT:C_IDENT + 128] = np.eye(128, dtype=np.float32)
    c[:, C_ONES:C_ONES + 128] = 1.0
    return c


def _consts2():
    k = np.zeros((128, NK), np.float32)
    for d in range(2):
        for h in range(16):
            k[d * 32 + h, K_SEL + (d * 16 + h) * 128:K_SEL + (d * 16 + h + 1) * 128] = 1.0
        for h in range(4):
            k[d * 32 + h, K_SELG + (d * 4 + h) * 128:K_SELG + (d * 4 + h + 1) * 128] = 1.0
    j = np.arange(128)[:, None]
    i = np.arange(128)[None, :]
    same = (j // 64) == (i // 64)
    k[:, K_NEGF:K_NEGF + 128] = np.where(j <= i, 0.0, NEG)
    k[:, K_NEGB:K_NEGB + 128] = np.where(j >= i, 0.0, NEG)
    k[127, K_LASTF:K_LASTF + 128] = 1.0
    k[0, K_LASTB:K_LASTB + 128] = 1.0
    k[:, K_GNEGF:K_GNEGF + 128] = np.where(same & (j <= i), 0.0, NEG)
    k[:, K_GNEGB:K_GNEGB + 128] = np.where(same & (j >= i), 0.0, NEG)
    k[:, K_GSTRF:K_GSTRF + 128] = np.where(same & (j < i), 1.0, 0.0)
    k[:, K_GSTRB:K_GSTRB + 128] = np.where(same & (j > i), 1.0, 0.0)
    k[:, K_GLASTF:K_GLASTF + 128] = np.where(j == 64 * (i // 64) + 63, 1.0, 0.0)
    k[:, K_GLASTB:K_GLASTB + 128] = np.where(j == 64 * (i // 64), 1.0, 0.0)
    for sc in range(2):
        k[64 * sc + 63, K_GCHF + sc * 128:K_GCHF + (sc + 1) * 128] = 1.0
        k[64 * sc, K_GCHB + sc * 128:K_GCHB + (sc + 1) * 128] = 1.0
    return k


_UID = [0]


def sbt(nc, name, shape, dt):
    _UID[0] += 1
    return nc.sbuf_tensor("%s_%d" % (name, _UID[0]), shape, dt)


def rawap(t_ap, off, dims):
    return bass.AP(t_ap.tensor, t_ap.offset + off, [list(t_ap.ap[0])] + [list(d) for d in dims])


def fm_tile(env, l, t, n=1):
    return env["wallb"][l, :, W_FM + t * 1024:W_FM + (t + n) * 1024].rearrange("p (n k c) -> p n k c", n=n, k=8)


def phase_norm(env, l):
    nc, P, banks = env["nc"], env["P"], env["banks"]
    with ExitStack() as ph:
        nw = ph.enter_context(sbt(nc, "nw", [128, 8], F32))
        P.dma("sp", nw[:], env["small_in"][l, :, S_NORMW:S_NORMW + 8], w=["nw"])
        hb = [ph.enter_context(sbt(nc, "nh%d" % i, [128, 8, 512], F32)) for i in range(2)]
        sqb = [ph.enter_context(sbt(nc, "nsq%d" % i, [128, 8, 512], BF16)) for i in range(2)]
        rsb = [ph.enter_context(sbt(nc, "nrs%d" % i, [128, 512], F32)) for i in range(2)]
        hnb = [ph.enter_context(sbt(nc, "nhn%d" % i, [128, 8, 512], BF16)) for i in range(2)]
        it = 0
        for s in range(env["nseq"]):
            for (c0, bw) in BLOCKS:
                i = it % 2
                it += 1
                key = "n%d" % i
                P.dma("sp", hb[i][:, :, 0:bw], env["hT"][s, :, :, c0:c0 + bw], w=[key + "h"])
                P.act(sqb[i][:, :, 0:bw], hb[i][:, :, 0:bw], AF.Square, [key + "h"], [key + "sq"])
                for kt in range(8):
                    P.mm(banks[it % 2][:, 0:bw], env["onesb"], sqb[i][:, kt, 0:bw], kt == 0, kt == 7, ["cstb", key + "sq"], [("bk", it % 2)])
                P.act(rsb[i][:, 0:bw], banks[it % 2][:, 0:bw], AF.Sqrt, [("bk", it % 2)], [key + "rs"], bias=EPS, scale=1.0 / D)
                P.recip(rsb[i][:, 0:bw], rsb[i][:, 0:bw], [key + "rs"], [key + "rs"])
                for kt in range(8):
                    P.stt(hnb[i][:, kt, 0:bw], hb[i][:, kt, 0:bw], nw[:, kt:kt + 1], rsb[i][:, 0:bw], ALU.mult, ALU.mult,
                          [key + "h", key + "rs", "nw"], [key + "hn"])
                P.dma("sp", env["hnT"][s, :, :, c0:c0 + bw], hnb[i][:, :, 0:bw], r=[key + "hn"])
        P.flush()


def rr_sin(P, out, x, shift, tmp, rk, wk, tk):
    P.ts("dve", out, x, float(shift), None, ALU.add, None, rk, [wk])
    P.ts("dve", tmp, out, 1.0 / TWO_PI, MAGIC, ALU.mult, ALU.add, [wk], [tk])
    P.ts("dve", tmp, tmp, -MAGIC, -TWO_PI, ALU.add, ALU.mult, [tk], [tk])
    P.tt("dve", tmp, tmp, out, ALU.add, [tk, wk], [tk])
    P.act(out, tmp, AF.Sin, [tk], [wk])


def s5_lambda(P, nc, ph, src, n, tag):
    t = {}
    for nm in ("dt", "xr", "th", "mag", "sn", "cs", "lr", "li", "tmp"):
        t[nm] = ph.enter_context(sbt(nc, tag + nm, [128, n], F32))
    k = tag
    P.act(t["dt"][:], src[:, 2, :], AF.Exp, [k + "src"], [k + "dt"])
    P.tt("dve", t["xr"][:], src[:, 0, :], t["dt"][:], ALU.mult, [k + "src", k + "dt"], [k + "xr"])
    P.tt("dve", t["th"][:], src[:, 1, :], t["dt"][:], ALU.mult, [k + "src", k + "dt"], [k + "th"])
    P.act(t["mag"][:], t["xr"][:], AF.Exp, [k + "xr"], [k + "mag"])
    rr_sin(P, t["sn"][:], t["th"][:], 0.0, t["tmp"][:], [k + "th"], k + "sn", k + "tmp")
    rr_sin(P, t["cs"][:], t["th"][:], float(np.pi / 2), t["tmp"][:], [k + "th"], k + "cs", k + "tmp")
    P.tt("dve", t["lr"][:], t["mag"][:], t["cs"][:], ALU.mult, [k + "mag", k + "cs"], [k + "lr"])
    P.tt("dve", t["li"][:], t["mag"][:], t["sn"][:], ALU.mult, [k + "mag", k + "sn"], [k + "li"])
    return t


def phase_s5(env, l):
    nc, P, banks = env["nc"], env["P"], env["banks"]
    small = env["small_in"]
    with ExitStack() as lay:
        LA = lay.enter_context(sbt(nc, "LA", [128, 32, 2], F32))
        LB = lay.enter_context(sbt(nc, "LB", [128, 32, 2], F32))
        WB = lay.enter_context(sbt(nc, "WB", [128, 2, 4096], BF16))
        WO = lay.enter_context(sbt(nc, "WO", [128, 2, 4096], BF16))
        s5d = lay.enter_context(sbt(nc, "s5d", [128, 8], F32))
        P.dma("sp", s5d[:], small[l, :, S_S5D:S_S5D + 8], w=["s5d"])
        with ExitStack() as ph:
            srcA = ph.enter_context(sbt(nc, "srcA", [128, 3, 32], F32))
            srcB = ph.enter_context(sbt(nc, "srcB", [128, 3, 512], F32))
            P.dma("sp", srcA[:], small[l, :, S_ARE_A:S_ARE_A + 96].rearrange("p (a n) -> p a n", a=3), w=["Asrc"])
            P.dma("sp", srcB[:], small[l, :, S_ARE_B:S_ARE_B + 1536].rearrange("p (a n) -> p a n", a=3), w=["Bsrc"])
            tA = s5_lambda(P, nc, ph, srcA, 32, "A")
            P.cp("dve", LA[:, :, 0], tA["lr"][:], ["Alr"], ["LA0"])
            P.cp("dve", LA[:, :, 1], tA["lr"][:], ["Alr"], ["LA1"])
            P.ts("dve", LB[:, :, 0], tA["li"][:], -1.0, None, ALU.mult, None, ["Ali"], ["LB0"])
            P.cp("dve", LB[:, :, 1], tA["li"][:], ["Ali"], ["LB1"])
            tB = s5_lambda(P, nc, ph, srcB, 512, "B")
            den = ph.enter_context(sbt(nc, "den", [128, 512], F32))
            t1 = ph.enter_context(sbt(nc, "pt1", [128, 512], F32))
            t2 = ph.enter_context(sbt(nc, "pt2", [128, 512], F32))
            fre = ph.enter_context(sbt(nc, "fre", [128, 512], F32))
            fim = ph.enter_context(sbt(nc, "fim", [128, 512], F32))
            are, aim = srcB[:, 0, :], srcB[:, 1, :]
            P.tt("dve", den[:], are, are, ALU.mult, ["Bsrc"], ["den"])
            P.tt("dve", t1[:], aim, aim, ALU.mult, ["Bsrc"], ["t1"])
            P.tt("dve", den[:], den[:], t1[:], ALU.add, ["den", "t1"], ["den"])
            P.recip(den[:], den[:], ["den"], ["den"])
            lm1 = tB["tmp"]
            P.ts("dve", lm1[:], tB["lr"][:], -1.0, None, ALU.add, None, ["Blr"], ["Btmp"])
            P.tt("dve", t1[:], lm1[:], are, ALU.mult, ["Btmp", "Bsrc", "den"], ["t1"])
            P.tt("dve", t2[:], tB["li"][:], aim, ALU.mult, ["Bli", "Bsrc"], ["t2"])
            P.tt("dve", t1[:], t1[:], t2[:], ALU.add, ["t1", "t2"], ["t1"])
            P.tt("dve", fre[:], t1[:], den[:], ALU.mult, ["t1", "den"], ["fre"])
            P.tt("dve", t1[:], tB["li"][:], are, ALU.mult, ["Bli", "Bsrc", "fre"], ["t1"])
            P.tt("dve", t2[:], lm1[:], aim, ALU.mult, ["Btmp", "Bsrc"], ["t2"])
            P.tt("dve", t1[:], t1[:], t2[:], ALU.subtract, ["t1", "t2"], ["t1"])
            P.tt("dve", fim[:], t1[:], den[:], ALU.mult, ["t1", "den"], ["fim"])
            braw = ph.enter_context(sbt(nc, "braw", [128, 2, 4096], F32))
            P.dma("sp", braw[:], small[l, :, S_BRE:S_BRE + 8192].rearrange("p (a n) -> p a n", a=2), w=["braw"])
            bt1 = ph.enter_context(sbt(nc, "bt1", [128, 4096], F32))
            bt2 = ph.enter_context(sbt(nc, "bt2", [128, 4096], F32))
            v4 = lambda ap: ap.rearrange("p (d q n) -> p d q n", d=8, q=8)
            fb = lambda f: f[:].rearrange("p (d n) -> p d n", d=8).unsqueeze(2).broadcast_to([128, 8, 8, 64])
            P.tt("dve", v4(bt1[:]), v4(braw[:, 0, :]), fb(fre), ALU.mult, ["braw", "fre"], ["bt1"])
            P.tt("dve", v4(bt2[:]), v4(braw[:, 1, :]), fb(fim), ALU.mult, ["braw", "fim"], ["bt2"])
            P.tt("dve", WB[:, 0, :], bt1[:], bt2[:], ALU.subtract, ["bt1", "bt2"], ["WB0"])
            P.tt("dve", v4(bt1[:]), v4(braw[:, 1, :]), fb(fre), ALU.mult, ["braw", "fre", "WB0"], ["bt1"])
            P.tt("dve", v4(bt2[:]), v4(braw[:, 0, :]), fb(fim), ALU.mult, ["braw", "fim", "WB0"], ["bt2"])
            P.tt("dve", WB[:, 1, :], bt1[:], bt2[:], ALU.add, ["bt1", "bt2"], ["WB1"])
            P.dma("sp", braw[:], small[l, :, S_CRE:S_CRE + 8192].rearrange("p (a n) -> p a n", a=2), r=["WB0", "WB1"], w=["craw"])
            P.cp("act", WO[:, 0, :], braw[:, 0, :], ["craw"], ["WO0"])
            P.ts("dve", WO[:, 1, :], braw[:, 1, :], -1.0, None, ALU.mult, None, ["craw"], ["WO1"])
            P.flush()

        NS = env["nseq"]
        TB = 64
        NB = LP // TB
        with ExitStack() as ph:
            uT = [ph.enter_context(sbt(nc, "uT", [128, 4, LP], BF16)) for s in range(NS)]
            ybuf = [ph.enter_context(sbt(nc, "ybuf", [128, 4, LP], BF16)) for s in range(NS)]
            with ExitStack() as pa:
                hnb = [pa.enter_context(sbt(nc, "s5hn%d" % i, [128, 8, 512], BF16)) for i in range(2)]
                wu = pa.enter_context(sbt(nc, "wu", [128, 4, 8, 128], BF16))
                P.dma("sp", wu[:], fm_tile(env, l, T_U, 4), w=["wu"])
                it = 0
                for s in range(NS):
                    for (c0, bw) in BLOCKS:
                        i = it % 2
                        it += 1
                        P.dma("sp", hnb[i][:, :, 0:bw], env["hnT"][s, :, :, c0:c0 + bw], w=[("hn", i)])
                        for a in range(4):
                            bk = (it * 4 + a) % 8
                            for kt in range(8):
                                P.mm(banks[bk][:, 0:bw], wu[:, a, kt, :], hnb[i][:, kt, 0:bw], kt == 0, kt == 7, ["wu", ("hn", i)], [("bk", bk)])
                            P.cp("act" if a % 2 else "dve", uT[s][:, a, c0:c0 + bw], banks[bk][:, 0:bw], [("bk", bk)], [("uT", s, a)])
                P.flush()
            with ExitStack() as phb:
                NX = NS * 64
                BU = phb.enter_context(sbt(nc, "BU", [128, TB, NX], F32))
                cur = phb.enter_context(sbt(nc, "Hh", [128, TB, NX], F32))
                Hc = phb.enter_context(sbt(nc, "Hc", [128, NX], F32))
                Hb = phb.enter_context(sbt(nc, "Hb", [128, TB, NX], BF16))
                P1 = phb.enter_context(sbt(nc, "P1", [128, NX], F32))
                Q1 = phb.enter_context(sbt(nc, "Q1", [128, NX], F32))
                ytmp = [phb.enter_context(sbt(nc, "ytmp%d" % i, [128, TB], F32)) for i in range(4)]
                CH = 1024
                cjobs = []
                if env.get("cast_next") and l + 1 < env["n_layers"]:
                    cstg = [phb.enter_context(sbt(nc, "cstg%d" % i, [128, CH], F32)) for i in range(2)]
                    cstgb = [phb.enter_context(sbt(nc, "cstgb%d" % i, [128, CH], BF16)) for i in range(2)]
                    cjobs = [(c0, min(CH, NW - c0)) for c0 in range(0, NW, CH)]
                P.ms("pool", Hc[:], 0.0, ["Hc"])
                LAd = [LA[:].rearrange("p (d q) t -> p d (q t)", d=2)[:, d, :].unsqueeze(1).broadcast_to([128, NS, 32]) for d in range(2)]
                LBd = [LB[:].rearrange("p (d q) t -> p d q t", d=2)[:, d, :, :].unsqueeze(1).broadcast_to([128, NS, 16, 2]) for d in range(2)]
                P1d = [P1[:, d * NS * 32:(d + 1) * NS * 32].rearrange("p (s x) -> p s x", s=NS) for d in range(2)]
                Q1d = [Q1[:, d * NS * 32:(d + 1) * NS * 32].rearrange("p (s x) -> p s x", s=NS) for d in range(2)]
                Q1d4 = [Q1[:, d * NS * 32:(d + 1) * NS * 32].rearrange("p (s q t) -> p s q t", s=NS, q=16) for d in range(2)]
                BUf = BU[:].rearrange("p x t -> p (x t)")
                nyt = 0
                for tb in range(NB):
                    tbd = (tb, NB - 1 - tb)
                    g = 0
                    for s in range(NS):
                        for d in range(2):
                            for kk in range(4):
                                bk = g % 8
                                g += 1
                                for j8 in range(8):
                                    pr, part = 4 * kk + j8 // 2, j8 % 2
                                    a, q0 = pr // 4, pr % 4
                                    col = ((d * 4 + a) * 4 + q0) * 128
                                    P.mm(banks[bk][:, j8 * TB:(j8 + 1) * TB], WB[:, part, col:col + 128],
                                         uT[s][:, a, tbd[d] * TB:(tbd[d] + 1) * TB], True, True, [("uT", s, a)], [("bk", bk)])
                                x0 = s * 64 + d * 32 + kk * 8
                                P.cp("act", BU[:, :, x0:x0 + 8].rearrange("p t x -> p x t"), banks[bk][:, :].rearrange("p (x t) -> p x t", x=8),
                                     [("bk", bk)], ["BU"])
                    for _ in range(3):
                        if cjobs:
                            c0_, cw_ = cjobs.pop(0)
                            ci_ = len(cjobs) % 2
                            P.dma("sp", cstg[ci_][:, 0:cw_], env["wall_in"][l + 1, :, c0_:c0_ + cw_], w=[("cstg", ci_)])
                            P.cp("pool", cstgb[ci_][:, 0:cw_], cstg[ci_][:, 0:cw_], [("cstg", ci_)], [("cstgb", ci_)])
                            P.dma("sp", env["wallb"][l + 1, :, c0_:c0_ + cw_], cstgb[ci_][:, 0:cw_], r=[("cstgb", ci_)])
                    P.relaxed = True
                    for j in range(TB):
                        tok = (j, TB - 1 - j)
                        hp, hps, hn_, bu_, pk = [], [], [], [], []
                        for d in range(2):
                            base = d * 32 * TB
                            if j == 0:
                                hp.append(rawap(Hc[:], d * 32, [[64, NS], [1, 32]]))
                                hps.append(rawap(Hc[:], d * 32 + 1, [[64, NS], [2, 16], [-1, 2]]))
                                pk.append("Hc")
                            else:
                                tp_ = tok[d] - 1 if d == 0 else tok[d] + 1
                                hp.append(rawap(cur[:], tp_ * NX + d * 32, [[64, NS], [1, 32]]))
                                hps.append(rawap(cur[:], tp_ * NX + d * 32 + 1, [[64, NS], [2, 16], [-1, 2]]))
                                pk.append(("Hh", d))
                            hn_.append(rawap(cur[:], tok[d] * NX + d * 32, [[64, NS], [1, 32]]))
                            bu_.append(rawap(BU[:], tok[d] * NX + d * 32, [[64, NS], [1, 32]]))
                        sg_ = bool(DBG.get("strict"))
                        for d in range(2):
                            P.tt("dve", P1d[d], LAd[d], hp[d], ALU.mult, [pk[d]], [("P1", d)], sig=sg_)
                        for d in range(2):
                            P.tt("dve", Q1d4[d], LBd[d], hps[d], ALU.mult, [pk[d]], [("Q1", d)], sig=sg_)
                        for d in range(2):
                            P.tt("dve", P1d[d], P1d[d], Q1d[d], ALU.add, [("P1", d), ("Q1", d)], [("P1", d)], sig=sg_)
                        for d in range(2):
                            P.tt("dve", hn_[d], P1d[d], bu_[d], ALU.add, [("P1", d), "BU"], [("Hh", d)], sig=sg_)
                    P.relaxed = False
                    if tb == NB - 1:
                        assert not cjobs
                    for d in range(2):
                        last = TB - 1 if d == 0 else 0
                        P.cp("dve", rawap(Hc[:], d * 32, [[64, NS], [1, 32]]), rawap(cur[:], last * NX + d * 32, [[64, NS], [1, 32]]),
                             [("Hh", d)], ["Hc"])
                    for s in range(NS):
                        P.cp("pool" if s % 2 == 0 else "act", Hb[:, :, s * 64:(s + 1) * 64], cur[:, :, s * 64:(s + 1) * 64], [("Hh", 0), ("Hh", 1)], [("Hb", s)])
                    for s in range(NS):
                        for d in range(2):
                            b_ = tbd[d]
                            first = (d == 0 and b_ <= NB // 2 - 1) or (d == 1 and b_ >= NB // 2)
                            for a in range(4):
                                bk = (s * 8 + d * 4 + a) % 8
                                n_ = 0
                                for q0 in range(4):
                                    for part in range(2):
                                        pr = 4 * a + q0
                                        col = (d * 16 + pr) * 128
                                        P.mm(banks[bk][:, 0:TB], WO[:, part, col:col + 128], Hb[:, :, s * 64 + d * 32 + pr * 2 + part],
                                             n_ == 0, n_ == 7, [("Hb", s)], [("bk", bk)])
                                        n_ += 1
                                ysl = ybuf[s][:, a, b_ * TB:(b_ + 1) * TB]
                                yk = ("yb", s, a, b_)
                                if first:
                                    P.cp("act", ysl, banks[bk][:, 0:TB], [("bk", bk)], [yk])
                                else:
                                    yt = ytmp[nyt % 4]
                                    P.cp("act", yt[:], banks[bk][:, 0:TB], [("bk", bk)], [("ytmp", nyt % 4)])
                                    P.tt("pool", ysl, ysl, yt[:], ALU.add, [("ytmp", nyt % 4), yk], [yk])
                                    nyt += 1
                P.flush()
            with ExitStack() as pc:
                hnb = [pc.enter_context(sbt(nc, "s5hnc%d" % i, [128, 8, 512], BF16)) for i in range(2)]
                wzs = pc.enter_context(sbt(nc, "wzs", [128, 4, 8, 128], BF16))
                wglu = pc.enter_context(sbt(nc, "wglu", [128, 4, 512], BF16))
                P.dma("sp", wzs[:], fm_tile(env, l, T_ZS5, 4), w=["wzs"])
                P.dma("sp", wglu[:], env["wallb"][l, :, W_GLU:W_GLU + 2048].rearrange("p (k c) -> p k c", k=4), w=["wglu"])
                yv = pc.enter_context(sbt(nc, "yv", [128, 4, 512], F32))
                ygb = pc.enter_context(sbt(nc, "ygb", [128, 4, 512], BF16))
                sg = pc.enter_context(sbt(nc, "sg", [128, 512], F32))
                zs = pc.enter_context(sbt(nc, "zs", [128, 512], F32))
                yo = [pc.enter_context(sbt(nc, "yo%d" % i, [128, 4, 512], BF16)) for i in range(2)]
                it = 0
                for s in range(NS):
                    for (c0, bw) in BLOCKS:
                        i = it % 2
                        it += 1
                        P.dma("sp", hnb[i][:, :, 0:bw], env["hnT"][s, :, :, c0:c0 + bw], w=[("hn", i)])
                        for a in range(4):
                            P.stt(yv[:, a, 0:bw], uT[s][:, a, c0:c0 + bw], s5d[:, a:a + 1], ybuf[s][:, a, c0:c0 + bw], ALU.mult, ALU.add,
                                  [], [("yv", a)])
                            P.act(yv[:, a, 0:bw], yv[:, a, 0:bw], AF.Gelu_apprx_tanh, [("yv", a)], [("yv", a)])
                            P.cp("pool", ygb[:, a, 0:bw], yv[:, a, 0:bw], [("yv", a)], [("ygb", a)])
                        for ao in range(4):
                            b0, b1 = ao % 2, 2 + ao % 2
                            for kt in range(4):
                                P.mm(banks[b0][:, 0:bw], wglu[:, kt, ao * 128:(ao + 1) * 128], ygb[:, kt, 0:bw], kt == 0, kt == 3,
                                     [("ygb", k) for k in range(4)] + ["wglu"], [("bk", b0)])
                            P.act(sg[:, 0:bw], banks[b0][:, 0:bw], AF.Sigmoid, [("bk", b0)], ["sg"], bias=s5d[:, 4 + ao:5 + ao])
                            for kt in range(8):
                                P.mm(banks[b1][:, 0:bw], wzs[:, ao, kt, :], hnb[i][:, kt, 0:bw], kt == 0, kt == 7, [("hn", i), "wzs"], [("bk", b1)])
                            P.act(zs[:, 0:bw], banks[b1][:, 0:bw], AF.Silu, [("bk", b1)], ["zs"])
                            P.tt("dve", sg[:, 0:bw], sg[:, 0:bw], yv[:, ao, 0:bw], ALU.mult, ["sg", ("yv", ao)], ["sg"])
                            P.tt("dve", yo[i][:, ao, 0:bw], sg[:, 0:bw], zs[:, 0:bw], ALU.mult, ["sg", "zs"], [("yo", i)])
                        P.dma("sp", env["ys5T"][s, :, :, c0:c0 + bw], yo[i][:, :, 0:bw], r=[("yo", i)])
                P.flush()


def phase_ssd(env, l):
    nc, P, banks = env["nc"], env["P"], env["banks"]
    small, K2 = env["small_in"], env["const2"]
    identf, identb = env["identf"], env["identb"]
    with ExitStack() as lay:
        scw = lay.enter_context(sbt(nc, "scw", [128, 72], F32))
        sal = lay.enter_context(sbt(nc, "sal", [64, 2], F32))
        nega = lay.enter_context(sbt(nc, "nega", [64, 1], F32))
        sdn = lay.enter_context(sbt(nc, "sdn", [128, 1040], F32))
        wz = lay.enter_context(sbt(nc, "wz", [128, 8, 1024], BF16))
        selb = lay.enter_context(sbt(nc, "selb", [64, 32, 128], BF16))
        negm = lay.enter_context(sbt(nc, "negm", [128, 2, 128], BF16))
        lastm = lay.enter_context(sbt(nc, "lastm", [128, 2, 128], F32))
        with ExitStack() as p0:
            stg = p0.enter_context(sbt(nc, "kstg", [128, 4096 + 256], F32))
            P.dma("sp", scw[:], small[l, :, S_SCW:S_SCW + 72], w=["scw"])
            P.dma("sp", sal[:], small[l, 0:64, S_SALOG:S_SALOG + 2], w=["sal"])
            P.dma("sp", sdn[:], small[l, :, S_SD:S_SD + 1040], w=["sdn"])
            P.dma("sp", wz[:], env["wallb"][l, :, W_ZSSD:W_ZSSD + 8192].rearrange("p (k c) -> p k c", k=8), w=["wz"])
            P.dma("sp", stg[:], K2[:, K_SEL:K_SEL + 4096 + 256], w=["kstg"])
            P.dma("sp", lastm[:], K2[:, K_LASTF:K_LASTF + 256].rearrange("p (a c) -> p a c", a=2), w=["lastm"])
            P.cp("dve", selb[:].rearrange("p a c -> p (a c)"), stg[0:64, 0:4096], ["kstg"], ["selb"])
            P.cp("dve", negm[:].rearrange("p a c -> p (a c)"), stg[:, 4096:4352], ["kstg"], ["negm"])
            P.act(nega[:], sal[:, 0:1], AF.Exp, ["sal"], ["nega"])
            P.ts("dve", nega[:], nega[:], -1.0, None, ALU.mult, None, ["nega"], ["nega"])
            P.flush()

        for s in range(env["nseq"]):
            with ExitStack() as ph:
                hn = ph.enter_context(sbt(nc, "shn", [128, 8, LP], BF16))
                xtok = ph.enter_context(sbt(nc, "xtok", [128, NT, 1024], BF16))
                Btok = ph.enter_context(sbt(nc, "Btok", [128, NT, 256], BF16))
                BT = ph.enter_context(sbt(nc, "BT", [128, 2, LP], BF16))
                CT = ph.enter_context(sbt(nc, "CT", [128, 2, LP], BF16))
                atok = ph.enter_context(sbt(nc, "atok", [128, NT, 4, 64], F32))
                ahl = ph.enter_context(sbt(nc, "ahl", [64, 2, LP], BF16))
                for kt in range(8):
                    P.dma("sp", hn[:, kt, :], env["hnT"][s, :, kt, :], w=[("hn", kt)])
                hnk = [("hn", kt) for kt in range(8)]
                with ExitStack() as p1:
                    wcv = p1.enter_context(sbt(nc, "wcv", [128, 12, 8, 128], BF16))
                    raw = [p1.enter_context(sbt(nc, "raw%d" % i, [128, LP + 4], F32)) for i in range(2)]
                    acc1 = p1.enter_context(sbt(nc, "acc", [128, LP], F32))
                    acc = [acc1, acc1]
                    xc1 = p1.enter_context(sbt(nc, "xc", [128, LP], BF16))
                    xc = [xc1, xc1]
                    P.dma("sp", wcv[:], fm_tile(env, l, T_X, 12), w=["wcv"])
                    for i in range(2):
                        P.ms("pool", raw[i][:, 0:2], 0.0, [("raw", i)])
                        P.ms("pool", raw[i][:, LP + 2:LP + 4], 0.0, [("raw", i)])
                    nb_ = 0
                    for ct in range(12):
                        i = ct % 2
                        for (c0, bw) in BLOCKS:
                            bk = nb_ % 4
                            nb_ += 1
                            for kt in range(8):
                                P.mm(banks[bk][:, 0:bw], wcv[:, ct, kt, :], hn[:, kt, c0:c0 + bw], kt == 0, kt == 7, ["wcv"] + hnk, [("bk", bk)])
                            P.cp("act", raw[i][:, 2 + c0:2 + c0 + bw], banks[bk][:, 0:bw], [("bk", bk)], [("raw", i)])
                        P.ts("dve", acc[i][:], raw[i][:, 0:LP], scw[:, ct * 5:ct * 5 + 1], None, ALU.mult, None, [("raw", i), "scw"], ["acc"])
                        for k in range(1, 5):
                            P.stt(acc[i][:], raw[i][:, k:k + LP], scw[:, ct * 5 + k:ct * 5 + k + 1], acc[i][:], ALU.mult, ALU.add,
                                  [("raw", i), "acc", "scw"], ["acc"])
                        if ct < 8:
                            dst, dk = xc[i][:], "xc"
                        elif ct < 10:
                            dst, dk = BT[:, ct - 8, :], ("BT", ct - 8)
                        else:
                            dst, dk = CT[:, ct - 10, :], ("CT", ct - 10)
                        P.act(dst, acc[i][:], AF.Silu, ["acc", "scw"], [dk], bias=scw[:, 60 + ct:61 + ct])
                        P.ms("pool", dst[:, 0:PAD], 0.0, [dk])
                        if ct < 10:
                            for t0 in range(0, NT, 4):
                                n = min(4, NT - t0)
                                bk = 4 + (t0 // 4) % 4
                                bkb = banks[bk][:].bitcast(BF16)
                                for q in range(n):
                                    P.tr(bkb[:, q * 128:(q + 1) * 128], dst[:, (t0 + q) * 128:(t0 + q + 1) * 128], identb, [dk, "cstb"], [("bk", bk)])
                                if ct < 8:
                                    dd = xtok[:, t0:t0 + n, ct * 128:(ct + 1) * 128]
                                    dkk = "xtok"
                                else:
                                    dd = Btok[:, t0:t0 + n, (ct - 8) * 128:(ct - 7) * 128]
                                    dkk = "Btok"
                                P.cp("dve", dd, bkb[:, 0:n * 128].rearrange("p (a c) -> p a c", a=n), [("bk", bk)], [dkk])
                    P.flush()
                with ExitStack() as p2:
                    wdt = p2.enter_context(sbt(nc, "wdt", [128, 8, 128], BF16))
                    dtT = p2.enter_context(sbt(nc, "dtT", [64, LP], F32))
                    dtaT = p2.enter_context(sbt(nc, "dtaT", [64, LP], F32))
                    acT = p2.enter_context(sbt(nc, "acT", [64, LP], F32))
                    rmf = p2.enter_context(sbt(nc, "rmf", [64, LP], F32))
                    rmb = p2.enter_context(sbt(nc, "rmb", [64, LP], F32))
                    P.dma("sp", wdt[:], fm_tile(env, l, T_DT, 1)[:, 0], w=["wdt"])
                    for bi, (c0, bw) in enumerate(BLOCKS):
                        bk = bi % 4
                        for kt in range(8):
                            P.mm(banks[bk][0:64, 0:bw], wdt[:, kt, 0:64], hn[:, kt, c0:c0 + bw], kt == 0, kt == 7, ["wdt"], [("bk", bk)])
                        P.act(dtT[:, c0:c0 + bw], banks[bk][0:64, 0:bw], AF.Exp, [("bk", bk)], ["dtT"], bias=sal[:, 1:2])
                        P.act(dtT[:, c0:c0 + bw], dtT[:, c0:c0 + bw], AF.Ln, ["dtT"], ["dtT"], bias=1.0)
                    P.ms("dve", dtT[:, 0:PAD], 0.0, ["dtT"])
                    P.ts("dve", dtaT[:], dtT[:], nega[:, 0:1], None, ALU.mult, None, ["dtT"], ["dtaT"])
                    P.ms("pool", rmf[:], 1.0, ["rmf"])
                    P.ms("pool", rmf[:, 0:LP:128], 0.0, ["rmf"])
                    P.ms("pool", rmb[:], 1.0, ["rmb"])
                    P.ms("pool", rmb[:, 127:LP:128], 0.0, ["rmb"])
                    P.scan(acT[0:32, :], rmf[0:32, :], dtaT[0:32, :], 0.0, ALU.mult, ALU.add, ["rmf", "dtaT"], ["acT0"])
                    P.scan(acT[32:64, ::-1], rmb[32:64, ::-1], dtaT[32:64, ::-1], 0.0, ALU.mult, ALU.add, ["rmb", "dtaT"], ["acT1"])
                    P.cp("dve", ahl[:, 0, :], acT[:], ["acT0", "acT1"], ["ahl0"])
                    P.tt("dve", ahl[:, 1, :], acT[:], ahl[:, 0, :], ALU.subtract, ["acT0", "acT1", "ahl0"], ["ahl1"])
                    for tt_ in range(NT):
                        bk = 4 + tt_ % 4
                        P.tr(banks[bk][:, 0:64], acT[:, tt_ * 128:(tt_ + 1) * 128], identf[0:64, 0:64], ["acT0", "acT1", "cst"], [("bk", bk)])
                        P.tr(banks[bk][:, 64:128], dtT[:, tt_ * 128:(tt_ + 1) * 128], identf[0:64, 0:64], ["dtT", "cst"], [("bk", bk)])
                        P.cp("act", atok[:, tt_, 0, :], banks[bk][:, 0:64], [("bk", bk)], ["atok0"])
                        P.cp("act", atok[:, tt_, 3, :], banks[bk][:, 64:128], [("bk", bk)], ["atok3"])
                    P.ts("dve", atok[:, :, 1, :], atok[:, :, 0, :], -1.0, None, ALU.mult, None, ["atok0"], ["atok1"])
                    P.act(atok[:, :, 2, :], atok[:, :, 0, :], AF.Exp, ["atok0"], ["atok2"])
                    P.flush()
                with ExitStack() as p3:
                    S = p3.enter_context(sbt(nc, "S", [128, 1024], F32))
                    Sb = p3.enter_context(sbt(nc, "Sb", [128, 1024], BF16))
                    xdt = p3.enter_context(sbt(nc, "xdt", [128, 1024], BF16))
                    xdtt = p3.enter_context(sbt(nc, "xdtt", [128, 1024], BF16))
                    E = [p3.enter_context(sbt(nc, "E%d" % i, [128, 128], BF16)) for i in range(2)]
                    M = [p3.enter_context(sbt(nc, "M%d" % i, [128, 128], BF16)) for i in range(2)]
                    tl = p3.enter_context(sbt(nc, "tl", [128, 2, 16], F32))
                    tY = p3.enter_context(sbt(nc, "tY", [128, 1024], F32))
                    yv = p3.enter_context(sbt(nc, "yv", [128, 1024], F32))
                    yfb = p3.enter_context(sbt(nc, "yfb", [128, 1024], BF16))
                    zs = p3.enter_context(sbt(nc, "zs", [128, 1024], F32))
                    junk = p3.enter_context(sbt(nc, "junk", [128, 512], F32))
                    ss = p3.enter_context(sbt(nc, "ss", [128, 2], F32))
                    yn = p3.enter_context(sbt(nc, "yn", [128, 1024], BF16))
                    yTs = p3.enter_context(sbt(nc, "yTs", [128, 8, 128], BF16))
                    h3 = lambda ap, nh: ap.rearrange("p (h c) -> p h c", h=nh)
                    bc = lambda ap, nh: ap.unsqueeze(2).broadcast_to([128, nh, 64])
                    for d in range(2):
                        rb = d * 32
                        P.ms("pool", S[:], 0.0, ["S0", "S1"])
                        P.ms("pool", Sb[:], 0.0, ["Sb0", "Sb1"])
                        order = list(range(NT)) if d == 0 else list(range(NT - 1, -1, -1))
                        for c in order:
                            cs, ce = c * 128, (c + 1) * 128
                            for g in range(2):
                                P.mm(banks[0][:, g * 128:(g + 1) * 128], BT[:, g, cs:ce], CT[:, g, cs:ce], True, True, [], [("bk", 0)])
                            P.mm(banks[0][:, 256:272], lastm[:, d, :], atok[:, c, 0, rb:rb + 16], True, True, [], [("bk", 0)])
                            P.tt("dve", tl[:, 0, :], banks[0][:, 256:272], atok[:, c, 0, rb:rb + 16], ALU.subtract, [("bk", 0)], ["tl0"])
                            P.act(tl[:, 0, :], tl[:, 0, :], AF.Exp, ["tl0"], ["tl0"])
                            P.act(tl[:, 1, :], banks[0][:, 256:272], AF.Exp, [("bk", 0)], ["tl1"])
                            P.tt("dve", h3(xdt[:], 16), h3(xtok[:, c, :], 16), bc(atok[:, c, 3, rb:rb + 16], 16), ALU.mult, [], ["xdt"])

                            def dp_mm(h):
                                e = h % 2
                                dpb = 1 if e == 0 else 6
                                dp = banks[dpb][:, 0:128]
                                P.mm(dp, selb[:, d * 16 + h, :], ahl[:, 0, cs:ce], True, False, [], [("bk", dpb)])
                                P.mm(dp, selb[:, d * 16 + h, :], ahl[:, 1, cs:ce], False, False, [], [("bk", dpb)])
                                P.mm(dp, identb, negm[:, d, :], False, True, [], [("bk", dpb)])

                            dp_mm(0)
                            dp_mm(1)
                            for h in range(16):
                                g, e = h // 8, h % 2
                                dpb = 1 if e == 0 else 6
                                P.act(E[e][:], banks[dpb][:, 0:128], AF.Exp, [("bk", dpb)], [("E", e)], bias=atok[:, c, 1, rb + h:rb + h + 1])
                                P.tt("dve", M[e][:], E[e][:], banks[0][:, g * 128:(g + 1) * 128], ALU.mult, [("E", e), ("bk", 0)], [("M", e)])
                                if h + 2 < 16:
                                    dp_mm(h + 2)
                                P.mm(banks[2 + g][:, (h % 8) * 64:(h % 8 + 1) * 64], M[e][:], xdt[:, h * 64:(h + 1) * 64], True, True,
                                     [("M", e), "xdt"], [("bk", 2 + g)])
                            for g in range(2):
                                P.mm(banks[4 + g][:, :], CT[:, g, cs:ce], Sb[:, g * 512:(g + 1) * 512], True, True, ["Sb%d" % g], [("bk", 4 + g)])
                            P.tt("dve", h3(xdtt[:], 16), h3(xdt[:], 16), bc(tl[:, 0, :], 16), ALU.mult, ["xdt", "tl0"], ["xdtt"])
                            ea = atok[:, c, 2, rb:rb + 16]
                            for g in range(2):
                                gs = slice(g * 512, (g + 1) * 512)
                                P.tt("dve", h3(tY[:, gs], 8), h3(banks[4 + g][:, :], 8), bc(ea[:, g * 8:(g + 1) * 8], 8), ALU.mult,
                                     [("bk", 4 + g)], [("tY", g)])
                                if d == 0:
                                    P.tt("dve", yfb[:, gs], tY[:, gs], banks[2 + g][:, :], ALU.add, [("tY", g), ("bk", 2 + g)], ["yfb"])
                                else:
                                    P.tt("dve", yv[:, gs], tY[:, gs], banks[2 + g][:, :], ALU.add, [("tY", g), ("bk", 2 + g)], [("yv", g)])
                            if d == 0:
                                P.dma("sp", env["yfD"][s, c], yfb[:], r=["yfb"])
                            for g in range(2):
                                gs = slice(g * 512, (g + 1) * 512)
                                P.mm(banks[4 + g][:, :], Btok[:, c, g * 128:(g + 1) * 128], xdtt[:, gs], True, True, ["xdtt"], [("bk", 4 + g)])
                                P.tt("dve", h3(S[:, gs], 8), h3(S[:, gs], 8), bc(tl[:, 1, g * 8:(g + 1) * 8], 8), ALU.mult, ["S%d" % g, "tl1"], ["S%d" % g])
                                P.tt("dve", S[:, gs], S[:, gs], banks[4 + g][:, :], ALU.add, ["S%d" % g, ("bk", 4 + g)], ["S%d" % g])
                                P.cp("act", Sb[:, gs], S[:, gs], ["S%d" % g], ["Sb%d" % g])
                            if d == 1:
                                P.dma("sp", yfb[:], env["yfD"][s, c], w=["yfb"])
                                P.tt("dve", yv[:], yv[:], yfb[:], ALU.add, [("yv", 0), ("yv", 1), "yfb"], ["yvv"])
                                P.tt("dve", h3(tY[:], 16), h3(xtok[:, c, :], 16), bc(sdn[:, 0:16], 16), ALU.mult, [("tY", 0), ("tY", 1)], ["tYd"])
                                P.tt("dve", yv[:], yv[:], tY[:], ALU.add, ["yvv", "tYd"], ["yvv"])
                                for b in range(2):
                                    for kt in range(8):
                                        P.mm(banks[6 + b][:, :], hn[:, kt, cs:ce], wz[:, kt, b * 512:(b + 1) * 512], kt == 0, kt == 7, [], [("bk", 6 + b)])
                                    P.act(zs[:, b * 512:(b + 1) * 512], banks[6 + b][:, :], AF.Silu, [("bk", 6 + b)], [("zs", b)])
                                P.tt("dve", yv[:], yv[:], zs[:], ALU.mult, ["yvv", ("zs", 0), ("zs", 1)], ["yvv"])
                                for g in range(2):
                                    P.act(junk[:], yv[:, g * 512:(g + 1) * 512], AF.Square, ["yvv"], ["junk"], accum_out=ss[:, g:g + 1])
                                P.act(ss[:], ss[:], AF.Sqrt, ["junk"], ["ss"], bias=EPS, scale=1.0 / 512)
                                P.recip(ss[:], ss[:], ["ss"], ["ss"])
                                for g in range(2):
                                    gs = slice(g * 512, (g + 1) * 512)
                                    P.stt(yn[:, gs], yv[:, gs], ss[:, g:g + 1], sdn[:, 16 + g * 512:16 + (g + 1) * 512], ALU.mult, ALU.mult,
                                          ["yvv", "ss"], ["yn"])
                                bkb = banks[6][:].bitcast(BF16)
                                for q in range(8):
                                    P.tr(bkb[:, q * 128:(q + 1) * 128], yn[:, q * 128:(q + 1) * 128], identb, ["yn"], [("bk", 6)])
                                P.cp("act", yTs[:].rearrange("p a c -> p (a c)"), bkb[:, :], [("bk", 6)], ["yTs"])
                                P.dma("sp", env["yssdT"][s, :, :, cs:ce], yTs[:], r=["yTs"])
                    P.flush()


def phase_gdn(env, l):
    nc, P, banks = env["nc"], env["P"], env["banks"]
    small, K2 = env["small_in"], env["const2"]
    identf, identb, onesb = env["identf"], env["identb"], env["onesb"]
    with ExitStack() as lay:
        gcw = lay.enter_context(sbt(nc, "gcw", [128, 60], F32))
        gal = lay.enter_context(sbt(nc, "gal", [64, 2], F32))
        negg = lay.enter_context(sbt(nc, "negg", [64, 1], F32))
        gnw = lay.enter_context(sbt(nc, "gnw", [128, 128], F32))
        wzg = lay.enter_context(sbt(nc, "wzg", [128, 8, 512], BF16))
        selg = lay.enter_context(sbt(nc, "selg", [64, 8, 128], F32))
        selgb = lay.enter_context(sbt(nc, "selgb", [64, 8, 128], BF16))
        gmk = lay.enter_context(sbt(nc, "gmk", [128, 4, 128], BF16))
        glast = lay.enter_context(sbt(nc, "glast", [128, 2, 128], F32))
        gch = lay.enter_context(sbt(nc, "gch", [128, 4, 128], F32))
        with ExitStack() as p0:
            stg = p0.enter_context(sbt(nc, "gstg", [128, 512], F32))
            P.dma("sp", gcw[:], small[l, :, S_GCW:S_GCW + 60], w=["gcw"])
            P.dma("sp", gal[:], small[l, 0:64, S_GALOG:S_GALOG + 2], w=["gal"])
            P.dma("sp", gnw[:], small[l, :, S_GNW:S_GNW + 128], w=["gnw"])
            P.dma("sp", wzg[:], env["wallb"][l, :, W_ZGDN:W_ZGDN + 4096].rearrange("p (k c) -> p k c", k=8), w=["wzg"])
            P.dma("sp", selg[:].rearrange("p a c -> p (a c)"), K2[0:64, K_SELG:K_SELG + 1024], w=["selg"])
            P.dma("sp", stg[:], K2[:, K_GNEGF:K_GNEGF + 512], w=["gstg"])
            P.dma("sp", glast[:].rearrange("p a c -> p (a c)"), K2[:, K_GLASTF:K_GLASTF + 256], w=["glast"])
            P.dma("sp", gch[:].rearrange("p a c -> p (a c)"), K2[:, K_GCHF:K_GCHF + 512], w=["gch"])
            P.cp("dve", selgb[:], selg[:], ["selg"], ["selgb"])
            P.cp("dve", gmk[:].rearrange("p a c -> p (a c)"), stg[:], ["gstg"], ["gmk"])
            P.act(negg[:], gal[:, 0:1], AF.Exp, ["gal"], ["negg"])
            P.ts("dve", negg[:], negg[:], -1.0, None, ALU.mult, None, ["negg"], ["negg"])
            P.flush()

        for s in range(env["nseq"]):
            with ExitStack() as ph:
                hn = ph.enter_context(sbt(nc, "ghn", [128, 8, LP], BF16))
                qkv = ph.enter_context(sbt(nc, "qkv", [128, 12, LP], BF16))
                gtok = ph.enter_context(sbt(nc, "gtok", [128, NT, 4, 64], F32))
                ghl = ph.enter_context(sbt(nc, "ghl", [64, 2, LP], BF16))
                gcT = ph.enter_context(sbt(nc, "gcT", [64, LP], F32))
                betaT = ph.enter_context(sbt(nc, "betaT", [64, LP], F32))
                betab = ph.enter_context(sbt(nc, "betab", [64, LP], BF16))
                for kt in range(8):
                    P.dma("sp", hn[:, kt, :], env["hnT"][s, :, kt, :], w=[("hn", kt)])
                hnk = [("hn", kt) for kt in range(8)]
                with ExitStack() as p1:
                    wcv = p1.enter_context(sbt(nc, "gwcv", [128, 12, 8, 128], BF16))
                    raw0 = p1.enter_context(sbt(nc, "graw", [128, LP + 4], F32))
                    raw = [raw0, raw0]
                    acc = p1.enter_context(sbt(nc, "gacc", [128, LP], F32))
                    sq = p1.enter_context(sbt(nc, "gsq", [128, LP], BF16))
                    rs = p1.enter_context(sbt(nc, "grs", [128, 512], F32))
                    P.dma("sp", wcv[:], fm_tile(env, l, T_Q, 12), w=["wcv"])
                    for i in range(2):
                        P.ms("pool", raw[i][:, 0:2], 0.0, ["raw"])
                        P.ms("pool", raw[i][:, LP + 2:LP + 4], 0.0, ["raw"])
                    nb_ = 0
                    for ct in range(12):
                        i = ct % 2
                        for (c0, bw) in BLOCKS:
                            bk = nb_ % 4
                            nb_ += 1
                            for kt in range(8):
                                P.mm(banks[bk][:, 0:bw], wcv[:, ct, kt, :], hn[:, kt, c0:c0 + bw], kt == 0, kt == 7, ["wcv"] + hnk, [("bk", bk)])
                            P.cp("act", raw[i][:, 2 + c0:2 + c0 + bw], banks[bk][:, 0:bw], [("bk", bk)], ["raw"])
                        P.ts("dve", acc[:], raw[i][:, 0:LP], gcw[:, ct * 5:ct * 5 + 1], None, ALU.mult, None, ["raw", "gcw"], ["acc"])
                        for k in range(1, 5):
                            P.stt(acc[:], raw[i][:, k:k + LP], gcw[:, ct * 5 + k:ct * 5 + k + 1], acc[:], ALU.mult, ALU.add,
                                  ["raw", "acc", "gcw"], ["acc"])
                        dst, dk = qkv[:, ct, :], ("qkv", ct)
                        P.act(dst, acc[:], AF.Silu, ["acc"], [dk])
                        P.ms("pool", dst[:, 0:PAD], 0.0, [dk])
                        if ct < 8:
                            P.tt("pool", sq[:], dst, dst, ALU.mult, [dk], ["sq"])
                            for bi, (c0, bw) in enumerate(BLOCKS):
                                bk = 4 + bi % 4
                                P.mm(banks[bk][:, 0:bw], onesb, sq[:, c0:c0 + bw], True, True, ["sq", "cstb"], [("bk", bk)])
                                P.act(rs[:, 0:bw], banks[bk][:, 0:bw], AF.Sqrt, [("bk", bk)], ["rs"], bias=EPS)
                                P.recip(rs[:, 0:bw], rs[:, 0:bw], ["rs"], ["rs"])
                                P.stt(dst[:, c0:c0 + bw], dst[:, c0:c0 + bw], float(128 ** -0.5) if ct < 4 else 1.0, rs[:, 0:bw], ALU.mult, ALU.mult,
                                      [dk, "rs"], [dk])
                    P.flush()
                if DBG.get("gdn_stop") == 1:
                    continue
                with ExitStack() as p2:
                    wba = p2.enter_context(sbt(nc, "wba", [128, 2, 8, 128], BF16))
                    gT = p2.enter_context(sbt(nc, "gT", [64, LP], F32))
                    rmf = p2.enter_context(sbt(nc, "grmf", [64, LP], F32))
                    rmb = p2.enter_context(sbt(nc, "grmb", [64, LP], F32))
                    P.dma("sp", wba[:], fm_tile(env, l, T_BETA, 2), w=["wba"])
                    for bi, (c0, bw) in enumerate(BLOCKS):
                        b0, b1 = bi % 2, 2 + bi % 2
                        for kt in range(8):
                            P.mm(banks[b0][0:64, 0:bw], wba[:, 0, kt, 0:64], hn[:, kt, c0:c0 + bw], kt == 0, kt == 7, ["wba"], [("bk", b0)])
                        P.act(betaT[:, c0:c0 + bw], banks[b0][0:64, 0:bw], AF.Sigmoid, [("bk", b0)], ["betaT"])
                        for kt in range(8):
                            P.mm(banks[b1][0:64, 0:bw], wba[:, 1, kt, 0:64], hn[:, kt, c0:c0 + bw], kt == 0, kt == 7, ["wba"], [("bk", b1)])
                        P.act(gT[:, c0:c0 + bw], banks[b1][0:64, 0:bw], AF.Exp, [("bk", b1)], ["gT"], bias=gal[:, 1:2])
                        P.act(gT[:, c0:c0 + bw], gT[:, c0:c0 + bw], AF.Ln, ["gT"], ["gT"], bias=1.0)
                    P.ms("dve", betaT[:, 0:PAD], 0.0, ["betaT"])
                    P.cp("dve", betab[:], betaT[:], ["betaT"], ["betab"])
                    P.ms("dve", gT[:, 0:PAD], 0.0, ["gT"])
                    P.ts("dve", gT[:], gT[:], negg[:, 0:1], None, ALU.mult, None, ["gT"], ["gT"])
                    P.ms("pool", rmf[:], 1.0, ["rmf"])
                    P.ms("pool", rmf[:, 0:LP:64], 0.0, ["rmf"])
                    P.ms("pool", rmb[:], 1.0, ["rmb"])
                    P.ms("pool", rmb[:, 63:LP:64], 0.0, ["rmb"])
                    P.scan(gcT[0:32, :], rmf[0:32, :], gT[0:32, :], 0.0, ALU.mult, ALU.add, ["rmf", "gT"], ["gcT0"])
                    P.scan(gcT[32:64, ::-1], rmb[32:64, ::-1], gT[32:64, ::-1], 0.0, ALU.mult, ALU.add, ["rmb", "gT"], ["gcT1"])
                    P.cp("dve", ghl[:, 0, :], gcT[:], ["gcT0", "gcT1"], ["ghl0"])
                    P.tt("dve", ghl[:, 1, :], gcT[:], ghl[:, 0, :], ALU.subtract, ["gcT0", "gcT1", "ghl0"], ["ghl1"])
                    for tt_ in range(NT):
                        bk = 4 + tt_ % 4
                        P.tr(banks[bk][:, 0:64], gcT[:, tt_ * 128:(tt_ + 1) * 128], identf[0:64, 0:64], ["gcT0", "gcT1"], [("bk", bk)])
                        P.tr(banks[bk][:, 64:128], betaT[:, tt_ * 128:(tt_ + 1) * 128], identf[0:64, 0:64], ["betaT"], [("bk", bk)])
                        P.cp("act", gtok[:, tt_, 0:2, :], banks[bk][:, 0:128].rearrange("p (a c) -> p a c", a=2), [("bk", bk)], ["gtok01"])
                    P.act(gtok[:, :, 2, :], gtok[:, :, 0, :], AF.Exp, ["gtok01"], ["gtok2"])
                    P.tt("dve", gtok[:, :, 2, :], gtok[:, :, 2, :], gtok[:, :, 1, :], ALU.mult, ["gtok2", "gtok01"], ["gtok2"])
                    P.ts("dve", gtok[:, :, 3, :], gtok[:, :, 0, :], -1.0, None, ALU.mult, None, ["gtok01"], ["gtok3"])
                    P.flush()
                if DBG.get("gdn_stop") == 2:
                    continue
                with ExitStack() as p3:
                    obuf = p3.enter_context(sbt(nc, "obuf", [128, NT, 512], BF16))
                    S = p3.enter_context(sbt(nc, "gS", [128, 4, 128], F32))
                    Sb = p3.enter_context(sbt(nc, "gSb", [128, 4, 128], BF16))
                    ekt = p3.enter_context(sbt(nc, "ekt", [128, 4], F32))
                    dec = p3.enter_context(sbt(nc, "dec", [128, 8], F32))
                    osum = p3.enter_context(sbt(nc, "osum", [128, 4, 128], F32))
                    zs = p3.enter_context(sbt(nc, "gzs", [128, 512], F32))
                    junk = p3.enter_context(sbt(nc, "gjunk", [128, 128], F32))
                    ss = p3.enter_context(sbt(nc, "gss", [128, 4], F32))
                    yn = p3.enter_context(sbt(nc, "gyn", [128, 512], BF16))
                    yTs = p3.enter_context(sbt(nc, "gyTs", [128, 4, 128], BF16))
                    T_ = []
                    if DBG.get("gdn_padalloc"):
                        padt = p3.enter_context(sbt(nc, "gpad", [128, DBG["gdn_padalloc"]], F32))
                    for h in range(4):
                        t = {}
                        for nm, dt_ in (("egb", F32), ("kbT", BF16), ("qgT", BF16), ("vb", BF16), ("kbeg", BF16), ("ktok", BF16),
                                        ("Ei", BF16), ("attnT", BF16), ("Es", BF16), ("N0", BF16), ("N1", BF16), ("NT0", BF16), ("NT1", BF16),
                                        ("R32", F32), ("Rb", BF16), ("u32", F32), ("wT", BF16), ("vnew0", BF16), ("vnew1", BF16), ("av", F32)):
                            t[nm] = p3.enter_context(sbt(nc, "g%s%d" % (nm, h), [128, 128], dt_))
                            if DBG.get("addr"):
                                print("ALLOC", nm, h, t[nm], nc.sbuf_bytes_remaining)
                        P.ms("pool", t["vnew0"][:], 0.0, [("vnew0", h)])
                        P.ms("pool", t["vnew1"][:], 0.0, [("vnew1", h)])
                        T_.append(t)

                    def R(h, i):
                        return banks[2 * h + i // 4][:, (i % 4) * 128:(i % 4 + 1) * 128]

                    def Rb16(h, i):
                        return banks[2 * h + i // 4][:].bitcast(BF16)[:, (i % 4) * 256:(i % 4 + 1) * 256]

                    def chain(d, c, h):
                        t = T_[h]
                        rb = d * 32
                        hrow = rb + h
                        cs, ce = c * 128, (c + 1) * 128
                        rk = lambda i: ("bk", 2 * h + i // 4)
                        tk = lambda nm: (nm, h)
                        qT, kT, vT = qkv[:, h, cs:ce], qkv[:, 4 + h, cs:ce], qkv[:, 8 + h, cs:ce]
                        P.mm(R(h, 0), selgb[:, d * 4 + h, :], betab[:, cs:ce], True, True, [], [rk(0)])
                        P.mm(R(h, 1), selgb[:, d * 4 + h, :], ghl[:, 0, cs:ce], True, False, [], [rk(1)])
                        P.mm(R(h, 1), selgb[:, d * 4 + h, :], ghl[:, 1, cs:ce], False, True, [], [rk(1)])
                        r2b = Rb16(h, 2)
                        P.tr(r2b[:, 0:128], kT, identb, [], [rk(2)])
                        P.tr(r2b[:, 128:256], vT, identb, [], [rk(2)])
                        yield
                        nsub = DBG.get("gdn_sub", 6)
                        if nsub > 0:
                            P.act(t["egb"][:], R(h, 1), AF.Exp, [rk(1)], [tk("egb")])
                        if nsub > 1:
                            P.cp("act", t["av"][:], R(h, 0), [rk(0)], [tk("av")])
                            P.tt("dve", t["kbT"][:], kT, t["av"][:], ALU.mult, [tk("av")], [tk("kbT")])
                        if nsub > 2:
                            P.tt("dve", t["qgT"][:], qT, t["egb"][:], ALU.mult, [tk("egb")], [tk("qgT")])
                        if nsub > 3:
                            P.ts("dve", t["vb"][:], r2b[:, 128:256], gtok[:, c, 1, hrow:hrow + 1], None, ALU.mult, None, [rk(2)], [tk("vb")])
                        if nsub > 4:
                            P.ts("dve", t["kbeg"][:], r2b[:, 0:128], gtok[:, c, 2, hrow:hrow + 1], None, ALU.mult, None, [rk(2)], [tk("kbeg")])
                        if nsub > 5:
                            P.ts("dve", t["ktok"][:], r2b[:, 0:128], ekt[:, h:h + 1], None, ALU.mult, None, [rk(2), "ekt"], [tk("ktok")])
                        yield
                        P.mm(R(h, 3), kT, t["kbT"][:], True, True, [tk("kbT")], [rk(3)])
                        P.mm(R(h, 4), kT, qT, True, True, [], [rk(4)])
                        P.mm(R(h, 5), selgb[:, d * 4 + h, :], ghl[:, 0, cs:ce], True, False, [], [rk(5)])
                        P.mm(R(h, 5), selgb[:, d * 4 + h, :], ghl[:, 1, cs:ce], False, False, [], [rk(5)])
                        P.mm(R(h, 5), identb, gmk[:, d, :], False, True, [], [rk(5)])
                        yield
                        P.act(t["Ei"][:], R(h, 5), AF.Exp, [rk(5)], [tk("Ei")], bias=gtok[:, c, 3, hrow:hrow + 1])
                        P.tt("dve", t["attnT"][:], R(h, 4), t["Ei"][:], ALU.mult, [rk(4), tk("Ei")], [tk("attnT")])
                        P.tt("pool", t["Es"][:], t["Ei"][:], gmk[:, 2 + d, :], ALU.mult, [tk("Ei")], [tk("Es")])
                        P.stt(t["N0"][:], R(h, 3), -1.0, t["Es"][:], ALU.mult, ALU.mult, [rk(3), tk("Es")], [tk("N0")])
                        yield
                        r0b = Rb16(h, 0)
                        P.tr(r0b[:, 0:128], t["N0"][:], identb, [tk("N0")], [rk(0)])
                        P.tt("dve", t["R32"][:], t["N0"][:], identf, ALU.add, [tk("N0")], [tk("R32")])
                        P.cp("pool", t["Rb"][:], t["R32"][:], [tk("R32")], [tk("Rb")])
                        yield
                        P.cp("act", t["NT0"][:], r0b[:, 0:128], [rk(0)], [tk("NT0")])
                        yield
                        for lvl in range(1, 6):
                            a, b = (lvl - 1) % 2, lvl % 2
                            Na, NTa, Nb, NTb = t["N%d" % a], t["NT%d" % a], t["N%d" % b], t["NT%d" % b]
                            if lvl < 5:
                                P.mm(R(h, 1), NTa[:], Na[:], True, True, [tk("N%d" % a), tk("NT%d" % a)], [rk(1)])
                            P.mm(R(h, 2), Na[:], NTa[:], True, True, [tk("N%d" % a), tk("NT%d" % a)], [rk(2)])
                            yield
                            if lvl < 5:
                                P.cp("act", Nb[:], R(h, 1), [rk(1)], [tk("N%d" % b)])
                            P.cp("dve", NTb[:], R(h, 2), [rk(2)], [tk("NT%d" % b)])
                            yield
                            P.mm(R(h, 6), NTb[:], t["Rb"][:], True, True, [tk("NT%d" % b), tk("Rb")], [rk(6)])
                            yield
                            P.tt("dve", t["R32"][:], t["R32"][:], R(h, 6), ALU.add, [rk(6), tk("R32")], [tk("R32")])
                            P.cp("act", t["Rb"][:], t["R32"][:], [tk("R32")], [tk("Rb")])
                            yield
                        P.mm(R(h, 7), t["Rb"][:], t["vb"][:], True, True, [tk("Rb"), tk("vb")], [rk(7)])
                        P.mm(R(h, 0), t["kbeg"][:], t["Rb"][:], True, True, [tk("Rb"), tk("kbeg")], [rk(0)])
                        yield
                        P.cp("act", t["u32"][:], R(h, 7), [rk(7)], [tk("u32")])
                        P.cp("dve", t["wT"][:], R(h, 0), [rk(0)], [tk("wT")])
                        yield
                        for sc in ((0, 1) if d == 0 else (1, 0)):
                            rows = slice(64 * sc, 64 * sc + 64)
                            P.mm(R(h, 1), t["wT"][:], Sb[:, h, :], True, True, [tk("wT"), ("Sb", h)], [rk(1)])
                            P.mm(R(h, 2 + sc), t["qgT"][:], Sb[:, h, :], True, True, [tk("qgT"), ("Sb", h)], [rk(2 + sc)])
                            yield
                            vn = "vnew%d" % sc
                            P.tt("dve", t[vn][rows, :], t["u32"][rows, :], R(h, 1)[rows, :], ALU.subtract, [tk("u32"), rk(1)], [tk(vn)])
                            yield
                            P.mm(R(h, 4), t["ktok"][:], t[vn][:], True, True, [tk("ktok"), tk(vn)], [rk(4)])
                            yield
                            P.stt(S[:, h, :], S[:, h, :], dec[:, sc * 4 + h:sc * 4 + h + 1], R(h, 4), ALU.mult, ALU.add, [rk(4), ("S", h), "dec"], [("S", h)])
                            P.cp("act", Sb[:, h, :], S[:, h, :], [("S", h)], [("Sb", h)])
                            yield
                        P.mm(R(h, 5), t["attnT"][:], t["vnew0"][:], True, False, [tk("attnT"), tk("vnew0")], [rk(5)])
                        P.mm(R(h, 5), t["attnT"][:], t["vnew1"][:], False, True, [tk("attnT"), tk("vnew1")], [rk(5)])
                        yield
                        P.cp("act", t["av"][:], R(h, 5), [rk(5)], [tk("av")])
                        for sc in range(2):
                            rows = slice(64 * sc, 64 * sc + 64)
                            if d == 0:
                                P.tt("dve", obuf[rows, c, h * 128:(h + 1) * 128], R(h, 2 + sc)[rows, :], t["av"][rows, :], ALU.add,
                                     [rk(2 + sc), tk("av")], [("obuf", h)])
                            else:
                                P.tt("dve", osum[rows, h, :], R(h, 2 + sc)[rows, :], t["av"][rows, :], ALU.add, [rk(2 + sc), tk("av")], [("osum", h)])
                        if d == 1:
                            P.tt("dve", osum[:, h, :], osum[:, h, :], obuf[:, c, h * 128:(h + 1) * 128], ALU.add, [("osum", h)], [("osum", h)])
                        yield

                    for d in range(2):
                        rb = d * 32
                        P.ms("pool", S[:], 0.0, [("S", h) for h in range(4)])
                        P.ms("pool", Sb[:], 0.0, [("Sb", h) for h in range(4)])
                        order = list(range(NT)) if d == 0 else list(range(NT - 1, -1, -1))
                        for c in order[:DBG.get("gdn_tiles", NT)]:
                            cs, ce = c * 128, (c + 1) * 128
                            r06 = R(0, 6)
                            P.mm(r06[:, 0:16], glast[:, d, :], gtok[:, c, 0, rb:rb + 16], True, True, [], [("bk", 1)])
                            for sc in range(2):
                                P.mm(r06[:, 16 + 16 * sc:32 + 16 * sc], gch[:, d * 2 + sc, :], gtok[:, c, 0, rb:rb + 16], True, True, [], [("bk", 1)])
                            P.tt("dve", ekt[:], r06[:, 0:4], gtok[:, c, 0, rb:rb + 4], ALU.subtract, [("bk", 1)], ["ekt"])
                            P.act(ekt[:], ekt[:], AF.Exp, ["ekt"], ["ekt"])
                            P.act(dec[:].rearrange("p (a c) -> p a c", a=2), r06[:, 16:48].rearrange("p (a c) -> p a c", a=2)[:, :, 0:4], AF.Exp, [("bk", 1)], ["dec"])
                            gens = [chain(d, c, h) for h in range(DBG.get("gdn_heads", 4))]
                            nst = 0
                            while gens and nst < DBG.get("gdn_stage", 10 ** 9):
                                nst += 1
                                for g in list(gens):
                                    try:
                                        next(g)
                                    except StopIteration:
                                        gens.remove(g)
                            if d == 1 and not DBG.get("gdn_noepi"):
                                r0k = [("bk", 0)]
                                for kt in range(8):
                                    P.mm(banks[0][:, :], hn[:, kt, cs:ce], wzg[:, kt, :], kt == 0, kt == 7, [], r0k)
                                P.act(zs[:], banks[0][:, :], AF.Silu, r0k, ["zs"])
                                for h in range(4):
                                    P.act(junk[:], osum[:, h, :], AF.Square, [("osum", h)], ["junk"], accum_out=ss[:, h:h + 1])
                                P.act(ss[:], ss[:], AF.Sqrt, ["junk"], ["ss"], bias=EPS, scale=1.0 / 128)
                                P.recip(ss[:], ss[:], ["ss"], ["ss"])
                                for h in range(4):
                                    P.stt(osum[:, h, :], osum[:, h, :], ss[:, h:h + 1], gnw[:], ALU.mult, ALU.mult, [("osum", h), "ss"], [("osum", h)])
                                P.tt("dve", yn[:], osum[:].rearrange("p a c -> p (a c)"), zs[:], ALU.mult, [("osum", h) for h in range(4)] + ["zs"], ["yn"])
                                bkb = banks[2][:].bitcast(BF16)
                                r1k = [("bk", 2)]
                                for q in range(4):
                                    P.tr(bkb[:, q * 128:(q + 1) * 128], yn[:, q * 128:(q + 1) * 128], identb, ["yn"], r1k)
                                P.cp("act", yTs[:].rearrange("p a c -> p (a c)"), bkb[:, 0:512], r1k, ["yTs"])
                                P.dma("sp", env["ygdnT"][s, :, :, cs:ce], yTs[:], r=["yTs"])
                    P.flush()


def phase_merge(env, l):
    nc, P, banks = env["nc"], env["P"], env["banks"]
    wallb = env["wallb"]
    with ExitStack() as lay:
        wg = lay.enter_context(sbt(nc, "wg", [128, 24, 8, 128], BF16))
        wo5 = lay.enter_context(sbt(nc, "wo5", [128, 4, 1024], BF16))
        wos = lay.enter_context(sbt(nc, "wos", [128, 8, 1024], BF16))
        wog = lay.enter_context(sbt(nc, "wog", [128, 4, 1024], BF16))
        wout = lay.enter_context(sbt(nc, "wout", [128, 8, 1024], BF16))
        for t0 in range(0, 24, 8):
            P.dma("sp", wg[:, t0:t0 + 8], fm_tile(env, l, T_GATE + t0, 8), w=["wg"])
        P.dma("sp", wo5[:], wallb[l, :, W_S5O:W_S5O + 4096].rearrange("p (k c) -> p k c", k=4), w=["wo5"])
        P.dma("sp", wos[:], wallb[l, :, W_SSDO:W_SSDO + 8192].rearrange("p (k c) -> p k c", k=8), w=["wos"])
        P.dma("sp", wog[:], wallb[l, :, W_GDNO:W_GDNO + 4096].rearrange("p (k c) -> p k c", k=4), w=["wog"])
        P.dma("sp", wout[:], wallb[l, :, W_OUT:W_OUT + 8192].rearrange("p (k c) -> p k c", k=8), w=["wout"])
        hnb = [lay.enter_context(sbt(nc, "mhn%d" % i, [128, 8, 512], BF16)) for i in range(2)]
        yb = [lay.enter_context(sbt(nc, "myb%d" % i, [128, 16, 512], BF16)) for i in range(2)]
        hb = [lay.enter_context(sbt(nc, "mh%d" % i, [128, 8, 512], F32)) for i in range(2)]
        mg = lay.enter_context(sbt(nc, "mg", [128, 8, 512], BF16))
        gt = [lay.enter_context(sbt(nc, "gt%d" % i, [128, 512], F32)) for i in range(2)]
        tmp = [lay.enter_context(sbt(nc, "mtmp%d" % i, [128, 512], F32)) for i in range(2)]
        msum = lay.enter_context(sbt(nc, "msum", [128, 512], F32))
        it = 0
        nb_ = 0
        for s in range(env["nseq"]):
            for (c0, bw) in BLOCKS:
                i = it % 2
                it += 1
                P.dma("sp", hnb[i][:, :, 0:bw], env["hnT"][s, :, :, c0:c0 + bw], w=[("hn", i)])
                P.dma("sp", yb[i][:, 0:4, 0:bw], env["ys5T"][s, :, :, c0:c0 + bw], w=[("y5", i)])
                P.dma("sp", yb[i][:, 4:12, 0:bw], env["yssdT"][s, :, :, c0:c0 + bw], w=[("ys", i)])
                P.dma("sp", yb[i][:, 12:16, 0:bw], env["ygdnT"][s, :, :, c0:c0 + bw], w=[("yg", i)])
                P.dma("sp", hb[i][:, :, 0:bw], env["hT"][s, :, :, c0:c0 + bw], w=[("h", i)])
                srcs = [(0, 4, wo5, ("y5", i), "wo5"), (4, 8, wos, ("ys", i), "wos"), (12, 4, wog, ("yg", i), "wog")]
                for dtl in range(8):
                    for b, (y0, nk, wsrc, yk, wk) in enumerate(srcs):
                        ba, bb = nb_ % 4, 4 + nb_ % 4
                        nb_ += 1
                        for kt in range(8):
                            P.mm(banks[ba][:, 0:bw], wg[:, b * 8 + dtl, kt, :], hnb[i][:, kt, 0:bw], kt == 0, kt == 7, ["wg", ("hn", i)], [("bk", ba)])
                        P.act(gt[b % 2][:, 0:bw], banks[ba][:, 0:bw], AF.Sigmoid, [("bk", ba)], [("gt", b % 2)])
                        for kt in range(nk):
                            P.mm(banks[bb][:, 0:bw], wsrc[:, kt, dtl * 128:(dtl + 1) * 128], yb[i][:, y0 + kt, 0:bw], kt == 0, kt == nk - 1,
                                 [wk, yk], [("bk", bb)])
                        if b == 0:
                            P.tt("dve", msum[:, 0:bw], gt[0][:, 0:bw], banks[bb][:, 0:bw], ALU.mult, [("gt", 0), ("bk", bb)], ["msum"])
                        else:
                            P.tt("dve", tmp[b % 2][:, 0:bw], gt[b % 2][:, 0:bw], banks[bb][:, 0:bw], ALU.mult, [("gt", b % 2), ("bk", bb)], [("tmp", b % 2)])
                            if b == 1:
                                P.tt("dve", msum[:, 0:bw], msum[:, 0:bw], tmp[1][:, 0:bw], ALU.add, ["msum", ("tmp", 1)], ["msum"])
                            else:
                                P.tt("dve", mg[:, dtl, 0:bw], msum[:, 0:bw], tmp[0][:, 0:bw], ALU.add, ["msum", ("tmp", 0)], [("mg", dtl)])
                mgk = [("mg", k) for k in range(8)]
                for dtl in range(8):
                    bk = nb_ % 4
                    nb_ += 1
                    for kt in range(8):
                        P.mm(banks[bk][:, 0:bw], wout[:, kt, dtl * 128:(dtl + 1) * 128], mg[:, kt, 0:bw], kt == 0, kt == 7, ["wout"] + mgk, [("bk", bk)])
                    P.tt("dve", hb[i][:, dtl, 0:bw], hb[i][:, dtl, 0:bw], banks[bk][:, 0:bw], ALU.add, [("h", i), ("bk", bk)], [("h", i)])
                off = PAD if c0 == 0 else 0
                P.dma("sp", env["hT"][s, :, :, c0 + off:c0 + bw], hb[i][:, :, off:bw], r=[("h", i)])
        P.flush()


def build(n_layers=DEPTH, nseq=2, dbg=False):
    nc = bass.Bass("TRN2", target_bir_lowering=False)
    x_in = nc.dram_tensor("x", [nseq, SEQ, D], F32, kind="ExternalInput").ap()
    meta_in = nc.dram_tensor("meta", [NMETA, D], F32, kind="ExternalInput").ap()
    wall_in = nc.dram_tensor("wall", [max(n_layers, 1), 128, NW], F32, kind="ExternalInput").ap()
    small_in = nc.dram_tensor("small", [max(n_layers, 1), 128, NS], F32, kind="ExternalInput").ap()
    fnw_in = nc.dram_tensor("fnw", [128, 8], F32, kind="ExternalInput").ap()
    const_in = nc.dram_tensor("consts", [128, NC_CONST], F32, kind="ExternalInput").ap()
    const2_in = nc.dram_tensor("consts2", [128, NK], F32, kind="ExternalInput").ap()
    y_out = nc.dram_tensor("y", [nseq, SEQ, D], F32, kind="ExternalOutput").ap()
    hT = nc.dram_tensor("hT", [nseq, 128, 8, LP], F32, kind="Internal").ap()
    hnT = nc.dram_tensor("hnT", [nseq, 128, 8, LP], BF16, kind="Internal").ap()
    wallb = nc.dram_tensor("wallb", [max(n_layers, 1), 128, NW], BF16, kind="Internal").ap()

    top = ExitStack()
    P = Prog(nc, top)
    banks = [top.enter_context(nc.psum_tensor("bank%d" % b, [128, 512], F32)) for b in range(8)]
    cst = top.enter_context(sbt(nc, "cst", [128, NC_CONST], F32))
    cstb = top.enter_context(sbt(nc, "cstb", [128, NC_CONST], BF16))
    fnw = top.enter_context(sbt(nc, "fnw_s", [128, 8], F32))
    identf = cst[:, C_IDENT:C_IDENT + 128]
    identb = cstb[:, C_IDENT:C_IDENT + 128]
    onesb = cstb[:, C_ONES:C_ONES + 128]

    with ExitStack() as ph:
        P.dma("sp", cst[:], const_in, w=["cst"])
        P.dma("sp", fnw[:], fnw_in, w=["fnw"])
        P.cp("dve", cstb[:], cst[:], ["cst"], ["cstb"])
        CH = 2048
        stg = [ph.enter_context(sbt(nc, "wstg%d" % i, [128, CH], F32)) for i in range(3)]
        stgb = [ph.enter_context(sbt(nc, "wstgb%d" % i, [128, CH], BF16)) for i in range(3)]
        engs = ["dve", "pool", "act"]
        ci = 0
        for l in range(min(n_layers, 1)):
            for c0 in range(0, NW, CH):
                cw = min(CH, NW - c0)
                i = ci % 3
                P.dma("sp", stg[i][:, 0:cw], wall_in[l, :, c0:c0 + cw], w=[("stg", i)])
                P.cp(engs[i], stgb[i][:, 0:cw], stg[i][:, 0:cw], [("stg", i)], [("stgb", i)])
                P.dma("sp", wallb[l, :, c0:c0 + cw], stgb[i][:, 0:cw], r=[("stgb", i)])
                ci += 1
        xt = [ph.enter_context(sbt(nc, "xt%d" % i, [128, D], F32)) for i in range(2)]
        hs = [ph.enter_context(sbt(nc, "hs%d" % i, [128, 8, 128], F32)) for i in range(2)]
        it = 0
        for s in range(nseq):
            for tt in range(NT):
                i = it % 2
                it += 1
                if tt == 0:
                    P.ms("pool", hs[i][:], 0.0, [("hs", i)])
                    P.dma("sp", xt[i][0:NMETA, :], meta_in, w=[("xt", i)])
                    np_ = NMETA
                else:
                    P.dma("sp", xt[i][:], x_in[s, (tt - 1) * 128:tt * 128, :], w=[("xt", i)])
                    np_ = 128
                for kt in range(8):
                    bk = banks[kt % 2]
                    P.tr(bk[:, 0:np_], xt[i][0:np_, kt * 128:(kt + 1) * 128], identf[0:np_, 0:np_], [("xt", i), "cst"], [("bk", kt % 2)])
                    P.cp("act" if kt % 2 else "dve", hs[i][:, kt, 128 - np_:128], bk[:, 0:np_], [("bk", kt % 2)], [("hs", i)])
                P.dma("sp", hT[s, :, :, tt * 128:(tt + 1) * 128], hs[i][:], r=[("hs", i)])
        P.flush()

    ys5T = nc.dram_tensor("ys5T", [nseq, 128, 4, LP], BF16, kind="ExternalOutput" if dbg else "Internal").ap()
    yssdT = nc.dram_tensor("yssdT", [nseq, 128, 8, LP], BF16, kind="ExternalOutput" if dbg else "Internal").ap()
    ygdnT = nc.dram_tensor("ygdnT", [nseq, 128, 4, LP], BF16, kind="ExternalOutput" if dbg else "Internal").ap()
    yfD = nc.dram_tensor("yfD", [nseq, NT, 128, 1024], BF16, kind="Internal").ap()
    env = dict(cast_next=True, n_layers=n_layers, wall_in=wall_in, yfD=yfD, nc=nc, P=P, banks=banks, hT=hT, hnT=hnT, wallb=wallb, small_in=small_in, identf=identf, identb=identb,
               onesb=onesb, const2=const2_in, ys5T=ys5T, yssdT=yssdT, ygdnT=ygdnT, nseq=nseq, cst=cst, cstb=cstb)
    for l in range(n_layers):
        phase_norm(env, l)
        if dbg in (False, "s5"):
            phase_s5(env, l)
        if dbg in (False, "ssd"):
            phase_ssd(env, l)
        if dbg in (False, "gdn"):
            phase_gdn(env, l)
        if dbg:
            break
        phase_merge(env, l)

    def rms_block(ph_tiles, s, c0, bw, hblk, key):
        sq, rstd = ph_tiles
        P.dma("sp", hblk[:, :, 0:bw], hT[s, :, :, c0:c0 + bw], w=[key + "h"])
        P.act(sq[:, :, 0:bw], hblk[:, :, 0:bw], AF.Square, [key + "h"], [key + "sq"])
        for kt in range(8):
            P.mm(banks[7][:, 0:bw], onesb, sq[:, kt, 0:bw], kt == 0, kt == 7, ["cstb", key + "sq"], ["b7"])
        P.act(rstd[:, 0:bw], banks[7][:, 0:bw], AF.Sqrt, ["b7"], [key + "rs"], bias=EPS, scale=1.0 / D)
        P.recip(rstd[:, 0:bw], rstd[:, 0:bw], [key + "rs"], [key + "rs"])

    with ExitStack() as ph:
        hb = [ph.enter_context(sbt(nc, "fh%d" % i, [128, 8, 512], F32)) for i in range(2)]
        sqb = [ph.enter_context(sbt(nc, "fsq%d" % i, [128, 8, 512], BF16)) for i in range(2)]
        rsb = [ph.enter_context(sbt(nc, "frs%d" % i, [128, 512], F32)) for i in range(2)]
        hnf = [ph.enter_context(sbt(nc, "fhn%d" % i, [128, 8, 512], F32)) for i in range(2)]
        ot = [ph.enter_context(sbt(nc, "fot%d" % i, [128, D], F32)) for i in range(2)]
        it = 0
        oi = 0
        for s in range(nseq):
            for (c0, bw) in BLOCKS:
                i = it % 2
                it += 1
                key = "f%d" % i
                rms_block((sqb[i], rsb[i]), s, c0, bw, hb[i], key)
                for kt in range(8):
                    P.stt(hnf[i][:, kt, 0:bw], hb[i][:, kt, 0:bw], fnw[:, kt:kt + 1], rsb[i][:, 0:bw], ALU.mult, ALU.mult,
                          [key + "h", key + "rs", "fnw"], [key + "hn"])
                for t0 in range(0, bw, 128):
                    tt = (c0 + t0) // 128
                    if tt == 0:
                        continue
                    j = oi % 2
                    oi += 1
                    for kt in range(8):
                        bk = banks[kt % 4]
                        P.tr(bk[:, 0:128], hnf[i][:, kt, t0:t0 + 128], identf, [key + "hn", "cst"], [("bk", kt % 4)])
                        P.cp("act" if kt % 2 else "dve", ot[j][:, kt * 128:(kt + 1) * 128], bk[:, 0:128], [("bk", kt % 4)], [("ot", j)])
                    P.dma("sp", y_out[s, (tt - 1) * 128:tt * 128, :], ot[j][:], r=[("ot", j)])
        P.flush()
    top.close()
    return nc


_CACHE = {}


def kernel(**inputs):
    n_layers = inputs.pop("_n_layers", DEPTH)
    dbg = inputs.pop("_dbg", False)
    x = np.asarray(inputs["x"], np.float32)
    nseq = x.shape[0] // N_CORES
    walls, smalls = [], []
    for i in range(max(n_layers, 1)):
        w, s = _arrange_layer(i, inputs)
        walls.append(w)
        smalls.append(s)
    wall = np.stack(walls)
    small = np.stack(smalls)
    fnw = np.ascontiguousarray(np.asarray(inputs["final_norm_w"], np.float32).reshape(8, 128).T)
    meta = np.asarray(inputs["meta_tokens"], np.float32)
    consts = _consts()
    consts2 = _consts2()
    key = (n_layers, nseq, dbg)
    if key not in _CACHE:
        _CACHE[key] = build(n_layers, nseq, dbg)
    nc = _CACHE[key]
    in_maps = []
    for c in range(N_CORES):
        in_maps.append({"x": np.ascontiguousarray(x[c * nseq:(c + 1) * nseq]), "meta": meta, "wall": wall,
                        "small": small, "fnw": fnw, "consts": consts, "consts2": consts2})
    res = run_bass_kernel_spmd(nc, in_maps, core_ids=list(range(N_CORES)))
    if dbg:
        return res.results
    return np.concatenate([r["y"] for r in res.results], axis=0)
```
